# Optimizing a Trainium2 kernel written in Bass

```python
import math
import jax, jax.numpy as jnp
from jax import lax
import numpy as np

D_MODEL = 1024
BATCH = 8
SEQ = 4096
DEPTH = 2

HEAD_DIM = 64
RWKV_HEADS = 6
RWKV_W = RWKV_HEADS * HEAD_DIM
D_DECAY_LORA = 64
D_AAA_LORA = 64
D_GATE_LORA = 128
RWKV_SPLITS = (RWKV_W, RWKV_W, RWKV_W, D_DECAY_LORA, D_AAA_LORA, D_GATE_LORA)
RWKV_COLS = sum(RWKV_SPLITS)
GN_EPS = 64e-5
NSA_Q_HEADS = 6
NSA_KV_GROUPS = 2
NSA_Q_PER_KV = NSA_Q_HEADS // NSA_KV_GROUPS
NSA_W = NSA_Q_HEADS * HEAD_DIM
NSA_KV_W = NSA_KV_GROUPS * HEAD_DIM
NSA_N_BRANCH = 3
NSA_SPLITS = (NSA_W,) + (NSA_KV_W,) * 6 + (NSA_Q_HEADS * NSA_N_BRANCH,)
NSA_COLS = sum(NSA_SPLITS)
CMP_LEN = 32
CMP_STRIDE = 16
CMP_HIDDEN = 128
SEL_LEN = 64
SEL_TOPK = 16
WINDOW = 512
Q_BLOCK = 64
FORCED_BONUS = 1e3
NEG_INF = -1e30
S5_GROUPS = 16
S5_CH = 16
S5_W = S5_GROUPS * S5_CH
S5_STATE = 64
DT_MIN = 0.001
DT_MAX = 0.1
D_MIX = RWKV_W + NSA_W + S5_W
N_IN = RWKV_COLS + NSA_COLS + S5_W
D_FF = 2816
CONV_W = 3
PLE_DIM = 256
NORM_EPS = 1e-6

kernel_name = "hybrid_rwkv7_nsa_s5_sandwich_block"


def _split(z, sizes):
    cuts = [int(c) for c in np.cumsum(sizes)[:-1]]
    return jnp.split(z, cuts, axis=-1)


def rms_norm(x, g):
    xf = x.astype(jnp.float32)
    y = xf * lax.rsqrt(jnp.mean(xf * xf, axis=-1, keepdims=True) + NORM_EPS)
    return (y * g.astype(jnp.float32)).astype(x.dtype)


def masked_softmax(s, mask):
    s = jnp.where(mask, s, NEG_INF)
    return jnp.where(mask, jax.nn.softmax(s, axis=-1), 0.0)


def alibi_slopes(n):
    def pow2(m):
        start = 2.0 ** (-8.0 / m)
        return [start ** (i + 1) for i in range(m)]
    if math.log2(n).is_integer():
        return pow2(n)
    c = 2 ** math.floor(math.log2(n))
    return pow2(c) + pow2(2 * c)[0::2][: n - c]


def token_shift(z, mu):
    prev = jnp.pad(z, ((0, 0), (1, 0), (0, 0)))[:, :-1]
    return z + (prev - z) * mu


def rwkv7_mixer(z, w0, w2, a0, a2, g2, k_k, k_a, r_k, gn_w, gn_b):
    B, T, _ = z.shape
    H, N = RWKV_HEADS, HEAD_DIM
    z = z.astype(jnp.float32)
    r, k, v, wl, al, gl = _split(z, RWKV_SPLITS)
    w = -jax.nn.softplus(-(w0 + jnp.tanh(wl) @ w2)) - 0.5
    a = jax.nn.sigmoid(a0 + al @ a2)
    g = jax.nn.sigmoid(gl) @ g2
    heads = lambda t: t.reshape(B, T, H, N)
    kk = heads(k * k_k)
    kk = kk / jnp.maximum(jnp.sqrt(jnp.sum(kk * kk, axis=-1, keepdims=True)), 1e-12)
    k = k * (1.0 + (a - 1.0) * k_a)
    decay = jnp.exp(-jnp.exp(w))
    r_h, k_h, v_h, a_h, d_h = (heads(t) for t in (r, k, v, a, decay))

    def step(S, inp):
        r_t, d_t, k_t, v_t, kk_t, a_t = inp
        sa = jnp.einsum('bhij,bhj->bhi', S, -kk_t)
        S = S * d_t[:, :, None, :] + sa[..., None] * (kk_t * a_t)[:, :, None, :] + v_t[..., None] * k_t[:, :, None, :]
        return S, jnp.einsum('bhij,bhj->bhi', S, r_t)

    tm = lambda t: jnp.moveaxis(t, 1, 0)
    S0 = jnp.zeros((B, H, N, N), jnp.float32)
    _, o = lax.scan(step, S0, tuple(tm(t) for t in (r_h, d_h, k_h, v_h, kk, a_h)))
    o = jnp.moveaxis(o, 0, 1)
    mu = jnp.mean(o, axis=-1, keepdims=True)
    var = jnp.mean(jnp.square(o - mu), axis=-1, keepdims=True)
    o = ((o - mu) * lax.rsqrt(var + GN_EPS)).reshape(B, T, RWKV_W) * gn_w + gn_b
    bonus = jnp.sum(r_h * k_h * r_k.reshape(H, N), axis=-1, keepdims=True) * v_h
    return (o + bonus.reshape(B, T, RWKV_W)) * g


def nsa_mixer(z, pe_k, pe_v, w1_k, w2_k, w1_v, w2_v):
    B, T, _ = z.shape
    G, R, Dh = NSA_KV_GROUPS, NSA_Q_PER_KV, HEAD_DIM
    f32 = jnp.float32
    z = z.astype(f32)
    q, kc, vc, ks, vs, kw, vw, gl = _split(z, NSA_SPLITS)
    kv_heads = lambda t: t.reshape(B, T, G, Dh).transpose(0, 2, 1, 3)
    kc, vc, ks, vs, kw, vw = (kv_heads(t) for t in (kc, vc, ks, vs, kw, vw))

    n_cmp = (T - CMP_LEN) // CMP_STRIDE + 1
    cmp_start = jnp.arange(n_cmp) * CMP_STRIDE
    cmp_end = cmp_start + CMP_LEN - 1
    cmp_idx = cmp_start[:, None] + jnp.arange(CMP_LEN)[None, :]

    def compress(t, pe, w1, w2):
        blocks = (t[:, :, cmp_idx] + pe.astype(f32)).reshape(B, G, n_cmp, CMP_LEN * Dh)
        return jax.nn.gelu(blocks @ w1.astype(f32)) @ w2.astype(f32)

    k_cmp = compress(kc, pe_k, w1_k, w2_k)
    v_cmp = compress(vc, pe_v, w1_v, w2_v)

    n_sel = T // SEL_LEN
    k_top = min(SEL_TOPK, n_sel)
    ks_blk = ks.reshape(B, G, n_sel, SEL_LEN, Dh)
    vs_blk = vs.reshape(B, G, n_sel, SEL_LEN, Dh)
    blk_ids = jnp.arange(n_sel)
    blk_start = blk_ids * SEL_LEN
    overlap = ((cmp_start[:, None] < blk_start[None, :] + SEL_LEN)
               & (cmp_end[:, None] >= blk_start[None, :])).astype(f32)
    gather_blocks = jax.vmap(jax.vmap(lambda blk, sel: blk[sel]))

    win_len = WINDOW + Q_BLOCK - 1
    kw_pad = jnp.pad(kw, ((0, 0), (0, 0), (WINDOW, 0), (0, 0)))
    vw_pad = jnp.pad(vw, ((0, 0), (0, 0), (WINDOW, 0), (0, 0)))

    slopes = jnp.asarray(alibi_slopes(NSA_Q_HEADS), f32).reshape(G, R)[None, :, :, None, None]
    scale = HEAD_DIM ** -0.5
    n_qb = T // Q_BLOCK
    q_blocks = q.reshape(B, n_qb, Q_BLOCK, G, R, Dh).transpose(1, 0, 3, 4, 2, 5)
    g_blocks = jax.nn.sigmoid(gl).reshape(B, n_qb, Q_BLOCK, G, R, NSA_N_BRANCH).transpose(1, 0, 3, 4, 2, 5)

    def query_block(args):
        qb, gb, bi = args
        t0 = bi * Q_BLOCK
        t = t0 + jnp.arange(Q_BLOCK)
        d_c = t[:, None] - cmp_end[None, :]
        s_c = jnp.einsum('bgrqd,bgnd->bgrqn', qb, k_cmp) * scale - slopes * jnp.abs(d_c).astype(f32)
        p_c = masked_softmax(s_c, d_c >= 0)
        o_c = jnp.einsum('bgrqn,bgnd->bgrqd', p_c, v_cmp)
        imp = jnp.einsum('bgrqn,nj->bgqj', p_c, overlap)
        cur = t // SEL_LEN
        valid = blk_start[None, :] <= t[:, None]
        forced = (blk_ids[None, :] == 0) | (blk_ids[None, :] == cur[:, None]) | (blk_ids[None, :] == cur[:, None] - 1)
        score = jnp.where(valid, imp + FORCED_BONUS * forced.astype(f32), -jnp.inf)
        _, sel = lax.top_k(score, k_top)
        k_sel = gather_blocks(ks_blk, sel)
        v_sel = gather_blocks(vs_blk, sel)
        pos = sel[..., None] * SEL_LEN + jnp.arange(SEL_LEN)
        d_s = (t[:, None, None] - pos)[:, :, None]
        s_s = jnp.einsum('bgrqd,bgqnld->bgrqnl', qb, k_sel) * scale - slopes[..., None] * jnp.abs(d_s).astype(f32)
        p_s = masked_softmax(s_s.reshape(B, G, R, Q_BLOCK, k_top * SEL_LEN),
                             (d_s >= 0).reshape(B, G, 1, Q_BLOCK, k_top * SEL_LEN))
        o_s = jnp.einsum('bgrqnl,bgqnld->bgrqd', p_s.reshape(s_s.shape), v_sel)
        k_win = lax.dynamic_slice_in_dim(kw_pad, t0 + 1, win_len, axis=2)
        v_win = lax.dynamic_slice_in_dim(vw_pad, t0 + 1, win_len, axis=2)
        pos_w = t0 - WINDOW + 1 + jnp.arange(win_len)
        d_w = t[:, None] - pos_w[None, :]
        mask_w = (d_w >= 0) & (d_w < WINDOW) & (pos_w[None, :] >= 0)
        s_w = jnp.einsum('bgrqd,bgkd->bgrqk', qb, k_win) * scale - slopes * jnp.abs(d_w).astype(f32)
        o_w = jnp.einsum('bgrqk,bgkd->bgrqd', masked_softmax(s_w, mask_w), v_win)
        return gb[..., 0:1] * o_c + gb[..., 1:2] * o_s + gb[..., 2:3] * o_w

    o = lax.map(query_block, (q_blocks, g_blocks, jnp.arange(n_qb)))
    return o.transpose(1, 0, 4, 2, 3, 5).reshape(B, T, NSA_W)


def _complex_linear_combine(e1, e2):
    a1r, a1i, b1r, b1i = e1
    a2r, a2i, b2r, b2i = e2
    return (a1r * a2r - a1i * a2i, a1r * a2i + a1i * a2r,
            a2r * b1r - a2i * b1i + b2r, a2r * b1i + a2i * b1r + b2i)


def s5_mixer(u, lam_re, lam_im, log_dt, b_re, b_im, c_re, c_im, d_skip, w_glu):
    B, T, _ = u.shape
    f32 = jnp.float32
    u = u.astype(f32).reshape(B, T, S5_GROUPS, S5_CH)
    lam_re, lam_im = lam_re.astype(f32), lam_im.astype(f32)
    b_re, b_im, c_re, c_im = (t.astype(f32) for t in (b_re, b_im, c_re, c_im))
    dt = jnp.exp(log_dt.astype(f32))[:, None]
    mag = jnp.exp(lam_re * dt)
    ab_re, ab_im = mag * jnp.cos(lam_im * dt), mag * jnp.sin(lam_im * dt)
    den = lam_re * lam_re + lam_im * lam_im
    f_re = ((ab_re - 1.0) * lam_re + ab_im * lam_im) / den
    f_im = (ab_im * lam_re - (ab_re - 1.0) * lam_im) / den
    bb_re = f_re[..., None] * b_re - f_im[..., None] * b_im
    bb_im = f_re[..., None] * b_im + f_im[..., None] * b_re
    bu_re = jnp.einsum('gpc,btgc->btgp', bb_re, u)
    bu_im = jnp.einsum('gpc,btgc->btgp', bb_im, u)
    a_re = jnp.broadcast_to(ab_re, bu_re.shape)
    a_im = jnp.broadcast_to(ab_im, bu_im.shape)
    _, _, s_re, s_im = lax.associative_scan(_complex_linear_combine, (a_re, a_im, bu_re, bu_im), axis=1)
    y = (jnp.einsum('gcp,btgp->btgc', c_re, s_re) - jnp.einsum('gcp,btgp->btgc', c_im, s_im)
         + d_skip.astype(f32).reshape(S5_GROUPS, S5_CH) * u)
    y = jax.nn.gelu(y.reshape(B, T, S5_W))
    val, gate = jnp.split(y @ w_glu, 2, axis=-1)
    return val * jax.nn.sigmoid(gate)


def conv_ffn(x, w_up, conv_w, conv_b, w_down):
    hu = x @ w_up
    c = hu.shape[-1]
    hu = lax.conv_general_dilated(hu, conv_w[:, None, :], window_strides=(1,), padding=[(CONV_W - 1, 0)],
                                  dimension_numbers=('NWC', 'WIO', 'NWC'), feature_group_count=c) + conv_b
    gate, up = jnp.split(hu, 2, axis=-1)
    return (jax.nn.gelu(gate, approximate=True) * up) @ w_down


def setup_inputs(seed: int = 0) -> dict:
    key = jax.random.key(seed)
    keys = iter(jax.random.split(key, 48))
    f32 = jnp.float32

    def nrm(shape, scale):
        return scale * jax.random.normal(next(keys), shape, f32)

    def gain(shape):
        return 1.0 + 0.02 * jax.random.normal(next(keys), shape, f32)

    def unif(shape, lo, hi):
        return jax.random.uniform(next(keys), shape, f32, lo, hi)

    L = DEPTH
    n = jnp.arange(S5_STATE, dtype=f32)
    return {
        'x': nrm((BATCH, SEQ, D_MODEL), 1.0),
        'p': nrm((DEPTH, BATCH, SEQ, PLE_DIM), 1.0),
        'pre_mix_norm': gain((L, D_MODEL)),
        'post_mix_norm': gain((L, D_MODEL)),
        'pre_ffn_norm': gain((L, D_MODEL)),
        'post_ffn_norm': gain((L, D_MODEL)),
        'w_in': nrm((L, D_MODEL, N_IN), D_MODEL ** -0.5),
        'w_out': nrm((L, D_MIX, D_MODEL), D_MIX ** -0.5),
        'shift_mu': unif((L, RWKV_COLS), 0.0, 1.0),
        'rw_w0': unif((L, RWKV_W), -5.0, -1.0),
        'rw_w2': nrm((L, D_DECAY_LORA, RWKV_W), 0.1),
        'rw_a0': nrm((L, RWKV_W), 0.1),
        'rw_a2': nrm((L, D_AAA_LORA, RWKV_W), 0.3 * D_AAA_LORA ** -0.5),
        'rw_g2': nrm((L, D_GATE_LORA, RWKV_W), D_GATE_LORA ** -0.5),
        'rw_k_k': 0.85 + nrm((L, RWKV_W), 0.02),
        'rw_k_a': gain((L, RWKV_W)),
        'rw_r_k': nrm((L, RWKV_W), 0.1),
        'rw_gn_w': gain((L, RWKV_W)),
        'rw_gn_b': nrm((L, RWKV_W), 0.02),
        'cmp_pe_k': nrm((L, CMP_LEN, HEAD_DIM), 0.02),
        'cmp_pe_v': nrm((L, CMP_LEN, HEAD_DIM), 0.02),
        'cmp_w1_k': nrm((L, CMP_LEN * HEAD_DIM, CMP_HIDDEN), (CMP_LEN * HEAD_DIM) ** -0.5),
        'cmp_w2_k': nrm((L, CMP_HIDDEN, HEAD_DIM), CMP_HIDDEN ** -0.5),
        'cmp_w1_v': nrm((L, CMP_LEN * HEAD_DIM, CMP_HIDDEN), (CMP_LEN * HEAD_DIM) ** -0.5),
        'cmp_w2_v': nrm((L, CMP_HIDDEN, HEAD_DIM), CMP_HIDDEN ** -0.5),
        's5_lam_re': -0.5 + nrm((L, S5_GROUPS, S5_STATE), 0.01),
        's5_lam_im': math.pi * n + nrm((L, S5_GROUPS, S5_STATE), 0.01),
        's5_log_dt': unif((L, S5_GROUPS), math.log(DT_MIN), math.log(DT_MAX)),
        's5_b_re': nrm((L, S5_GROUPS, S5_STATE, S5_CH), (2 * S5_CH) ** -0.5),
        's5_b_im': nrm((L, S5_GROUPS, S5_STATE, S5_CH), (2 * S5_CH) ** -0.5),
        's5_c_re': nrm((L, S5_GROUPS, S5_CH, S5_STATE), (2 * S5_STATE) ** -0.5),
        's5_c_im': nrm((L, S5_GROUPS, S5_CH, S5_STATE), (2 * S5_STATE) ** -0.5),
        's5_d': nrm((L, S5_W), 1.0),
        's5_w_glu': nrm((L, S5_W, 2 * S5_W), S5_W ** -0.5),
        'w_up': nrm((L, D_MODEL, 2 * D_FF), D_MODEL ** -0.5),
        'conv_w': nrm((L, CONV_W, 2 * D_FF), CONV_W ** -0.5),
        'conv_b': nrm((L, 2 * D_FF), 0.02),
        'w_down': nrm((L, D_FF, D_MODEL), D_FF ** -0.5),
        'w_ple': nrm((L, PLE_DIM, D_MODEL), PLE_DIM ** -0.5),
        'ple_norm': gain((L, D_MODEL)),
        'w_ple_gate': nrm((L, D_MODEL, D_MODEL), D_MODEL ** -0.5),
    }


def reference(x, p, pre_mix_norm, post_mix_norm, pre_ffn_norm, post_ffn_norm, w_in, w_out,
              shift_mu, rw_w0, rw_w2, rw_a0, rw_a2, rw_g2, rw_k_k, rw_k_a, rw_r_k, rw_gn_w, rw_gn_b,
              cmp_pe_k, cmp_pe_v, cmp_w1_k, cmp_w2_k, cmp_w1_v, cmp_w2_v,
              s5_lam_re, s5_lam_im, s5_log_dt, s5_b_re, s5_b_im, s5_c_re, s5_c_im, s5_d, s5_w_glu,
              w_up, conv_w, conv_b, w_down, w_ple, ple_norm, w_ple_gate):
    h = x
    for i in range(DEPTH):
        z = rms_norm(h, pre_mix_norm[i]) @ w_in[i]
        z_rw, z_nsa, z_s5 = _split(z, (RWKV_COLS, NSA_COLS, S5_W))
        o_rw = rwkv7_mixer(token_shift(z_rw, shift_mu[i]), rw_w0[i], rw_w2[i], rw_a0[i], rw_a2[i], rw_g2[i],
                           rw_k_k[i], rw_k_a[i], rw_r_k[i], rw_gn_w[i], rw_gn_b[i])
        o_nsa = nsa_mixer(z_nsa, cmp_pe_k[i], cmp_pe_v[i], cmp_w1_k[i], cmp_w2_k[i], cmp_w1_v[i], cmp_w2_v[i])
        o_s5 = s5_mixer(z_s5, s5_lam_re[i], s5_lam_im[i], s5_log_dt[i], s5_b_re[i], s5_b_im[i],
                        s5_c_re[i], s5_c_im[i], s5_d[i], s5_w_glu[i])
        mix = jnp.concatenate([o_rw, o_nsa, o_s5], axis=-1).astype(h.dtype) @ w_out[i]
        h = h + rms_norm(mix, post_mix_norm[i])
        f = conv_ffn(rms_norm(h, pre_ffn_norm[i]), w_up[i], conv_w[i], conv_b[i], w_down[i])
        h = h + rms_norm(f, post_ffn_norm[i])
        e = rms_norm(p[i] @ w_ple[i], ple_norm[i])
        gate = jax.nn.sigmoid((h @ w_ple_gate[i]).astype(jnp.float32)).astype(h.dtype)
        h = h + gate * e
    return h
```

```python
import math
import numpy as np
from contextlib import ExitStack
import concourse.bass as bass
import concourse.mybir as mybir
from concourse.bass_utils import run_bass_kernel_spmd

F32 = mybir.dt.float32
BF16 = mybir.dt.bfloat16
AF = mybir.ActivationFunctionType
ALU = mybir.AluOpType
AX = mybir.AxisListType

T = 4096
D = 1024
DEPTH = 2
NIN = 2834
DFF = 2816
C_DEC = math.exp(-0.5)
NEG = -1.0e30
NT_DBG = 4
SLOPES = [0.25, 0.0625, 0.015625, 0.00390625, 0.5, 0.125]


class Prog:
    COMPUTE = ("pe", "act", "dve", "pool")
    QUEUES = ("sp", "act", "pool")
    NSLOT = 8

    def __init__(self, nc, es):
        self.nc = nc
        self.ges = es
        self.es = es
        self.ops = {e: [] for e in ("pe", "act", "dve", "pool", "sp")}
        self.sem = {}
        self.cnt = {}
        for e in self.COMPUTE:
            self.sem[e] = es.enter_context(nc.semaphore("s_" + e))
            self.cnt[e] = 0
        self.slots = {}
        for q in self.QUEUES:
            self.slots[q] = [[es.enter_context(nc.semaphore(f"d_{q}{i}")), 0] for i in range(self.NSLOT)]
        self.slot_rr = {q: 0 for q in self.QUEUES}
        self.waited = {e: {} for e in self.ops}
        self.lastw = {}
        self.readers = {}
        self.nsb = 0
        self.ndram = 0

    def sb(self, shape, dt=F32, name=None):
        self.nsb += 1
        return self.es.enter_context(self.nc.sbuf_tensor(f"{name or 'sb'}_{self.nsb}", list(shape), dt))

    def ps(self, shape, dt=F32, name=None):
        self.nsb += 1
        return self.es.enter_context(self.nc.psum_tensor(f"{name or 'ps'}_{self.nsb}", list(shape), dt))

    def _need(self, eng, ev, waits):
        if ev is None:
            return
        sem_key, sem, val, src = ev
        if src == eng and sem_key == src and eng == "pe":
            return
        w = self.waited[eng]
        if w.get(sem_key, 0) >= val:
            return
        w[sem_key] = val
        waits.append((sem, val))

    @staticmethod
    def _k(r):
        if isinstance(r, (str, int)):
            return r
        if isinstance(r, tuple):
            return tuple(Prog._k(x) for x in r)
        return "T:" + str(getattr(r, "name", id(r)))

    def _deps(self, eng, reads, writes):
        waits = []
        for r in reads:
            self._need(eng, self.lastw.get(r), waits)
        for w in writes:
            self._need(eng, self.lastw.get(w), waits)
            for ev in self.readers.get(w, []):
                self._need(eng, ev, waits)
        return waits

    def _commit(self, ev, reads, writes):
        for r in reads:
            lst = self.readers.setdefault(r, [])
            lst.append(ev)
            if len(lst) > 32:
                del lst[0]
        for w in writes:
            self.lastw[w] = ev
            self.readers[w] = []

    def op(self, eng, fn, reads=(), writes=()):
        reads = [self._k(r) for r in reads]
        writes = [self._k(r) for r in writes]
        waits = self._deps(eng, reads, writes)
        self.cnt[eng] += 1
        ev = (eng, self.sem[eng], self.cnt[eng], eng)
        self.ops[eng].append((waits, fn, (self.sem[eng], 1)))
        self._commit(ev, reads, writes)

    def dma(self, q, out, in_, reads=(), writes=(), **kw):
        reads = [self._k(r) for r in reads]
        writes = [self._k(r) for r in writes]
        slots = self.slots[q]
        i = self.slot_rr[q]
        self.slot_rr[q] = (i + 1) % self.NSLOT
        sem, val = slots[i]
        waits = self._deps(q, reads, writes)
        key = f"d_{q}{i}"
        if val > 0 and self.waited[q].get(key, 0) < val:
            self.waited[q][key] = val
            waits.append((sem, val))
        slots[i][1] = val + 16
        ev = (key, sem, val + 16, q)

        def fn(e, out=out, in_=in_, kw=kw):
            return e.dma_start(out=out, in_=in_, **kw)

        self.ops[q].append((waits, fn, (sem, 16)))
        self._commit(ev, reads, writes)

    def barrier(self):
        for eng in self.ops:
            waits = []
            for e in self.COMPUTE:
                if e != eng and self.cnt[e] > self.waited[eng].get(e, 0):
                    self.waited[eng][e] = self.cnt[e]
                    waits.append((self.sem[e], self.cnt[e]))
            for q in self.QUEUES:
                for i, (sem, val) in enumerate(self.slots[q]):
                    key = f"d_{q}{i}"
                    if val > self.waited[eng].get(key, 0):
                        self.waited[eng][key] = val
                        waits.append((sem, val))
            if waits:
                self.ops[eng].append((waits, None, None))
        self.lastw = {}
        self.readers = {}
        self._fresh = True

    def renew_sems(self):
        self.gen = getattr(self, "gen", 0) + 1
        for e in self.COMPUTE:
            self.sem[e] = self.ges.enter_context(self.nc.semaphore(f"s_{e}_{self.gen}"))
            self.cnt[e] = 0
            for eng in self.waited:
                self.waited[eng].pop(e, None)

    def emit(self):
        nc = self.nc
        ops = self.ops
        with nc.Block() as block:
            def play(name):
                def run(e):
                    for waits, fn, inc in ops[name]:
                        for sem, val in waits:
                            e.wait_ge(sem, val)
                        if fn is not None:
                            fn(e).then_inc(inc[0], inc[1])
                return run
            block.tensor(play("pe"))
            block.scalar(play("act"))
            block.vector(play("dve"))
            block.gpsimd(play("pool"))
            block.sync(play("sp"))
        self.ops = {e: [] for e in ops}

    def mm(self, out, lhsT, rhs, start, stop, r=(), w=()):
        self.op("pe", lambda e: e.matmul(out, lhsT=lhsT, rhs=rhs, start=start, stop=stop), r, w)

    def tr(self, out, in_, ident, r=(), w=()):
        self.op("pe", lambda e: e.transpose(out, in_, ident), r, w)

    def tt(self, out, in0, in1, op, r=(), w=(), eng="dve"):
        self.op(eng, lambda e: e.tensor_tensor(out=out, in0=in0, in1=in1, op=op), r, w)

    def ts(self, out, in0, s1, s2, op0, op1=None, r=(), w=(), eng="dve"):
        if op1 is None:
            self.op(eng, lambda e: e.tensor_scalar(out=out, in0=in0, scalar1=s1, scalar2=None, op0=op0), r, w)
        else:
            self.op(eng, lambda e: e.tensor_scalar(out=out, in0=in0, scalar1=s1, scalar2=s2, op0=op0, op1=op1), r, w)

    def stt(self, out, in0, scalar, in1, op0, op1, r=(), w=()):
        self.op("dve", lambda e: e.scalar_tensor_tensor(out=out, in0=in0, scalar=scalar, in1=in1, op0=op0, op1=op1), r, w)

    def act(self, out, in_, func, bias=None, scale=None, accum=None, r=(), w=()):
        kw = {}
        if bias is not None:
            kw["bias"] = bias
        if scale is not None:
            kw["scale"] = scale
        if accum is not None:
            kw["accum_out"] = accum
        self.op("act", lambda e: e.activation(out=out, in_=in_, func=func, **kw), r, w)

    def cp(self, out, in_, r=(), w=(), eng="dve"):
        if eng == "act":
            self.op("act", lambda e: e.activation(out=out, in_=in_, func=AF.Copy), r, w)
        else:
            self.op(eng, lambda e: e.tensor_copy(out=out, in_=in_), r, w)

    def red(self, out, in_, op, r=(), w=(), axis=AX.X):
        self.op("dve", lambda e: e.tensor_reduce(out=out, in_=in_, axis=axis, op=op), r, w)

    def memset(self, ap, val, w=(), eng="dve"):
        self.op(eng, lambda e: e.memset(ap, val), (), w)


def bc(ap, shape):
    return ap.to_broadcast(list(shape))


def host_consts():
    c = {}
    t = np.arange(128)
    same = (t[:, None] // 64) == (t[None, :] // 64)
    c["tri_incl"] = (same & (t[:, None] <= t[None, :])).astype(np.float32)
    c["same"] = same.astype(np.float32)
    ci = np.zeros((128, 2), np.float32)
    ci[:64, 0] = 1
    ci[64:, 1] = 1
    c["chunkind"] = ci
    ci64 = np.zeros((128, 64), np.float32)
    ci64[:, 0:2] = ci
    c["chunkind64"] = ci64
    su = (same & (t[:, None] < t[None, :])).astype(np.float32)
    iu = (same & (t[:, None] <= t[None, :])).astype(np.float32)
    c["mask4"] = np.concatenate([su, iu, su, iu], axis=1)
    c["mask_lt"] = su.T.copy()
    c["ident"] = np.eye(128, dtype=np.float32)
    c["ones"] = np.ones((128, 128), np.float32)
    pos = np.arange(T)
    kaug = np.stack([np.ones(T), np.ones(T), pos // 64, pos % 64]).astype(np.float32)
    c["kaug"] = kaug
    qaug = np.zeros((6, 4, T), np.float32)
    for h, s in enumerate(SLOPES):
        qaug[h, 0] = -s * 64 * (pos // 64)
        qaug[h, 1] = -s * (pos % 64)
        qaug[h, 2] = s * 64
        qaug[h, 3] = s
    c["qaug"] = qaug
    ce = np.arange(256) * 16 + 31
    c["caug"] = np.stack([np.ones(256), np.ones(256), ce // 64, ce % 64]).astype(np.float32)
    p = np.arange(128)
    c["causal"] = np.where(p[None, :] <= p[:, None], 0.0, NEG).astype(np.float32)
    c["winlo"] = np.where(p[None, :] > p[:, None], 0.0, NEG).astype(np.float32)
    m = np.arange(-1, 7)
    c["cmask"] = np.where(16 * m[None, :] + 31 <= p[:, None], 0.0, NEG).astype(np.float32)
    rv = np.ones((128, 1), np.float32)
    rv[:31] = 0
    c["rowvalid0"] = rv
    n = np.arange(256)
    j = np.arange(64)
    ov = ((16 * n[:, None] < 64 * j[None, :] + 64) & (16 * n[:, None] + 31 >= 64 * j[None, :])).astype(np.float32)
    ov[255] = 0
    c["overlap"] = ov.reshape(2, 128, 64)
    sb = np.zeros((32, 128, 64), np.float32)
    for qi in range(32):
        tt = qi * 128 + p
        cur = tt // 64
        forced = (j[None, :] == 0) | (j[None, :] == cur[:, None]) | (j[None, :] == cur[:, None] - 1)
        valid = (64 * j[None, :]) <= tt[:, None]
        sb[qi] = np.where(valid, 1000.0 * forced, NEG)
    c["selbias"] = sb
    return c


CONST_SHAPES = None


def _dram_in(nc, name, arr_shape, dt=F32):
    return nc.dram_tensor(name, list(arr_shape), dt, kind="ExternalInput").ap()


def load_bcast(P, q, dst, src_row, n):
    P.dma(q, dst, src_row.unsqueeze(0).to_broadcast([128, n]), writes=[dst.tensor])


def rms_stats(P, sq_bf, nchunk, ones_bf, ps, rstd, ntok, eps=1e-6, dim=D):
    for c in range(nchunk):
        P.mm(ps[:, :ntok], ones_bf[:], sq_bf[:, c, :], c == 0, c == nchunk - 1, r=[sq_bf, ones_bf], w=[ps])
    P.ts(rstd[:, :ntok], ps[:, :ntok], 1.0 / dim, eps, ALU.mult, ALU.add, r=[ps], w=[rstd])
    P.act(rstd[:, :ntok], rstd[:, :ntok], AF.Sqrt, r=[rstd], w=[rstd])
    P.op("dve", lambda e: e.reciprocal(out=rstd[:, :ntok], in_=rstd[:, :ntok]), [Prog._k(rstd)], [Prog._k(rstd)])


def phase_ffn(P, A, l, hin, hout):
    nc = P.nc
    TB = 256
    with ExitStack() as es:
        P.es = es
        wup = P.sb([128, 8, 2 * DFF], BF16)
        wdn = P.sb([128, 22, D], BF16)
        ones_bf = P.sb([128, 128], BF16)
        cw = P.sb([128, 3, 44], F32)
        cb = P.sb([128, 44], F32)
        gpre = P.sb([128, 8], F32)
        gpost = P.sb([128, 8], F32)
        P.dma("pool", ones_bf[:], A["ones"], writes=[ones_bf])
        for c in range(8):
            P.dma("pool", wup[:, c, :], A["w_up"][l, c * 128:(c + 1) * 128, :], writes=[wup])
        for c in range(22):
            P.dma("pool", wdn[:, c, :], A["w_down"][l, c * 128:(c + 1) * 128, :], writes=[wdn])
        P.dma("sp", cw[:], A["conv_wT"][l], writes=[cw])
        P.dma("sp", cb[:], A["conv_bT"][l], writes=[cb])
        P.dma("sp", gpre[:], A["pre_ffn_normT"][l], writes=[gpre])
        P.dma("sp", gpost[:], A["post_ffn_normT"][l], writes=[gpost])

        hb512 = P.sb([128, 8, 512], F32)
        sq = P.sb([128, 8, TB], BF16)
        xn = P.sb([128, 8, TB], BF16)
        rstd = P.sb([128, TB], F32)
        hu = [P.sb([128, 2 + TB], F32, name=f"hu{i}") for i in range(4)]
        carry = P.sb([128, 44, 2], F32)
        cv = [P.sb([128, TB], F32, name=f"cv{i}") for i in range(4)]
        tmp = [P.sb([128, TB], F32, name=f"tmpf{i}") for i in range(2)]
        actT = P.sb([128, 22, TB], BF16)
        fo = P.sb([128, 8, TB], F32)
        pss = [P.ps([128, 512], F32, name=f"psf{i}") for i in range(6)]
        psst = P.ps([128, 512], F32, name="psfst")
        P.memset(carry[:], 0.0, w=[carry])
        GC = 2.0 * math.sqrt(2.0 / math.pi)
        pi = 0
        for b in range(T // TB):
            if b % 2 == 0:
                P.dma("sp", hb512[:], hin[b // 2], writes=[hb512])
            hb = hb512[:, :, (b % 2) * TB:(b % 2 + 1) * TB]
            P.act(sq[:], hb, AF.Square, r=[hb512], w=[sq])
            rms_stats(P, sq, 8, ones_bf, psst, rstd, TB)
            for c in range(8):
                P.stt(xn[:, c, :], hb[:, c, :], gpre[:, c:c + 1], rstd[:], ALU.mult, ALU.mult, r=[hb512, rstd, gpre], w=[xn])
            for i in range(22):
                res = []
                for half in range(2):
                    m = i + 22 * half
                    ps = pss[pi % 6]
                    pi += 1
                    for c in range(8):
                        P.mm(ps[:, :TB], wup[:, c, m * 128:(m + 1) * 128], xn[:, c, :], c == 0, c == 7, r=[wup, xn], w=[ps])
                    h = hu[(2 * i + half) % 4]
                    o = cv[(2 * i + half) % 4]
                    P.cp(h[:, 0:2], carry[:, m, :], r=[carry], w=[h], eng="pool")
                    P.act(h[:, 2:2 + TB], ps[:, :TB], AF.Copy, r=[ps], w=[h])
                    P.cp(carry[:, m, :], h[:, TB:TB + 2], r=[h], w=[carry], eng="pool")
                    P.ts(o[:], h[:, 0:TB], cw[:, 0, m:m + 1], cb[:, m:m + 1], ALU.mult, ALU.add, r=[h, cw, cb], w=[o])
                    P.stt(o[:], h[:, 1:1 + TB], cw[:, 1, m:m + 1], o[:], ALU.mult, ALU.add, r=[h, o], w=[o])
                    P.stt(o[:], h[:, 2:2 + TB], cw[:, 2, m:m + 1], o[:], ALU.mult, ALU.add, r=[h, o], w=[o])
                    res.append(o)
                gte, up = res
                t0_, t1_ = tmp
                P.tt(t0_[:], gte[:], gte[:], ALU.mult, r=[gte], w=[t0_], eng="pool")
                P.ts(t0_[:], t0_[:], 0.044715, 1.0, ALU.mult, ALU.add, r=[t0_], w=[t0_], eng="pool")
                P.tt(t0_[:], t0_[:], gte[:], ALU.mult, r=[t0_, gte], w=[t0_], eng="pool")
                P.act(t0_[:], t0_[:], AF.Sigmoid, scale=GC, r=[t0_], w=[t0_])
                P.tt(t1_[:], gte[:], up[:], ALU.mult, r=[gte, up], w=[t1_], eng="pool")
                P.tt(actT[:, i, :], t0_[:], t1_[:], ALU.mult, r=[t0_, t1_], w=[actT])
            for n in range(8):
                ps = pss[pi % 6]
                pi += 1
                for c in range(22):
                    P.mm(ps[:, :TB], wdn[:, c, n * 128:(n + 1) * 128], actT[:, c, :], c == 0, c == 21, r=[wdn, actT], w=[ps])
                P.act(fo[:, n, :], ps[:, :TB], AF.Copy, r=[ps], w=[fo])
            P.act(sq[:], fo[:], AF.Square, r=[fo], w=[sq])
            rms_stats(P, sq, 8, ones_bf, psst, rstd, TB)
            for c in range(8):
                P.stt(fo[:, c, :], fo[:, c, :], gpost[:, c:c + 1], rstd[:], ALU.mult, ALU.mult, r=[fo, rstd, gpost], w=[fo])
            P.tt(hb, hb, fo[:], ALU.add, r=[hb512, fo], w=[hb512])
            if b % 2 == 1:
                P.dma("sp", hout[b // 2], hb512[:], reads=[hb512], writes=["hout"])
        P.barrier()
        P.emit()
        P.renew_sems()
    P.es = P.ges


def phase_ple(P, A, l, hin, hout):
    TB = 512
    with ExitStack() as es:
        P.es = es
        wg = P.sb([128, 8, D], BF16)
        wpl = P.sb([128, 2, D], BF16)
        ones_bf = P.sb([128, 128], BF16)
        gple = P.sb([128, 8], F32)
        P.dma("pool", ones_bf[:], A["ones"], writes=[ones_bf])
        for c in range(8):
            P.dma("pool", wg[:, c, :], A["w_ple_gate"][l, c * 128:(c + 1) * 128, :], writes=[wg])
        for c in range(2):
            P.dma("pool", wpl[:, c, :], A["w_ple"][l, c * 128:(c + 1) * 128, :], writes=[wpl])
        P.dma("sp", gple[:], A["ple_normT"][l], writes=[gple])
        hb = [P.sb([128, 8, TB], F32, name=f"phb{i}") for i in range(2)]
        sq = P.sb([128, 8, TB], BF16)
        rstd = P.sb([128, TB], F32)
        fo = P.sb([128, 8, TB], F32)
        pb = P.sb([128, 2, TB], BF16)
        eo = P.sb([128, 8, TB], F32)
        h2b = P.sb([128, 8, TB], BF16)
        pss = [P.ps([128, 512], F32, name=f"psp{i}") for i in range(6)]
        psst = P.ps([128, 512], F32, name="pspst")
        pi = 0
        for b in range(T // TB):
            h = hb[b % 2]
            P.dma("sp", h[:], hin[b], writes=[h])
            P.dma("pool", pb[:], A["pT"][l, b], writes=[pb])
            P.cp(h2b[:], h[:], r=[h], w=[h2b], eng="act")
            for n in range(8):
                ps = pss[pi % 6]
                pi += 1
                for c in range(2):
                    P.mm(ps[:, :TB], wpl[:, c, n * 128:(n + 1) * 128], pb[:, c, :], c == 0, c == 1, r=[wpl, pb], w=[ps])
                P.act(eo[:, n, :], ps[:, :TB], AF.Copy, r=[ps], w=[eo])
            P.act(sq[:], eo[:], AF.Square, r=[eo], w=[sq])
            rms_stats(P, sq, 8, ones_bf, psst, rstd, TB)
            for c in range(8):
                P.stt(eo[:, c, :], eo[:, c, :], gple[:, c:c + 1], rstd[:], ALU.mult, ALU.mult, r=[eo, rstd, gple], w=[eo])
            for n in range(8):
                ps = pss[pi % 6]
                pi += 1
                for c in range(8):
                    P.mm(ps[:, :TB], wg[:, c, n * 128:(n + 1) * 128], h2b[:, c, :], c == 0, c == 7, r=[wg, h2b], w=[ps])
                P.act(fo[:, n, :], ps[:, :TB], AF.Sigmoid, r=[ps], w=[fo])
            P.tt(eo[:], eo[:], fo[:], ALU.mult, r=[eo, fo], w=[eo])
            P.tt(h[:], h[:], eo[:], ALU.add, r=[h, eo], w=[h])
            P.dma("sp", hout[b], h[:], reads=[h], writes=["hout"])
        P.barrier()
        P.emit()
        P.renew_sems()
    P.es = P.ges


def normT(g):
    L = g.shape[0]
    return np.ascontiguousarray(g.reshape(L, -1, 128).transpose(0, 2, 1))


def prep_shared(inp):
    f = lambda a: np.ascontiguousarray(np.asarray(a, dtype=np.float32))
    S = {}
    S.update(host_consts())
    for k in ("w_in", "w_out", "w_up", "w_down", "w_ple", "w_ple_gate"):
        S[k] = f(inp[k])
    L = DEPTH
    S["conv_wT"] = f(inp["conv_w"].reshape(L, 3, 44, 128).transpose(0, 3, 1, 2))
    S["conv_bT"] = f(inp["conv_b"].reshape(L, 44, 128).transpose(0, 2, 1))
    for k in ("pre_mix_norm", "post_mix_norm", "pre_ffn_norm", "post_ffn_norm", "ple_norm"):
        S[k + "T"] = f(normT(np.asarray(inp[k])))
    for k in ("shift_mu", "rw_w2", "rw_a2", "rw_g2", "rw_w0", "rw_a0", "rw_k_k", "rw_k_a", "rw_r_k", "rw_gn_w", "rw_gn_b"):
        S[k] = f(inp[k])
    z64 = np.zeros((DEPTH, 64, 384), np.float32)
    S["rw_w2p"] = f(np.concatenate([np.asarray(inp["rw_w2"]), z64], axis=1))
    S["rw_a2p"] = f(np.concatenate([z64, np.asarray(inp["rw_a2"])], axis=1))
    prep_s5(inp, S)
    for kv in ("k", "v"):
        S["cmp_w1_" + kv] = f(inp["cmp_w1_" + kv])
        w2 = np.asarray(inp["cmp_w2_" + kv])
        S["cmp_w2p_" + kv] = f(np.concatenate([w2, np.zeros_like(w2)], axis=2))
        pe = np.asarray(inp["cmp_pe_" + kv]).reshape(DEPTH, 16, 128)
        S["cmp_pe2_" + kv] = f(np.repeat(pe.transpose(0, 2, 1)[:, :, :, None], 64, axis=3))
    return S


def build(shared, mode="full"):
    nc = bass.Bass("TRN2", target_bir_lowering=False)
    A = {}
    for k, v in shared.items():
        A[k] = _dram_in(nc, k, v.shape)
    A["xT"] = _dram_in(nc, "xT", [8, 128, 8, 512])
    A["pT"] = _dram_in(nc, "pT", [DEPTH, 8, 128, 2, 512])
    yT = nc.dram_tensor("yT", [8, 128, 8, 512], F32, kind="ExternalOutput").ap()
    dbg = mode != "full" and not mode.startswith("layer")
    kind = "ExternalOutput" if dbg else "Internal"
    scr = {}

    def scratch(name, shape, dt=F32):
        scr[name] = nc.dram_tensor(name, list(shape), dt, kind=kind).ap()
        return scr[name]

    hA = scratch("hA", [8, 128, 8, 512])
    hB = scratch("hB", [8, 128, 8, 512])
    S = {}
    S["rkv"] = scratch("rkv", [T, 1152])
    S["lor"] = scratch("lor", [T, 1152])
    for nm in ("q0", "q1", "q2", "kc", "vc", "ks", "kw"):
        S[nm] = scratch(nm, [128, T], BF16)
    S["vsw"] = scratch("vsw", [T, 256], BF16)
    S["gates"] = scratch("gates", [T, 18])
    S["uT"] = scratch("uT", [256, T])
    mixT = scratch("mixT", [8, 128, 8, 512], BF16)
    if dbg:
        S["dbg"] = scratch("dbg", [128, 128])
        S["dbgO"] = scratch("dbgO", [128, 2400])
    with ExitStack() as es:
        P = Prog(nc, es)
        if mode.startswith("layer"):
            l = int(mode[5:])
            phase_proj(P, A, l, A["xT"], S)
            phase_rwkv(P, A, l, S, mixT)
            phase_nsa(P, A, l, S, mixT)
            phase_s5(P, A, l, S, mixT)
            phase_out(P, A, l, A["xT"], mixT, hA)
            phase_ffn(P, A, l, hA, hB)
            phase_ple(P, A, l, hB, yT)
        if mode == "full" or mode.startswith("ph:"):
            import os
            sel = mode[3:].split(",") if mode.startswith("ph:") else os.environ.get("FULLSEL", "proj,rwkv,nsa,s5,out,ffn,ple").split(",")
            nl = int(os.environ.get("NLAYERS", DEPTH))
            hcur = A["xT"]
            for l in range(nl):
                if "proj" in sel:
                    phase_proj(P, A, l, hcur, S)
                if "rwkv" in sel:
                    phase_rwkv(P, A, l, S, mixT, ntiles=int(os.environ.get("RWT", "32")), stage=int(os.environ.get("STAGE", "9")))
                if "nsa" in sel:
                    phase_nsa(P, A, l, S, mixT, ntiles=int(os.environ.get("NST", "32")))
                if "s5" in sel:
                    phase_s5(P, A, l, S, mixT)
                if "out" in sel:
                    phase_out(P, A, l, hcur, mixT, hA)
                if "ffn" in sel:
                    phase_ffn(P, A, l, hA, hB)
                if "ple" in sel:
                    phase_ple(P, A, l, hB, yT if l == nl - 1 else hA)
                hcur = hA
        if mode == "proj":
            phase_proj(P, A, 0, A["xT"], S)
            phase_s5(P, A, 0, S, mixT)
        if mode == "rwkv":
            phase_proj(P, A, 0, A["xT"], S)
            phase_rwkv(P, A, 0, S, mixT, ntiles=NT_DBG)
        if mode == "rwkv_only":
            import os
            phase_rwkv(P, A, 0, S, mixT, ntiles=1, stage=int(os.environ.get("STAGE", "9")))
        if mode == "s5":
            phase_s5(P, A, 0, S, mixT)
        if mode == "ffn":
            phase_ffn(P, A, 0, A["xT"], hA)
            phase_ple(P, A, 0, hA, yT)
        P.barrier()
        P.emit()
    return nc, list(scr.keys())


def kernel(**inputs):
    return run_layers(inputs)


def run_layers(inputs, cores=8):
    import os
    shared = prep_shared(inputs)
    x = np.asarray(inputs["x"], dtype=np.float32)
    p = np.asarray(inputs["p"], dtype=np.float32)
    hs = [np.ascontiguousarray(x[b].reshape(8, 512, 8, 128).transpose(0, 3, 2, 1)) for b in range(cores)]
    pTs = [np.ascontiguousarray(p[:, b].reshape(DEPTH, 8, 512, 2, 128).transpose(0, 1, 4, 3, 2)) for b in range(cores)]
    for l in range(DEPTH):
        nc, _ = build(shared, "layer%d" % l)
        in_maps = []
        for b in range(cores):
            m = dict(shared)
            m["xT"] = hs[b]
            m["pT"] = pTs[b]
            in_maps.append(m)
        res = run_bass_kernel_spmd(nc, in_maps, core_ids=list(range(cores)))
        hs = [np.ascontiguousarray(r["yT"]) for r in res.results]
    out = np.stack([np.ascontiguousarray(h.transpose(0, 3, 2, 1)).reshape(T, D) for h in hs], axis=0)
    return out.astype(np.float32)


def run(inputs, mode="full", cores=8):
    shared = prep_shared(inputs)
    nc, scr = build(shared, mode)
    x = np.asarray(inputs["x"], dtype=np.float32)
    p = np.asarray(inputs["p"], dtype=np.float32)
    in_maps = []
    import os
    boff = int(os.environ.get("BOFF", "0"))
    for b in range(boff, boff + cores):
        m = dict(shared)
        m["xT"] = np.ascontiguousarray(x[b].reshape(8, 512, 8, 128).transpose(0, 3, 2, 1))
        m["pT"] = np.ascontiguousarray(p[:, b].reshape(DEPTH, 8, 512, 2, 128).transpose(0, 1, 4, 3, 2))
        in_maps.append(m)
    cids = [int(c) for c in os.environ["CIDS"].split(",")] if "CIDS" in os.environ else list(range(cores))
    res = run_bass_kernel_spmd(nc, in_maps, core_ids=cids)
    if mode != "full":
        return res.results
    out = np.stack([np.ascontiguousarray(r["yT"].transpose(0, 3, 2, 1)).reshape(T, D) for r in res.results], axis=0)
    return out.astype(np.float32)


def phase_proj(P, A, l, hin, S):
    TB = 512
    with ExitStack() as es:
        P.es = es
        W1 = P.sb([128, 8, 1408], BF16)
        W2 = P.sb([128, 8, 1408], BF16)
        wn = P.sb([128, 8, 1426], BF16)
        ones_bf = P.sb([128, 128], BF16)
        gpre = P.sb([128, 8], F32)
        mu = P.sb([128, 1408], F32)
        stg = [P.sb([128, 1408], F32, name=f"stg{i}") for i in range(2)]
        w2b = P.sb([128, 384], BF16)
        a2b = P.sb([128, 384], BF16)
        g2b = P.sb([128, 384], BF16)
        P.dma("pool", ones_bf[:], A["ones"], writes=[ones_bf])
        P.dma("sp", gpre[:], A["pre_mix_normT"][l], writes=[gpre])
        load_bcast(P, "sp", mu[:], A["shift_mu"][l], 1408)
        P.dma("pool", w2b[:], A["rw_w2p"][l], writes=[w2b])
        P.dma("pool", a2b[:], A["rw_a2p"][l], writes=[a2b])
        P.dma("pool", g2b[:], A["rw_g2"][l], writes=[g2b])
        for c in range(8):
            P.dma("pool", wn[:, c, :], A["w_in"][l, c * 128:(c + 1) * 128, 1408:2834], writes=[wn])
            st = stg[c % 2]
            P.dma("sp", st[:], A["w_in"][l, c * 128:(c + 1) * 128, 0:1408], writes=[st])
            P.tt(W2[:, c, :], st[:], mu[:], ALU.mult, r=[st, mu], w=[W2])
            P.tt(W1[:, c, :], st[:], W2[:, c, :], ALU.subtract, r=[st, W2], w=[W1], eng="pool")

        hb = P.sb([128, 8, TB], F32)
        sq = P.sb([128, 8, TB], BF16)
        xn = P.sb([128, 8, 1 + TB], BF16)
        rstd = P.sb([128, TB], F32)
        rkv = [P.sb([128, 1152], F32, name=f"rkv{i}") for i in range(2)]
        lor = [P.sb([128, 1152], F32, name=f"lor{i}") for i in range(2)]
        twa = P.sb([128, TB], BF16)
        tg = P.sb([128, TB], BF16)
        fmo = [P.sb([128, TB], BF16, name=f"fmo{i}") for i in range(3)]
        uo = [P.sb([128, TB], F32, name=f"uo{i}") for i in range(2)]
        vsw = [P.sb([128, 256], BF16, name=f"vsw{i}") for i in range(2)]
        gts = [P.sb([128, 18], F32, name=f"gts{i}") for i in range(2)]
        pss = [P.ps([128, 512], F32, name=f"psj{i}") for i in range(6)]
        psst = P.ps([128, 512], F32, name="psjst")
        P.memset(xn[:], 0.0, w=[xn])
        pi = 0

        def shifted_group(ps_ap, cols, tok_lo, tok_n, fm):
            for c in range(8):
                for sh, W in ((1, W1), (0, W2)):
                    xs = xn[:, c, sh + tok_lo: sh + tok_lo + tok_n]
                    ws = W[:, c, cols]
                    first = (c == 0 and sh == 1)
                    last = (c == 7 and sh == 0)
                    if fm:
                        P.mm(ps_ap, ws, xs, first, last, r=[W1, W2, xn], w=[ps_ap.tensor])
                    else:
                        P.mm(ps_ap, xs, ws, first, last, r=[W1, W2, xn], w=[ps_ap.tensor])

        for b in range(T // TB):
            t0 = b * TB
            ts_ = slice(t0, t0 + TB)
            P.dma("sp", hb[:], hin[b], writes=[hb])
            P.act(sq[:], hb[:], AF.Square, r=[hb], w=[sq])
            rms_stats(P, sq, 8, ones_bf, psst, rstd, TB)
            if b > 0:
                P.cp(xn[:, :, 0:1], xn[:, :, TB:TB + 1], r=[xn], w=[xn])
            for c in range(8):
                P.stt(xn[:, c, 1:1 + TB], hb[:, c, :], gpre[:, c:c + 1], rstd[:], ALU.mult, ALU.mult, r=[hb, rstd, gpre], w=[xn])
            ps = pss[pi % 6]
            pi += 1
            shifted_group(ps[:, :], slice(1152, 1280), 0, TB, True)
            P.act(twa[0:64, :], ps[0:64, :], AF.Tanh, r=[ps], w=[twa])
            P.act(twa[64:128, :], ps[64:128, :], AF.Copy, r=[ps], w=[twa])
            ps = pss[pi % 6]
            pi += 1
            shifted_group(ps[:, :], slice(1280, 1408), 0, TB, True)
            P.act(tg[:, :], ps[:, :], AF.Sigmoid, r=[ps], w=[tg])
            for tt in range(4):
                tsl = slice(tt * 128, (tt + 1) * 128)
                rk = rkv[tt % 2]
                for j in range(3):
                    ps = pss[pi % 6]
                    pi += 1
                    shifted_group(ps[:, 0:384], slice(j * 384, (j + 1) * 384), tt * 128, 128, False)
                    if j == 1:
                        P.cp(rk[:, j * 384:(j + 1) * 384], ps[:, 0:384], r=[ps], w=[rk])
                    else:
                        P.act(rk[:, j * 384:(j + 1) * 384], ps[:, 0:384], AF.Copy, r=[ps], w=[rk])
                P.dma("sp", S["rkv"][t0 + tt * 128: t0 + (tt + 1) * 128, :], rk[:], reads=[rk], writes=["rkv_d"])
                lo = lor[tt % 2]
                for j, (src, wgt) in enumerate(((twa, w2b), (twa, a2b), (tg, g2b))):
                    ps = pss[pi % 6]
                    pi += 1
                    P.mm(ps[:, 0:384], src[:, tsl], wgt[:, :], True, True, r=[src, wgt], w=[ps])
                    P.cp(lo[:, j * 384:(j + 1) * 384], ps[:, 0:384], r=[ps], w=[lo])
                P.dma("sp", S["lor"][t0 + tt * 128: t0 + (tt + 1) * 128, :], lo[:], reads=[lo], writes=["lor_d"])
                ps = pss[pi % 6]
                pi += 1
                for (c0, n, o0) in ((2176 - 1408, 128, 0), (2432 - 1408, 128, 128), (2560 - 1408, 18, 256)):
                    for c in range(8):
                        P.mm(ps[:, o0:o0 + n], xn[:, c, 1 + tt * 128: 1 + (tt + 1) * 128], wn[:, c, c0:c0 + n], c == 0, c == 7,
                             r=[xn, wn], w=[ps])
                vv = vsw[tt % 2]
                gg = gts[tt % 2]
                P.cp(vv[:], ps[:, 0:256], r=[ps], w=[vv])
                P.act(gg[:], ps[:, 256:274], AF.Sigmoid, r=[ps], w=[gg])
                P.dma("sp", S["vsw"][t0 + tt * 128: t0 + (tt + 1) * 128, :], vv[:], reads=[vv], writes=["vsw_d"])
                P.dma("sp", S["gates"][t0 + tt * 128: t0 + (tt + 1) * 128, :], gg[:], reads=[gg], writes=["gates_d"])
            for k, (c0, dname, scale) in enumerate(((0, "q0", 0.125), (128, "q1", 0.125), (256, "q2", 0.125),
                                                    (1792 - 1408, "kc", 1.0), (1920 - 1408, "vc", 1.0),
                                                    (2048 - 1408, "ks", 1.0), (2304 - 1408, "kw", 1.0))):
                ps = pss[pi % 6]
                pi += 1
                for c in range(8):
                    P.mm(ps[:, :], wn[:, c, c0:c0 + 128], xn[:, c, 1:1 + TB], c == 0, c == 7, r=[wn, xn], w=[ps])
                o = fmo[k % 3]
                P.act(o[:], ps[:], AF.Copy, scale=scale, r=[ps], w=[o])
                P.dma("sp", S[dname][:, ts_], o[:], reads=[o], writes=[dname + "_d"])
            for k in range(2):
                ps = pss[pi % 6]
                pi += 1
                c0 = 2578 - 1408 + k * 128
                for c in range(8):
                    P.mm(ps[:, :], wn[:, c, c0:c0 + 128], xn[:, c, 1:1 + TB], c == 0, c == 7, r=[wn, xn], w=[ps])
                o = uo[k]
                P.cp(o[:], ps[:], r=[ps], w=[o])
                P.dma("sp", S["uT"][k * 128:(k + 1) * 128, ts_], o[:], reads=[o], writes=["uT_d"])
        P.barrier()
        P.emit()
        P.renew_sems()
    P.es = P.ges


def phase_out(P, A, l, hin, mixT, hout):
    TB = 512
    with ExitStack() as es:
        P.es = es
        wo = P.sb([128, 8, D], BF16)
        ones_bf = P.sb([128, 128], BF16)
        gpost = P.sb([128, 8], F32)
        P.dma("pool", ones_bf[:], A["ones"], writes=[ones_bf])
        P.dma("sp", gpost[:], A["post_mix_normT"][l], writes=[gpost])
        for c in range(8):
            P.dma("pool", wo[:, c, :], A["w_out"][l, c * 128:(c + 1) * 128, :], writes=[wo])
        hb = [P.sb([128, 8, TB], F32, name=f"ohb{i}") for i in range(2)]
        mx = [P.sb([128, 8, TB], BF16, name=f"omx{i}") for i in range(2)]
        fo = P.sb([128, 8, TB], F32)
        sq = P.sb([128, 8, TB], BF16)
        rstd = P.sb([128, TB], F32)
        pss = [P.ps([128, 512], F32, name=f"pso{i}") for i in range(6)]
        psst = P.ps([128, 512], F32, name="psost")
        pi = 0
        for b in range(T // TB):
            ts_ = slice(b * TB, (b + 1) * TB)
            h = hb[b % 2]
            m = mx[b % 2]
            P.dma("sp", h[:], hin[b], writes=[h])
            P.dma("sp", m[:], mixT[b], writes=[m])
            for n in range(8):
                ps = pss[pi % 6]
                pi += 1
                for c in range(8):
                    P.mm(ps[:], wo[:, c, n * 128:(n + 1) * 128], m[:, c, :], c == 0, c == 7, r=[wo, m], w=[ps])
                P.act(fo[:, n, :], ps[:], AF.Copy, r=[ps], w=[fo])
            P.act(sq[:], fo[:], AF.Square, r=[fo], w=[sq])
            rms_stats(P, sq, 8, ones_bf, psst, rstd, TB)
            for c in range(8):
                P.stt(fo[:, c, :], fo[:, c, :], gpost[:, c:c + 1], rstd[:], ALU.mult, ALU.mult, r=[fo, rstd, gpost], w=[fo])
            P.tt(h[:], h[:], fo[:], ALU.add, r=[h, fo], w=[h])
            P.dma("sp", hout[b], h[:], reads=[h], writes=["hout"])
        P.barrier()
        P.emit()
        P.renew_sems()
    P.es = P.ges


def prep_s5(inp, S):
    f = lambda a: np.ascontiguousarray(np.asarray(a, dtype=np.float32))
    L = DEPTH
    st = lambda a: f(np.asarray(a).reshape(L, 8, 128).transpose(0, 2, 1))
    S["s5_lam_reT"] = st(inp["s5_lam_re"])
    S["s5_lam_imT"] = st(inp["s5_lam_im"])
    S["s5_logdtT"] = st(np.repeat(np.asarray(inp["s5_log_dt"])[:, :, None], 64, axis=2))
    bre = np.zeros((L, 256, 1024), np.float32)
    bim = np.zeros((L, 256, 1024), np.float32)
    cre = np.zeros((L, 1024, 256), np.float32)
    cim = np.zeros((L, 1024, 256), np.float32)
    for g in range(16):
        bre[:, g * 16:(g + 1) * 16, g * 64:(g + 1) * 64] = np.asarray(inp["s5_b_re"])[:, g].transpose(0, 2, 1)
        bim[:, g * 16:(g + 1) * 16, g * 64:(g + 1) * 64] = np.asarray(inp["s5_b_im"])[:, g].transpose(0, 2, 1)
        cre[:, g * 64:(g + 1) * 64, g * 16:(g + 1) * 16] = np.asarray(inp["s5_c_re"])[:, g].transpose(0, 2, 1)
        cim[:, g * 64:(g + 1) * 64, g * 16:(g + 1) * 16] = np.asarray(inp["s5_c_im"])[:, g].transpose(0, 2, 1)
    S["s5_bre"], S["s5_bim"], S["s5_cre"], S["s5_cim"] = bre, bim, cre, cim
    S["s5_dT"] = f(np.asarray(inp["s5_d"]).reshape(L, 2, 128).transpose(0, 2, 1))
    S["s5_w_glu"] = f(inp["s5_w_glu"])


def phase_s5(P, A, l, S, mixT):
    TB = 512
    NL = 9
    PI = math.pi
    with ExitStack() as es:
        P.es = es
        lre = P.sb([128, 8], F32)
        lim = P.sb([128, 8], F32)
        ldt = P.sb([128, 8], F32)
        bre = P.sb([128, 2, 1024], BF16)
        bim = P.sb([128, 2, 1024], BF16)
        cre = P.sb([128, 8, 256], BF16)
        cim = P.sb([128, 8, 256], BF16)
        dsk = P.sb([128, 2], F32)
        wgl = P.sb([128, 2, 512], BF16)
        P.dma("sp", lre[:], A["s5_lam_reT"][l], writes=[lre])
        P.dma("sp", lim[:], A["s5_lam_imT"][l], writes=[lim])
        P.dma("sp", ldt[:], A["s5_logdtT"][l], writes=[ldt])
        P.dma("sp", dsk[:], A["s5_dT"][l], writes=[dsk])
        for c in range(2):
            P.dma("pool", bre[:, c, :], A["s5_bre"][l, c * 128:(c + 1) * 128, :], writes=[bre])
            P.dma("pool", bim[:, c, :], A["s5_bim"][l, c * 128:(c + 1) * 128, :], writes=[bim])
            P.dma("pool", wgl[:, c, :], A["s5_w_glu"][l, c * 128:(c + 1) * 128, :], writes=[wgl])
        for c in range(8):
            P.dma("pool", cre[:, c, :], A["s5_cre"][l, c * 128:(c + 1) * 128, :], writes=[cre])
            P.dma("pool", cim[:, c, :], A["s5_cim"][l, c * 128:(c + 1) * 128, :], writes=[cim])
        sm = lambda n: P.sb([128, 8], F32, name=n)
        dt, mag, ang, x, acc, tmp = sm("dt"), sm("mag"), sm("ang"), sm("x5"), sm("acc5"), sm("tmp5")
        abr, abi, fre, fim, den, t2 = sm("abr"), sm("abi"), sm("fre"), sm("fim"), sm("den"), sm("t25")
        nfim = sm("nfim")
        P.act(dt[:], ldt[:], AF.Exp, r=[ldt], w=[dt])
        P.tt(mag[:], lre[:], dt[:], ALU.mult, r=[lre, dt], w=[mag])
        P.act(mag[:], mag[:], AF.Exp, r=[mag], w=[mag])
        P.tt(ang[:], lim[:], dt[:], ALU.mult, r=[lim, dt], w=[ang])

        def sin_of(dst, shift):
            P.ts(x[:], ang[:], shift + PI, None, ALU.add, r=[ang], w=[x])
            P.cp(acc[:], x[:], r=[x], w=[acc])
            for k in (1, 2, 3):
                P.ts(tmp[:], x[:], 2 * PI * k, -2 * PI, ALU.is_ge, ALU.mult, r=[x], w=[tmp])
                P.tt(acc[:], acc[:], tmp[:], ALU.add, r=[acc, tmp], w=[acc])
            P.ts(acc[:], acc[:], -PI, None, ALU.add, r=[acc], w=[acc])
            P.ts(tmp[:], acc[:], -1.0, PI, ALU.mult, ALU.add, r=[acc], w=[tmp])
            P.tt(tmp[:], tmp[:], acc[:], ALU.min, r=[tmp, acc], w=[tmp])
            P.ts(acc[:], acc[:], -1.0, -PI, ALU.mult, ALU.add, r=[acc], w=[acc])
            P.tt(acc[:], acc[:], tmp[:], ALU.max, r=[tmp, acc], w=[acc])
            P.tt(t2[:], acc[:], acc[:], ALU.mult, r=[acc], w=[t2])
            P.ts(tmp[:], t2[:], 1.0 / 6227020800.0, None, ALU.mult, r=[t2], w=[tmp])
            for cf in (-1.0 / 39916800.0, 1.0 / 362880.0, -1.0 / 5040.0, 1.0 / 120.0, -1.0 / 6.0):
                P.stt(tmp[:], tmp[:], cf, t2[:], ALU.add, ALU.mult, r=[tmp, t2], w=[tmp])
            P.stt(dst[:], tmp[:], 1.0, acc[:], ALU.add, ALU.mult, r=[tmp, acc], w=[dst])

        sin_of(abi, 0.0)
        sin_of(abr, PI / 2)
        P.tt(abr[:], abr[:], mag[:], ALU.mult, r=[abr, mag], w=[abr])
        P.tt(abi[:], abi[:], mag[:], ALU.mult, r=[abi, mag], w=[abi])
        P.tt(den[:], lre[:], lre[:], ALU.mult, r=[lre], w=[den])
        P.tt(t2[:], lim[:], lim[:], ALU.mult, r=[lim], w=[t2])
        P.tt(den[:], den[:], t2[:], ALU.add, r=[den, t2], w=[den])
        P.op("dve", lambda e: e.reciprocal(out=den[:], in_=den[:]), [Prog._k(den)], [Prog._k(den)])
        P.ts(tmp[:], abr[:], -1.0, None, ALU.add, r=[abr], w=[tmp])
        P.tt(fre[:], tmp[:], lre[:], ALU.mult, r=[tmp, lre], w=[fre])
        P.tt(t2[:], abi[:], lim[:], ALU.mult, r=[abi, lim], w=[t2])
        P.tt(fre[:], fre[:], t2[:], ALU.add, r=[fre, t2], w=[fre])
        P.tt(fre[:], fre[:], den[:], ALU.mult, r=[fre, den], w=[fre])
        P.tt(fim[:], abi[:], lre[:], ALU.mult, r=[abi, lre], w=[fim])
        P.tt(t2[:], tmp[:], lim[:], ALU.mult, r=[tmp, lim], w=[t2])
        P.tt(fim[:], fim[:], t2[:], ALU.subtract, r=[fim, t2], w=[fim])
        P.tt(fim[:], fim[:], den[:], ALU.mult, r=[fim, den], w=[fim])
        P.ts(nfim[:], fim[:], -1.0, None, ALU.mult, r=[fim], w=[nfim])
        pwr = [abr] + [sm(f"pwr{k}") for k in range(1, NL)]
        pwi = [abi] + [sm(f"pwi{k}") for k in range(1, NL)]
        npwi = [sm(f"npwi{k}") for k in range(NL)]
        for k in range(NL):
            P.ts(npwi[k][:], pwi[k][:], -1.0, None, ALU.mult, r=[pwi[k]], w=[npwi[k]])
            if k + 1 < NL:
                P.tt(pwr[k + 1][:], pwr[k][:], pwr[k][:], ALU.mult, r=[pwr[k]], w=[pwr[k + 1]])
                P.tt(t2[:], pwi[k][:], pwi[k][:], ALU.mult, r=[pwi[k]], w=[t2])
                P.tt(pwr[k + 1][:], pwr[k + 1][:], t2[:], ALU.subtract, r=[pwr[k + 1], t2], w=[pwr[k + 1]])
                P.tt(pwi[k + 1][:], pwr[k][:], pwi[k][:], ALU.mult, r=[pwr[k], pwi[k]], w=[pwi[k + 1]])
                P.ts(pwi[k + 1][:], pwi[k + 1][:], 2.0, None, ALU.mult, r=[pwi[k + 1]], w=[pwi[k + 1]])
        if "dbg" in S:
            for i_, t_ in enumerate((dt, mag, ang, abr, abi, fre, fim, den, pwr[NL - 1], pwi[NL - 1])):
                P.dma("sp", S["dbg"][:, i_ * 8:(i_ + 1) * 8], t_[:], reads=[t_])
        cst_re = P.sb([128, 8], F32)
        cst_im = P.sb([128, 8], F32)
        P.memset(cst_re[:], 0.0, w=[cst_re])
        P.memset(cst_im[:], 0.0, w=[cst_im])
        uf = P.sb([128, 2, TB], F32)
        ub = P.sb([128, 2, TB], BF16)
        Are = [P.sb([128, TB], F32, name=f"Are{i}") for i in range(2)]
        Aim = [P.sb([128, TB], F32, name=f"Aim{i}") for i in range(2)]
        sre = P.sb([128, 8, TB], BF16)
        sim = P.sb([128, 8, TB], BF16)
        yv = P.sb([128, 2, TB], F32)
        yt = P.sb([128, TB], F32)
        yg = P.sb([128, 2, TB], BF16)
        gl = P.sb([128, 4, TB], F32)
        ob = P.sb([128, 2, TB], BF16)
        c4 = P.sb([128, 4], F32)
        psx = [P.ps([128, 512], F32, name=f"ps5{i}") for i in range(4)]
        psy = [P.ps([128, 512], F32, name=f"ps5y{i}") for i in range(2)]
        GC = 2.0 * math.sqrt(2.0 / math.pi)
        uT = S["uT"].rearrange("(c p) t -> p c t", p=128)
        for b in range(T // TB):
            ts_ = slice(b * TB, (b + 1) * TB)
            P.dma("sp", uf[:], uT[:, :, ts_], writes=[uf])
            P.cp(ub[:], uf[:], r=[uf], w=[ub], eng="act")
            for m in range(8):
                kc = m // 4
                pr, pim = psx[(2 * m) % 4], psx[(2 * m + 1) % 4]
                P.mm(pr[:], bre[:, kc, m * 128:(m + 1) * 128], ub[:, kc, :], True, True, r=[bre, ub], w=[pr])
                P.mm(pim[:], bim[:, kc, m * 128:(m + 1) * 128], ub[:, kc, :], True, True, r=[bim, ub], w=[pim])
                a_re, a_im = Are[0], Aim[0]
                mc = slice(m, m + 1)
                P.ts(a_re[:], pr[:], fre[:, mc], None, ALU.mult, r=[pr, fre], w=[a_re])
                P.stt(a_re[:], pim[:], nfim[:, mc], a_re[:], ALU.mult, ALU.add, r=[pim, nfim, a_re], w=[a_re])
                P.ts(a_im[:], pim[:], fre[:, mc], None, ALU.mult, r=[pim, fre], w=[a_im])
                P.stt(a_im[:], pr[:], fim[:, mc], a_im[:], ALU.mult, ALU.add, r=[pr, fim, a_im], w=[a_im])
                P.tt(c4[:, 0:1], abr[:, mc], cst_re[:, mc], ALU.mult, r=[abr, cst_re], w=[c4])
                P.tt(c4[:, 1:2], abi[:, mc], cst_im[:, mc], ALU.mult, r=[abi, cst_im], w=[c4])
                P.tt(c4[:, 2:3], abr[:, mc], cst_im[:, mc], ALU.mult, r=[abr, cst_im], w=[c4])
                P.tt(c4[:, 3:4], abi[:, mc], cst_re[:, mc], ALU.mult, r=[abi, cst_re], w=[c4])
                P.tt(a_re[:, 0:1], a_re[:, 0:1], c4[:, 0:1], ALU.add, r=[a_re, c4], w=[a_re])
                P.tt(a_re[:, 0:1], a_re[:, 0:1], c4[:, 1:2], ALU.subtract, r=[a_re, c4], w=[a_re])
                P.tt(a_im[:, 0:1], a_im[:, 0:1], c4[:, 2:3], ALU.add, r=[a_im, c4], w=[a_im])
                P.tt(a_im[:, 0:1], a_im[:, 0:1], c4[:, 3:4], ALU.add, r=[a_im, c4], w=[a_im])
                cur = 0
                for k in range(NL):
                    d = 1 << k
                    sr, si = Are[cur], Aim[cur]
                    dr, di = Are[1 - cur], Aim[1 - cur]
                    P.stt(dr[:, d:], sr[:, :TB - d], pwr[k][:, mc], sr[:, d:], ALU.mult, ALU.add, r=[sr, pwr[k]], w=[dr])
                    P.stt(dr[:, d:], si[:, :TB - d], npwi[k][:, mc], dr[:, d:], ALU.mult, ALU.add, r=[si, npwi[k], dr], w=[dr])
                    P.stt(di[:, d:], si[:, :TB - d], pwr[k][:, mc], si[:, d:], ALU.mult, ALU.add, r=[si, pwr[k]], w=[di])
                    P.stt(di[:, d:], sr[:, :TB - d], pwi[k][:, mc], di[:, d:], ALU.mult, ALU.add, r=[sr, pwi[k], di], w=[di])
                    P.cp(dr[:, :d], sr[:, :d], r=[sr], w=[dr], eng="pool")
                    P.cp(di[:, :d], si[:, :d], r=[si], w=[di], eng="pool")
                    cur = 1 - cur
                fr, fi = Are[cur], Aim[cur]
                P.cp(cst_re[:, mc], fr[:, TB - 1:TB], r=[fr], w=[cst_re])
                P.cp(cst_im[:, mc], fi[:, TB - 1:TB], r=[fi], w=[cst_im])
                P.cp(sre[:, m, :], fr[:], r=[fr], w=[sre], eng="act")
                P.act(sim[:, m, :], fi[:], AF.Copy, scale=-1.0, r=[fi], w=[sim])
                if cur != 0:
                    pass
            for j in range(2):
                py = psy[j]
                n = 0
                for kc in range(4 * j, 4 * j + 4):
                    P.mm(py[:], cre[:, kc, j * 128:(j + 1) * 128], sre[:, kc, :], n == 0, False, r=[cre, sre], w=[py])
                    n += 1
                    P.mm(py[:], cim[:, kc, j * 128:(j + 1) * 128], sim[:, kc, :], False, kc == 4 * j + 3, r=[cim, sim], w=[py])
                P.stt(yv[:, j, :], uf[:, j, :], dsk[:, j:j + 1], py[:], ALU.mult, ALU.add, r=[uf, dsk, py], w=[yv])
                P.tt(yt[:], yv[:, j, :], yv[:, j, :], ALU.mult, r=[yv], w=[yt])
                P.ts(yt[:], yt[:], 0.044715, 1.0, ALU.mult, ALU.add, r=[yt], w=[yt])
                P.tt(yt[:], yt[:], yv[:, j, :], ALU.mult, r=[yt, yv], w=[yt])
                P.act(yt[:], yt[:], AF.Sigmoid, scale=GC, r=[yt], w=[yt])
                P.tt(yg[:, j, :], yt[:], yv[:, j, :], ALU.mult, r=[yt, yv], w=[yg])
            for n in range(4):
                pg = psx[n]
                for kc in range(2):
                    P.mm(pg[:], wgl[:, kc, n * 128:(n + 1) * 128], yg[:, kc, :], kc == 0, kc == 1, r=[wgl, yg], w=[pg])
                if n < 2:
                    P.cp(gl[:, n, :], pg[:], r=[pg], w=[gl])
                else:
                    P.act(gl[:, n, :], pg[:], AF.Sigmoid, r=[pg], w=[gl])
            P.tt(ob[:], gl[:, 0:2, :], gl[:, 2:4, :], ALU.mult, r=[gl], w=[ob])
            P.dma("sp", mixT[b][:, 6:8, :], ob[:], reads=[ob], writes=["mix_s5"])
        P.barrier()
        P.emit()
        P.renew_sems()
    P.es = P.ges


def phase_rwkv(P, A, l, S, mixT, ntiles=T // 128, stage=9):
    C = C_DEC
    with ExitStack() as es:
        P.es = es

        def cst(name, shape):
            t_ = P.sb(shape, F32, name="c_" + name)
            P.dma("sp", t_[:], A[name], writes=[t_])
            return t_

        tri = cst("tri_incl", [128, 128])
        same = cst("same", [128, 128])
        cind = cst("chunkind", [128, 2])
        cind64 = cst("chunkind64", [128, 64])
        mask4 = cst("mask4", [128, 512])
        masklt = cst("mask_lt", [128, 128])
        ident = cst("ident", [128, 128])
        identb = P.sb([128, 128], BF16)
        P.dma("pool", identb[:], A["ident"], writes=[identb])
        par = {}
        for nm in ("rw_w0", "rw_a0", "rw_k_k", "rw_k_a", "rw_r_k", "rw_gn_w", "rw_gn_b"):
            t_ = P.sb([128, 384], F32, name="p_" + nm)
            load_bcast(P, "sp", t_[:], A[nm][l], 384)
            par[nm] = t_
        ST = P.sb([128, 3, 2, 64], F32)
        P.memset(ST[:], 0.0, w=[ST])
        fmz = [[P.sb([128, 512], F32, name=f"fmz{g}{e}") for e in range(2)] for g in range(3)]
        Bdz = [P.sb([128, 384], F32, name=f"Bdz{c}") for c in range(2)]
        Kdz = [P.sb([128, 384], F32, name=f"Kdz{c}") for c in range(2)]
        rkvb = [P.sb([128, 1152], F32, name=f"rkvb{i}") for i in range(2)]
        lorb = [P.sb([128, 1152], F32, name=f"lorb{i}") for i in range(2)]
        w = lambda n: P.sb([128, 384], F32, name="w_" + n)
        sig, a, kk, kp, cs, tmpx, tmpe, Ep, Em, Ex, Eend, ka, Bd, Kd, t4, On = [w(n) for n in (
            "sig", "a", "kk", "kp", "cs", "tmpx", "tmpe", "Ep", "Em", "Ex", "Eend", "ka", "Bd", "Kd", "t4", "On")]
        Q4 = P.sb([128, 4, 384], F32)
        fm = [P.sb([128, 512], F32, name=f"fm{g}") for g in range(3)]
        Gs = P.sb([128, 6, 512], F32)
        MA = [P.sb([128, 6, 128], F32, name=f"MA{i}") for i in range(2)]
        MTA = [P.sb([128, 6, 128], F32, name=f"MTA{i}") for i in range(2)]
        Rall = P.sb([128, 6, 128], F32)
        XT = P.sb([128, 384], F32)
        WT = P.sb([128, 384], F32)
        P.memset(XT[:], 0.0, w=[XT])
        P.memset(WT[:], 0.0, w=[WT])
        O = P.sb([128, 384], F32)
        Ob = P.sb([128, 384], BF16)
        OT = [P.sb([128, 3, 512], BF16, name=f"OT{i}") for i in range(2)]
        ss = P.sb([128, 6], F32)
        bsum = P.sb([128, 6], F32)
        s1 = P.sb([128, 6], F32)
        s2 = P.sb([128, 6], F32)
        m2 = P.sb([128, 6], F32)
        pcs = P.sb([128, 6], F32)
        pool_ps = [P.ps([128, 512], F32, name=f"psr{i}") for i in range(5)]
        psM_fixed = [P.ps([128, 512], F32, name=f"psrM{i}") for i in range(2)]
        rr = [0]

        def nps():
            p_ = pool_ps[rr[0] % 5]
            rr[0] += 1
            return p_

        v3 = lambda t_: t_[:].rearrange("p (h j) -> p h j", j=64)
        b3 = lambda t_: t_[:].unsqueeze(2).to_broadcast([128, 6, 64])
        recip = lambda t_: P.op("dve", lambda e: e.reciprocal(out=t_[:], in_=t_[:]), [Prog._k(t_)], [Prog._k(t_)])

        for ti in range(ntiles):
            t0 = ti * 128
            RK = rkvb[ti % 2]
            LO = lorb[ti % 2]
            P.dma("sp", RK[:], S["rkv"][t0:t0 + 128, :], writes=[RK])
            P.dma("sp", LO[:], S["lor"][t0:t0 + 128, :], writes=[LO])
            R, Kx, V = RK[:, 0:384], RK[:, 384:768], RK[:, 768:1152]
            XW, XA, G = LO[:, 0:384], LO[:, 384:768], LO[:, 768:1152]
            P.tt(sig[:], XW, par["rw_w0"][:], ALU.add, r=[LO, par["rw_w0"]], w=[sig])
            P.act(sig[:], sig[:], AF.Sigmoid, r=[sig], w=[sig])
            P.tt(a[:], XA, par["rw_a0"][:], ALU.add, r=[LO, par["rw_a0"]], w=[a])
            P.act(a[:], a[:], AF.Sigmoid, r=[a], w=[a])
            P.tt(kk[:], Kx, par["rw_k_k"][:], ALU.mult, r=[RK, par["rw_k_k"]], w=[kk])
            P.tt(t4[:], kk[:], kk[:], ALU.mult, r=[kk], w=[t4])
            P.red(ss[:], v3(t4), ALU.add, r=[t4], w=[ss])
            P.ts(ss[:], ss[:], 1e-24, None, ALU.max, r=[ss], w=[ss])
            P.act(ss[:], ss[:], AF.Sqrt, r=[ss], w=[ss])
            recip(ss)
            P.tt(v3(kk), v3(kk), b3(ss), ALU.mult, r=[kk, ss], w=[kk])
            P.stt(t4[:], a[:], -1.0, par["rw_k_a"][:], ALU.add, ALU.mult, r=[a, par["rw_k_a"]], w=[t4])
            P.stt(kp[:], t4[:], 1.0, Kx, ALU.add, ALU.mult, r=[t4, RK], w=[kp])
            psA, psB = nps(), nps()
            P.mm(psA[:, 0:384], tri[:], sig[:], True, True, r=[tri, sig], w=[psA])
            P.mm(psB[:, 0:384], same[:], sig[:], True, True, r=[same, sig], w=[psB])
            P.cp(cs[:], psA[:, 0:384], r=[psA], w=[cs], eng="act")
            P.act(Ep[:], cs[:], AF.Exp, scale=-C, r=[cs], w=[Ep])
            P.act(Em[:], cs[:], AF.Exp, scale=C, r=[cs], w=[Em])
            P.tt(tmpx[:], cs[:], sig[:], ALU.subtract, r=[cs, sig], w=[tmpx])
            P.act(Ex[:], tmpx[:], AF.Exp, scale=-C, r=[tmpx], w=[Ex])
            P.tt(tmpe[:], psB[:, 0:384], cs[:], ALU.subtract, r=[psB, cs], w=[tmpe])
            P.act(Eend[:], tmpe[:], AF.Exp, scale=-C, r=[tmpe], w=[Eend])
            P.tt(ka[:], kk[:], a[:], ALU.mult, r=[kk, a], w=[ka])
            P.stt(Q4[:, 0, :], kk[:], -1.0, Ex[:], ALU.mult, ALU.mult, r=[kk, Ex], w=[Q4])
            P.tt(Q4[:, 1, :], R, Ep[:], ALU.mult, r=[RK, Ep], w=[Q4])
            P.tt(Q4[:, 2, :], ka[:], Em[:], ALU.mult, r=[ka, Em], w=[Q4])
            P.tt(Q4[:, 3, :], kp[:], Em[:], ALU.mult, r=[kp, Em], w=[Q4])
            P.tt(Bd[:], ka[:], Eend[:], ALU.mult, r=[ka, Eend], w=[Bd])
            P.tt(Kd[:], kp[:], Eend[:], ALU.mult, r=[kp, Eend], w=[Kd])
            for c_ in range(2):
                P.ts(Bdz[c_][:], Bd[:], cind[:, c_:c_ + 1], None, ALU.mult, r=[Bd, cind], w=[Bdz[c_]], eng="pool")
                P.ts(Kdz[c_][:], Kd[:], cind[:, c_:c_ + 1], None, ALU.mult, r=[Kd, cind], w=[Kdz[c_]], eng="pool")
            P.tt(t4[:], R, kp[:], ALU.mult, r=[RK, kp], w=[t4])
            P.tt(t4[:], t4[:], par["rw_r_k"][:], ALU.mult, r=[t4, par["rw_r_k"]], w=[t4])
            P.red(bsum[:], v3(t4), ALU.add, r=[t4], w=[bsum])
            if stage <= 1:
                continue
            psP = nps()
            for g in range(3):
                psT = nps()
                for q in range(4):
                    P.mm(psT[:, q * 128:(q + 1) * 128], Q4[:, q, g * 128:(g + 1) * 128], ident[:], True, True, r=[Q4, ident], w=[psT])
                P.cp(fm[g][:], psT[:], r=[psT], w=[fm[g]], eng="act" if g % 2 else "dve")
                for e_ in range(2):
                    P.ts(fmz[g][e_][:], fm[g][:], cind[:, e_:e_ + 1], None, ALU.mult, r=[fm[g], cind], w=[fmz[g][e_]],
                         eng="pool" if e_ else "dve")
                P.mm(psP[:, g * 64:(g + 1) * 64], sig[:, g * 128:(g + 1) * 128], cind64[:], True, True, r=[sig, cind64], w=[psP])
            P.act(pcs[:].rearrange("p (g c) -> p g c", c=2), psP[:, 0:192].rearrange("p (g c) -> p g c", c=64)[:, :, 0:2], AF.Exp, scale=-C,
                  r=[psP], w=[pcs])
            if stage <= 2:
                continue
            psM = psM_fixed
            for h in range(6):
                g, e_ = h // 2, h % 2
                f_, fz = fm[g], fmz[g][e_]
                psG = nps()
                P.mm(psG[:, 0:256], f_[:, 256:384], fz[:, 0:256], True, True, r=[f_, fz], w=[psG])
                P.mm(psG[:, 256:512], f_[:, 384:512], fz[:, 0:256], True, True, r=[f_, fz], w=[psG])
                P.tt(Gs[:, h, :], psG[:], mask4[:], ALU.mult, r=[psG, mask4], w=[Gs])
                pm = psM[h // 3]
                P.mm(pm[:, (h % 3) * 128:(h % 3 + 1) * 128], f_[:, 0:128], fz[:, 256:384], True, True, r=[f_, fz], w=[pm])
            for half in range(2):
                P.tt(MTA[0][:, 3 * half:3 * half + 3, :], psM[half][:, 0:384].rearrange("p (h u) -> p h u", u=128),
                     masklt[:].unsqueeze(1).to_broadcast([128, 3, 128]), ALU.mult, r=[psM[half], masklt], w=[MTA[0]])
            P.cp(MA[0][:], Gs[:, :, 0:128], r=[Gs], w=[MA[0]], eng="pool")
            P.tt(Rall[:], Gs[:, :, 0:128], ident[:].unsqueeze(1).to_broadcast([128, 6, 128]), ALU.add, r=[Gs, ident], w=[Rall])
            if stage <= 3:
                continue
            cur = 0
            for lvl in range(1, 6):
                last = lvl == 5
                Mc, MTc, Mn, MTn = MA[cur], MTA[cur], MA[1 - cur], MTA[1 - cur]
                for half in range(2):
                    hs = slice(3 * half, 3 * half + 3)
                    psa, psb, psc = nps(), nps(), nps()
                    for hh in range(3):
                        h = 3 * half + hh
                        cs_ = slice(hh * 128, (hh + 1) * 128)
                        if not last:
                            P.mm(psa[:, cs_], MTc[:, h, :], Mc[:, h, :], True, True, r=[MTc, Mc], w=[psa])
                        P.mm(psb[:, cs_], Mc[:, h, :], MTc[:, h, :], True, True, r=[MTc, Mc], w=[psb])
                    if not last:
                        P.cp(Mn[:, hs, :], psa[:, 0:384].rearrange("p (h u) -> p h u", u=128), r=[psa], w=[Mn], eng="act")
                    P.cp(MTn[:, hs, :], psb[:, 0:384].rearrange("p (h u) -> p h u", u=128), r=[psb], w=[MTn])
                    for hh in range(3):
                        h = 3 * half + hh
                        P.mm(psc[:, hh * 128:(hh + 1) * 128], MTn[:, h, :], Rall[:, h, :], True, True, r=[MTn, Rall], w=[psc])
                    P.tt(Rall[:, hs, :], Rall[:, hs, :], psc[:, 0:384].rearrange("p (h u) -> p h u", u=128), ALU.add,
                         r=[Rall, psc], w=[Rall])
                cur = 1 - cur
            if stage <= 4:
                continue
            for cc in range(2):
                cb = cc * 64
                psX, psW, psO, psS = nps(), nps(), nps(), nps()
                for h in range(6):
                    g, e_ = h // 2, h % 2
                    hc = slice(h * 64, (h + 1) * 64)
                    P.mm(psX[:, hc], fm[g][:, 0:128], ST[:, g, e_, :], True, False, r=[fm[g], ST], w=[psX])
                    P.mm(psX[:, hc], Gs[:, h, 256:384], RK[:, 768 + h * 64:768 + (h + 1) * 64], False, True,
                         r=[Gs, RK], w=[psX])
                P.cp(XT[cb:cb + 64, :], psX[cb:cb + 64, 0:384], r=[psX], w=[XT], eng="act")
                for h in range(6):
                    hc = slice(h * 64, (h + 1) * 64)
                    P.mm(psW[:, hc], Rall[:, h, :], XT[:, hc], True, True, r=[Rall, XT], w=[psW])
                P.cp(WT[cb:cb + 64, :], psW[cb:cb + 64, 0:384], r=[psW], w=[WT])
                for h in range(6):
                    g, e_ = h // 2, h % 2
                    hc = slice(h * 64, (h + 1) * 64)
                    Vh = RK[:, 768 + h * 64:768 + (h + 1) * 64]
                    P.mm(psO[:, hc], fm[g][:, 128:256], ST[:, g, e_, :], True, False, r=[fm[g], ST], w=[psO])
                    P.mm(psO[:, hc], Gs[:, h, 128:256], WT[:, hc], False, False, r=[Gs, WT], w=[psO])
                    P.mm(psO[:, hc], Gs[:, h, 384:512], Vh, False, True, r=[Gs, RK], w=[psO])
                    P.mm(psS[:, hc], Bdz[cc][:, g * 128:(g + 1) * 128], WT[:, hc], True, False, r=[Bdz[cc], WT], w=[psS])
                    P.mm(psS[:, hc], Kdz[cc][:, g * 128:(g + 1) * 128], Vh, False, True, r=[Kdz[cc], RK], w=[psS])
                P.cp(O[cb:cb + 64, :], psO[cb:cb + 64, 0:384], r=[psO], w=[O], eng="act")
                for h in range(6):
                    g, e_ = h // 2, h % 2
                    jb = e_ * 64
                    P.stt(ST[jb:jb + 64, g, e_, :], ST[jb:jb + 64, g, e_, :], pcs[jb:jb + 64, 2 * g + cc:2 * g + cc + 1],
                          psS[jb:jb + 64, h * 64:(h + 1) * 64], ALU.mult, ALU.add, r=[ST, pcs, psS], w=[ST])
            if stage <= 5:
                continue
            if "dbgO" in S and ti == 0:
                P.dma("sp", S["dbgO"][:, 0:384], O[:], reads=[O])
                P.dma("sp", S["dbgO"][:, 384:768], XT[:], reads=[XT])
                P.dma("sp", S["dbgO"][:, 768:1152], WT[:], reads=[WT])
                P.dma("sp", S["dbgO"][:, 1152:1664], Gs[:, 0, :], reads=[Gs])
                P.dma("sp", S["dbgO"][:, 1664:1792], Rall[:, 0, :], reads=[Rall])
                P.dma("sp", S["dbgO"][:, 1792:2304], fm[0][:], reads=[fm[0]])
                P.dma("sp", S["dbgO"][:, 2304:2310], pcs[:], reads=[pcs])
            P.red(s1[:], v3(O), ALU.add, r=[O], w=[s1])
            P.tt(t4[:], O[:], O[:], ALU.mult, r=[O], w=[t4])
            P.red(s2[:], v3(t4), ALU.add, r=[t4], w=[s2])
            P.ts(s1[:], s1[:], 1.0 / 64, None, ALU.mult, r=[s1], w=[s1])
            P.ts(s2[:], s2[:], 1.0 / 64, None, ALU.mult, r=[s2], w=[s2])
            P.tt(m2[:], s1[:], s1[:], ALU.mult, r=[s1], w=[m2])
            P.tt(s2[:], s2[:], m2[:], ALU.subtract, r=[s2, m2], w=[s2])
            P.ts(s2[:], s2[:], 64e-5, None, ALU.add, r=[s2], w=[s2])
            P.act(s2[:], s2[:], AF.Sqrt, r=[s2], w=[s2])
            recip(s2)
            P.tt(v3(On), v3(O), b3(s1), ALU.subtract, r=[O, s1], w=[On])
            P.tt(v3(On), v3(On), b3(s2), ALU.mult, r=[On, s2], w=[On])
            P.tt(On[:], On[:], par["rw_gn_w"][:], ALU.mult, r=[On, par["rw_gn_w"]], w=[On])
            P.tt(On[:], On[:], par["rw_gn_b"][:], ALU.add, r=[On, par["rw_gn_b"]], w=[On])
            P.tt(v3(t4), V.rearrange("p (h j) -> p h j", j=64), b3(bsum), ALU.mult, r=[RK, bsum], w=[t4])
            P.tt(On[:], On[:], t4[:], ALU.add, r=[On, t4], w=[On])
            P.tt(Ob[:], On[:], G, ALU.mult, r=[On, LO], w=[Ob])
            psTo = nps()
            for g in range(3):
                P.mm(psTo[:, g * 128:(g + 1) * 128], Ob[:, g * 128:(g + 1) * 128], identb[:], True, True, r=[Ob, identb], w=[psTo])
            ot = OT[(ti // 4) % 2]
            P.cp(ot[:, :, (ti % 4) * 128:(ti % 4 + 1) * 128], psTo[:, 0:384].rearrange("p (g t) -> p g t", t=128), r=[psTo], w=[ot])
            if ti % 4 == 3 or ti == ntiles - 1:
                P.dma("sp", mixT[ti // 4][:, 0:3, :], ot[:], reads=[ot], writes=["mix_rw"])
        P.barrier()
        P.emit()
        P.renew_sems()
    P.es = P.ges


def phase_nsa(P, A, l, S, mixT, ntiles=T // 128):
    GC = 2.0 * math.sqrt(2.0 / math.pi)
    with ExitStack() as es:
        P.es = es

        def cst(name, shape, dt=F32, q="sp", src=None):
            t_ = P.sb(shape, dt, name="n_" + name)
            P.dma(q, t_[:], A[name] if src is None else src, writes=[t_])
            return t_

        identb = cst("ident", [128, 128], BF16, "pool")
        causal = cst("causal", [128, 128])
        winlo = cst("winlo", [128, 128])
        cmask = cst("cmask", [128, 8])
        rowv0 = cst("rowvalid0", [128, 1])
        ovl = P.sb([128, 2, 64], BF16)
        for ch in range(2):
            P.dma("pool", ovl[:, ch, :], A["overlap"][ch], writes=[ovl])
        KA = {}
        for nm in ("ks", "kw"):
            for g in range(2):
                t_ = P.sb([128, T], BF16, name=f"KA{nm}{g}")
                P.memset(t_[:], 0.0, w=[t_], eng="pool")
                P.dma("sp", t_[0:64, :], S[nm][g * 64:(g + 1) * 64, :], writes=[t_])
                P.dma("pool", t_[64:68, :], A["kaug"], writes=[t_])
                KA[(nm, g)] = t_
        vsw = P.sb([128, 32, 256], BF16)
        P.dma("sp", vsw[:], S["vsw"].rearrange("(b p) c -> p b c", p=128), writes=[vsw])
        kcmpA = [P.sb([128, 256], BF16, name=f"kcmpA{g}") for g in range(2)]
        vcmp = P.sb([128, 2, 2, 64], BF16)
        with ExitStack() as es2:
            P.es = es2
            w1 = {}
            w2 = {}
            pe2 = {}
            for kv in ("k", "v"):
                w1[kv] = P.sb([128, 16, 128], BF16, name="w1" + kv)
                P.dma("pool", w1[kv][:], A["cmp_w1_" + kv][l].rearrange("(a p) m -> p a m", p=128), writes=[w1[kv]])
                w2[kv] = P.sb([128, 128], BF16, name="w2" + kv)
                P.dma("pool", w2[kv][:], A["cmp_w2p_" + kv][l], writes=[w2[kv]])
                pe2[kv] = P.sb([128, 16, 64], BF16, name="pe2" + kv)
                P.dma("pool", pe2[kv][:], A["cmp_pe2_" + kv][l], writes=[pe2[kv]])
            kc2 = P.sb([128, T], BF16)
            hg = P.sb([128, 256], BF16)
            hf = P.sb([128, 256], F32)
            ht = P.sb([128, 256], F32)
            bias = P.sb([128, 64], F32)
            psa = P.ps([128, 512], F32, name="pscA")
            psb = P.ps([128, 512], F32, name="pscB")
            psc = P.ps([128, 512], F32, name="pscC")
            P.memset(kc2[:], 0.0, w=[kc2])
            P.memset(hg[:], 0.0, w=[hg])
            for kv in ("k", "v"):
                for a_ in range(16):
                    P.mm(psb[:, 0:64], w1[kv][:, a_, :], pe2[kv][:, a_, :], a_ == 0, a_ == 15, r=[w1[kv], pe2[kv]], w=[psb])
                P.cp(bias[:], psb[:, 0:64], r=[psb], w=[bias])
                for g in range(2):
                    src = S["kc" if kv == "k" else "vc"]
                    P.dma("sp", kc2[0:64, :], src[g * 64:(g + 1) * 64, :], writes=[kc2])
                    P.dma("sp", kc2[64:128, 0:T - 1], src[g * 64:(g + 1) * 64, 1:T], writes=[kc2])
                    for a_ in range(16):
                        P.mm(psa[:, 0:255], w1[kv][:, a_, :], kc2[:, 2 * a_: 2 * a_ + 16 * 254 + 1: 16], a_ == 0, a_ == 15,
                             r=[w1[kv], kc2], w=[psa])
                    P.ts(hf[:, 0:255], psa[:, 0:255], bias[:, 0:1], None, ALU.add, r=[psa, bias], w=[hf])
                    P.tt(ht[:, 0:255], hf[:, 0:255], hf[:, 0:255], ALU.mult, r=[hf], w=[ht])
                    P.ts(ht[:, 0:255], ht[:, 0:255], 0.044715, 1.0, ALU.mult, ALU.add, r=[ht], w=[ht])
                    P.tt(ht[:, 0:255], ht[:, 0:255], hf[:, 0:255], ALU.mult, r=[ht, hf], w=[ht])
                    P.act(ht[:, 0:255], ht[:, 0:255], AF.Sigmoid, scale=GC, r=[ht], w=[ht])
                    P.tt(hg[:, 0:255], ht[:, 0:255], hf[:, 0:255], ALU.mult, r=[ht, hf], w=[hg])
                    if kv == "k":
                        P.mm(psc[:, 0:256], w2[kv][:], hg[:], True, True, r=[w2[kv], hg], w=[psc])
                        P.cp(kcmpA[g][:], psc[:, 0:256], r=[psc], w=[kcmpA[g]])
                        P.dma("pool", kcmpA[g][64:68, :], A["caug"], writes=[kcmpA[g]])
                    else:
                        for ch in range(2):
                            P.mm(psc[:, ch * 64:(ch + 1) * 64], hg[:, ch * 128:(ch + 1) * 128], w2[kv][:, 0:64], True, True,
                                 r=[w2[kv], hg], w=[psc])
                        P.cp(vcmp[:, :, g, :], psc[:, 0:128].rearrange("p (c d) -> p c d", d=64), r=[psc], w=[vcmp])
            P.barrier()
            P.emit()
        P.es = es
        qA = [P.sb([128, 128], BF16, name=f"qA{i}") for i in range(4)]
        for t_ in qA:
            P.memset(t_[:], 0.0, w=[t_], eng="pool")
        gts = [P.sb([128, 18], F32, name=f"ngt{i}") for i in range(2)]
        selb = [P.sb([128, 64], F32, name=f"selb{i}") for i in range(2)]
        sc = P.sb([128, T], F32)
        pb = P.sb([128, T], BF16)
        pT = [P.sb([128, 4, 128], BF16, name=f"npT{i}") for i in range(2)]
        pn = [P.sb([128, 256], BF16, name=f"pn{i}") for i in range(3)]
        for t_ in pn:
            P.memset(t_[:], 0.0, w=[t_])
        pnT = [P.sb([128, 2, 128], BF16, name=f"pnT{i}") for i in range(3)]
        acc = P.sb([128, 384], F32)
        accb = P.sb([128, 384], BF16)
        OT = [P.sb([128, 3, 512], BF16, name=f"nOT{i}") for i in range(2)]
        sm = lambda n: P.sb([128, 1], F32, name=n)
        rmax, ssum, gs = sm("rmax"), sm("ssum"), sm("gs")
        sc64 = P.sb([128, 64], F32)
        sc64b = P.sb([128, 64], F32)
        selneg = P.sb([128, 64], F32)
        m8a = P.sb([128, 8], F32)
        m8b = P.sb([128, 8], F32)
        psS = [P.ps([128, 512], F32, name=f"psn{i}") for i in range(3)]
        psTT = [P.ps([128, 512], F32, name=f"psnT{i}") for i in range(2)]
        psO = P.ps([128, 512], F32, name="psnO")
        psI = P.ps([128, 512], F32, name="psnI")
        cnt = {"s": 0, "t": 0, "q": 0, "p": 0}

        def softmax_pv(ncols, Vfn, nblk0, gate_ap, first, rowvalid=None):
            P.red(rmax[:], sc[:, 0:ncols], ALU.max, r=[sc], w=[rmax])
            P.ts(rmax[:], rmax[:], -1.0, None, ALU.mult, r=[rmax], w=[rmax])
            P.act(pb[:, 0:ncols], sc[:, 0:ncols], AF.Exp, bias=rmax[:], r=[sc, rmax], w=[pb])
            P.red(ssum[:], pb[:, 0:ncols], ALU.add, r=[pb], w=[ssum])
            P.op("dve", lambda e: e.reciprocal(out=ssum[:], in_=ssum[:]), [Prog._k(ssum)], [Prog._k(ssum)])
            P.tt(gs[:], ssum[:], gate_ap, ALU.mult, r=[ssum, "gates"], w=[gs])
            nb = ncols // 128
            for c0 in range(0, nb, 4):
                n4 = min(4, nb - c0)
                pst = psTT[cnt["t"] % 2]
                ptt = pT[cnt["t"] % 2]
                cnt["t"] += 1
                for k in range(n4):
                    P.mm(pst[:, k * 128:(k + 1) * 128], pb[:, (c0 + k) * 128:(c0 + k + 1) * 128], identb[:], True, True,
                         r=[pb, identb], w=[pst])
                P.cp(ptt[:, 0:n4, :], pst[:, 0:n4 * 128].rearrange("p (k q) -> p k q", q=128), r=[pst], w=[ptt],
                     eng="act" if cnt["t"] % 2 else "dve")
                for k in range(n4):
                    P.mm(psO[:, 0:64], ptt[:, k, :], Vfn(nblk0 + c0 + k), c0 + k == 0, c0 + k == nb - 1, r=[ptt, vsw], w=[psO])
            return gs

        for qi in range(ntiles):
            t0 = qi * 128
            G = gts[qi % 2]
            P.dma("sp", G[:], S["gates"][t0:t0 + 128, :], writes=[G, "gates"])
            for g in range(2):
                sbias = selb[g]
                if g == 0:
                    P.dma("sp", selb[0][:], A["selbias"][qi], writes=[selb[0]])
                qts = []
                n0 = 8 * qi
                ncol = min(255, n0 + 7)
                nch = (ncol + 127) // 128
                for r_ in range(3):
                    h = 3 * g + r_
                    qt = qA[cnt["q"] % 4]
                    cnt["q"] += 1
                    qsrc = S[f"q{h // 2}"]
                    P.dma("sp", qt[0:64, :], qsrc[(h % 2) * 64:(h % 2 + 1) * 64, t0:t0 + 128], writes=[qt])
                    P.dma("pool", qt[64:68, :], A["qaug"][h, :, t0:t0 + 128], writes=[qt])
                    qts.append(qt)
                    ps = psS[cnt["s"] % 3]
                    cnt["s"] += 1
                    P.mm(ps[:, 0:ncol], qt[:], kcmpA[g][:, 0:ncol], True, True, r=[qt, kcmpA[g]], w=[ps])
                    P.cp(sc[:, 0:ncol], ps[:, 0:ncol], r=[ps], w=[sc], eng="act")
                    lo = max(0, n0 - 1)
                    mlo = lo - (n0 - 1)
                    P.tt(sc[:, lo:ncol], sc[:, lo:ncol], cmask[:, mlo:mlo + (ncol - lo)], ALU.add, r=[sc, cmask], w=[sc])
                    P.red(rmax[:], sc[:, 0:ncol], ALU.max, r=[sc], w=[rmax])
                    P.ts(rmax[:], rmax[:], -1.0, None, ALU.mult, r=[rmax], w=[rmax])
                    P.act(pb[:, 0:ncol], sc[:, 0:ncol], AF.Exp, bias=rmax[:], r=[sc, rmax], w=[pb])
                    P.red(ssum[:], pb[:, 0:ncol], ALU.add, r=[pb], w=[ssum])
                    P.op("dve", lambda e: e.reciprocal(out=ssum[:], in_=ssum[:]), [Prog._k(ssum)], [Prog._k(ssum)])
                    if qi == 0:
                        P.tt(ssum[:], ssum[:], rowv0[:], ALU.mult, r=[ssum, rowv0], w=[ssum])
                    pnr = pn[r_]
                    P.ts(pnr[:, 0:ncol], pb[:, 0:ncol], ssum[:, 0:1], None, ALU.mult, r=[pb, ssum], w=[pnr])
                    pst = psTT[cnt["t"] % 2]
                    cnt["t"] += 1
                    for ch in range(nch):
                        P.mm(pst[:, ch * 128:(ch + 1) * 128], pnr[:, ch * 128:(ch + 1) * 128], identb[:], True, True,
                             r=[pnr, identb], w=[pst])
                    pnt = pnT[r_]
                    P.cp(pnt[:, 0:nch, :], pst[:, 0:nch * 128].rearrange("p (k q) -> p k q", q=128), r=[pst], w=[pnt])
                    for ch in range(nch):
                        P.mm(psO[:, 0:64], pnt[:, ch, :], vcmp[:, ch, g, :], ch == 0, ch == nch - 1, r=[pnt, vcmp], w=[psO])
                        P.mm(psI[:, 0:64], pnt[:, ch, :], ovl[:, ch, :], r_ == 0 and ch == 0, r_ == 2 and ch == nch - 1,
                             r=[pnt, ovl], w=[psI])
                    P.ts(acc[:, h * 64:(h + 1) * 64], psO[:, 0:64], G[:, 3 * h:3 * h + 1], None, ALU.mult, r=[psO, G], w=[acc])
                P.tt(sc64[:], psI[:, 0:64], selb[0][:], ALU.add, r=[psI, selb[0]], w=[sc64])
                P.op("dve", lambda e: e.max(out=m8a[:], in_=sc64[:]), [Prog._k(sc64)], [Prog._k(m8a)])
                P.op("dve", lambda e: e.match_replace(out=sc64b[:], in_to_replace=m8a[:], in_values=sc64[:], imm_value=-3.0e38),
                     [Prog._k(sc64), Prog._k(m8a)], [Prog._k(sc64b)])
                P.op("dve", lambda e: e.max(out=m8b[:], in_=sc64b[:]), [Prog._k(sc64b)], [Prog._k(m8b)])
                P.ts(selneg[:], sc64[:], m8b[:, 7:8], NEG, ALU.is_lt, ALU.mult, r=[sc64, m8b], w=[selneg])
                for r_ in range(3):
                    h = 3 * g + r_
                    qt = qts[r_]
                    nk = (qi + 1) * 128
                    for c0 in range(0, nk, 512):
                        wdt = min(512, nk - c0)
                        ps = psS[cnt["s"] % 3]
                        cnt["s"] += 1
                        P.mm(ps[:, 0:wdt], qt[:], KA[("ks", g)][:, c0:c0 + wdt], True, True, r=[qt, KA[("ks", g)]], w=[ps])
                        nj = wdt // 64
                        P.tt(sc[:, c0:c0 + wdt].rearrange("p (j k) -> p j k", k=64), ps[:, 0:wdt].rearrange("p (j k) -> p j k", k=64),
                             selneg[:, c0 // 64:c0 // 64 + nj].unsqueeze(2).to_broadcast([128, nj, 64]), ALU.add,
                             r=[ps, selneg], w=[sc])
                    P.tt(sc[:, nk - 128:nk], sc[:, nk - 128:nk], causal[:], ALU.add, r=[sc, causal], w=[sc])
                    gs_ = softmax_pv(nk, lambda kb, g=g: vsw[:, kb, g * 64:(g + 1) * 64], 0, G[:, 3 * h + 1:3 * h + 2], False)
                    P.stt(acc[:, h * 64:(h + 1) * 64], psO[:, 0:64], gs_[:, 0:1], acc[:, h * 64:(h + 1) * 64], ALU.mult, ALU.add,
                          r=[psO, gs_, acc], w=[acc])
                    kb0 = max(0, qi - 4)
                    nk = (qi - kb0 + 1) * 128
                    for c0 in range(0, nk, 512):
                        wdt = min(512, nk - c0)
                        ps = psS[cnt["s"] % 3]
                        cnt["s"] += 1
                        P.mm(ps[:, 0:wdt], qt[:], KA[("kw", g)][:, kb0 * 128 + c0:kb0 * 128 + c0 + wdt], True, True,
                             r=[qt, KA[("kw", g)]], w=[ps])
                        P.cp(sc[:, c0:c0 + wdt], ps[:, 0:wdt], r=[ps], w=[sc], eng="act")
                    P.tt(sc[:, nk - 128:nk], sc[:, nk - 128:nk], causal[:], ALU.add, r=[sc, causal], w=[sc])
                    if qi >= 4:
                        P.tt(sc[:, 0:128], sc[:, 0:128], winlo[:], ALU.add, r=[sc, winlo], w=[sc])
                    gs_ = softmax_pv(nk, lambda kb, g=g: vsw[:, kb, 128 + g * 64:128 + (g + 1) * 64], kb0, G[:, 3 * h + 2:3 * h + 3], False)
                    P.stt(acc[:, h * 64:(h + 1) * 64], psO[:, 0:64], gs_[:, 0:1], acc[:, h * 64:(h + 1) * 64], ALU.mult, ALU.add,
                          r=[psO, gs_, acc], w=[acc])
            P.cp(accb[:], acc[:], r=[acc], w=[accb])
            pst = psTT[cnt["t"] % 2]
            cnt["t"] += 1
            for k in range(3):
                P.mm(pst[:, k * 128:(k + 1) * 128], accb[:, k * 128:(k + 1) * 128], identb[:], True, True, r=[accb, identb], w=[pst])
            ot = OT[(qi // 4) % 2]
            P.cp(ot[:, :, (qi % 4) * 128:(qi % 4 + 1) * 128], pst[:, 0:384].rearrange("p (g t) -> p g t", t=128), r=[pst], w=[ot])
            if qi % 4 == 3 or qi == ntiles - 1:
                P.dma("sp", mixT[qi // 4][:, 3:6, :], ot[:], reads=[ot], writes=["mix_nsa"])
        P.barrier()
        P.emit()
        P.renew_sems()
    P.es = P.ges
```

```python
import math
import numpy as np
from contextlib import ExitStack
import concourse.bass as bass
import concourse.mybir as mybir
from concourse.bass_utils import run_bass_kernel_spmd

F32 = mybir.dt.float32
BF16 = mybir.dt.bfloat16
AF = mybir.ActivationFunctionType
ALU = mybir.AluOpType
AX = mybir.AxisListType

T = 4096
D = 1024
DEPTH = 2
NIN = 2834
DFF = 2816
C_DEC = math.exp(-0.5)
NEG = -1.0e30
NT_DBG = 4
SLOPES = [0.25, 0.0625, 0.015625, 0.00390625, 0.5, 0.125]


class Prog:
    COMPUTE = ("pe", "act", "dve", "pool")
    QUEUES = ("sp", "act", "pool")
    NSLOT = 8

    def __init__(self, nc, es):
        self.nc = nc
        self.ges = es
        self.es = es
        self.ops = {e: [] for e in ("pe", "act", "dve", "pool", "sp")}
        self.sem = {}
        self.cnt = {}
        for e in self.COMPUTE:
            self.sem[e] = es.enter_context(nc.semaphore("s_" + e))
            self.cnt[e] = 0
        self.slots = {}
        for q in self.QUEUES:
            self.slots[q] = [[es.enter_context(nc.semaphore(f"d_{q}{i}")), 0] for i in range(self.NSLOT)]
        self.slot_rr = {q: 0 for q in self.QUEUES}
        self.waited = {e: {} for e in self.ops}
        self.lastw = {}
        self.readers = {}
        self.nsb = 0
        self.ndram = 0

    def sb(self, shape, dt=F32, name=None):
        self.nsb += 1
        return self.es.enter_context(self.nc.sbuf_tensor(f"{name or 'sb'}_{self.nsb}", list(shape), dt))

    def ps(self, shape, dt=F32, name=None):
        self.nsb += 1
        return self.es.enter_context(self.nc.psum_tensor(f"{name or 'ps'}_{self.nsb}", list(shape), dt))

    def _need(self, eng, ev, waits):
        if ev is None:
            return
        sem_key, sem, val, src = ev
        if src == eng and sem_key == src and eng == "pe":
            return
        w = self.waited[eng]
        if w.get(sem_key, 0) >= val:
            return
        w[sem_key] = val
        waits.append((sem, val))

    @staticmethod
    def _k(r):
        if isinstance(r, (str, int)):
            return r
        if isinstance(r, tuple):
            return tuple(Prog._k(x) for x in r)
        return "T:" + str(getattr(r, "name", id(r)))

    def _deps(self, eng, reads, writes):
        waits = []
        for r in reads:
            self._need(eng, self.lastw.get(r), waits)
        for w in writes:
            self._need(eng, self.lastw.get(w), waits)
            for ev in self.readers.get(w, []):
                self._need(eng, ev, waits)
        return waits

    def _commit(self, ev, reads, writes):
        for r in reads:
            lst = self.readers.setdefault(r, [])
            lst.append(ev)
            if len(lst) > 32:
                del lst[0]
        for w in writes:
            self.lastw[w] = ev
            self.readers[w] = []

    def op(self, eng, fn, reads=(), writes=()):
        reads = [self._k(r) for r in reads]
        writes = [self._k(r) for r in writes]
        waits = self._deps(eng, reads, writes)
        self.cnt[eng] += 1
        ev = (eng, self.sem[eng], self.cnt[eng], eng)
        self.ops[eng].append((waits, fn, (self.sem[eng], 1)))
        self._commit(ev, reads, writes)

    def dma(self, q, out, in_, reads=(), writes=(), **kw):
        reads = [self._k(r) for r in reads]
        writes = [self._k(r) for r in writes]
        slots = self.slots[q]
        i = self.slot_rr[q]
        self.slot_rr[q] = (i + 1) % self.NSLOT
        sem, val = slots[i]
        waits = self._deps(q, reads, writes)
        key = f"d_{q}{i}"
        if val > 0 and self.waited[q].get(key, 0) < val:
            self.waited[q][key] = val
            waits.append((sem, val))
        slots[i][1] = val + 16
        ev = (key, sem, val + 16, q)

        def fn(e, out=out, in_=in_, kw=kw):
            return e.dma_start(out=out, in_=in_, **kw)

        self.ops[q].append((waits, fn, (sem, 16)))
        self._commit(ev, reads, writes)

    def barrier(self):
        for eng in self.ops:
            waits = []
            for e in self.COMPUTE:
                if e != eng and self.cnt[e] > self.waited[eng].get(e, 0):
                    self.waited[eng][e] = self.cnt[e]
                    waits.append((self.sem[e], self.cnt[e]))
            for q in self.QUEUES:
                for i, (sem, val) in enumerate(self.slots[q]):
                    key = f"d_{q}{i}"
                    if val > self.waited[eng].get(key, 0):
                        self.waited[eng][key] = val
                        waits.append((sem, val))
            if waits:
                self.ops[eng].append((waits, None, None))
        self.lastw = {}
        self.readers = {}
        self._fresh = True

    def renew_sems(self):
        self.gen = getattr(self, "gen", 0) + 1
        for e in self.COMPUTE:
            self.sem[e] = self.ges.enter_context(self.nc.semaphore(f"s_{e}_{self.gen}"))
            self.cnt[e] = 0
            for eng in self.waited:
                self.waited[eng].pop(e, None)

    def emit(self):
        nc = self.nc
        ops = self.ops
        with nc.Block() as block:
            def play(name):
                def run(e):
                    for waits, fn, inc in ops[name]:
                        for sem, val in waits:
                            e.wait_ge(sem, val)
                        if fn is not None:
                            fn(e).then_inc(inc[0], inc[1])
                return run
            block.tensor(play("pe"))
            block.scalar(play("act"))
            block.vector(play("dve"))
            block.gpsimd(play("pool"))
            block.sync(play("sp"))
        self.ops = {e: [] for e in ops}

    def mm(self, out, lhsT, rhs, start, stop, r=(), w=()):
        self.op("pe", lambda e: e.matmul(out, lhsT=lhsT, rhs=rhs, start=start, stop=stop), r, w)

    def tr(self, out, in_, ident, r=(), w=()):
        self.op("pe", lambda e: e.transpose(out, in_, ident), r, w)

    def tt(self, out, in0, in1, op, r=(), w=(), eng="dve"):
        self.op(eng, lambda e: e.tensor_tensor(out=out, in0=in0, in1=in1, op=op), r, w)

    def ts(self, out, in0, s1, s2, op0, op1=None, r=(), w=(), eng="dve"):
        if op1 is None:
            self.op(eng, lambda e: e.tensor_scalar(out=out, in0=in0, scalar1=s1, scalar2=None, op0=op0), r, w)
        else:
            self.op(eng, lambda e: e.tensor_scalar(out=out, in0=in0, scalar1=s1, scalar2=s2, op0=op0, op1=op1), r, w)

    def stt(self, out, in0, scalar, in1, op0, op1, r=(), w=()):
        self.op("dve", lambda e: e.scalar_tensor_tensor(out=out, in0=in0, scalar=scalar, in1=in1, op0=op0, op1=op1), r, w)

    def act(self, out, in_, func, bias=None, scale=None, accum=None, r=(), w=()):
        kw = {}
        if bias is not None:
            kw["bias"] = bias
        if scale is not None:
            kw["scale"] = scale
        if accum is not None:
            kw["accum_out"] = accum
        self.op("act", lambda e: e.activation(out=out, in_=in_, func=func, **kw), r, w)

    def cp(self, out, in_, r=(), w=(), eng="dve"):
        if eng == "act":
            self.op("act", lambda e: e.activation(out=out, in_=in_, func=AF.Copy), r, w)
        else:
            self.op(eng, lambda e: e.tensor_copy(out=out, in_=in_), r, w)

    def red(self, out, in_, op, r=(), w=(), axis=AX.X):
        self.op("dve", lambda e: e.tensor_reduce(out=out, in_=in_, axis=axis, op=op), r, w)

    def memset(self, ap, val, w=(), eng="dve"):
        self.op(eng, lambda e: e.memset(ap, val), (), w)


def bc(ap, shape):
    return ap.to_broadcast(list(shape))


def host_consts():
    c = {}
    t = np.arange(128)
    same = (t[:, None] // 64) == (t[None, :] // 64)
    c["tri_incl"] = (same & (t[:, None] <= t[None, :])).astype(np.float32)
    c["same"] = same.astype(np.float32)
    ci = np.zeros((128, 2), np.float32)
    ci[:64, 0] = 1
    ci[64:, 1] = 1
    c["chunkind"] = ci
    ci64 = np.zeros((128, 64), np.float32)
    ci64[:, 0:2] = ci
    c["chunkind64"] = ci64
    su = (same & (t[:, None] < t[None, :])).astype(np.float32)
    iu = (same & (t[:, None] <= t[None, :])).astype(np.float32)
    c["mask4"] = np.concatenate([su, iu, su, iu], axis=1)
    c["mask_lt"] = su.T.copy()
    c["ident"] = np.eye(128, dtype=np.float32)
    c["ones"] = np.ones((128, 128), np.float32)
    pos = np.arange(T)
    kaug = np.stack([np.ones(T), np.ones(T), pos // 64, pos % 64]).astype(np.float32)
    c["kaug"] = kaug
    qaug = np.zeros((6, 4, T), np.float32)
    for h, s in enumerate(SLOPES):
        qaug[h, 0] = -s * 64 * (pos // 64)
        qaug[h, 1] = -s * (pos % 64)
        qaug[h, 2] = s * 64
        qaug[h, 3] = s
    c["qaug"] = qaug
    ce = np.arange(256) * 16 + 31
    c["caug"] = np.stack([np.ones(256), np.ones(256), ce // 64, ce % 64]).astype(np.float32)
    p = np.arange(128)
    c["causal"] = np.where(p[None, :] <= p[:, None], 0.0, NEG).astype(np.float32)
    c["winlo"] = np.where(p[None, :] > p[:, None], 0.0, NEG).astype(np.float32)
    m = np.arange(-1, 7)
    c["cmask"] = np.where(16 * m[None, :] + 31 <= p[:, None], 0.0, NEG).astype(np.float32)
    rv = np.ones((128, 1), np.float32)
    rv[:31] = 0
    c["rowvalid0"] = rv
    n = np.arange(256)
    j = np.arange(64)
    ov = ((16 * n[:, None] < 64 * j[None, :] + 64) & (16 * n[:, None] + 31 >= 64 * j[None, :])).astype(np.float32)
    ov[255] = 0
    c["overlap"] = ov.reshape(2, 128, 64)
    sb = np.zeros((32, 128, 64), np.float32)
    for qi in range(32):
        tt = qi * 128 + p
        cur = tt // 64
        forced = (j[None, :] == 0) | (j[None, :] == cur[:, None]) | (j[None, :] == cur[:, None] - 1)
        valid = (64 * j[None, :]) <= tt[:, None]
        sb[qi] = np.where(valid, 1000.0 * forced, NEG)
    c["selbias"] = sb
    return c


CONST_SHAPES = None


def _dram_in(nc, name, arr_shape, dt=F32):
    return nc.dram_tensor(name, list(arr_shape), dt, kind="ExternalInput").ap()


def load_bcast(P, q, dst, src_row, n):
    P.dma(q, dst, src_row.unsqueeze(0).to_broadcast([128, n]), writes=[dst.tensor])


def rms_stats(P, sq_bf, nchunk, ones_bf, ps, rstd, ntok, eps=1e-6, dim=D):
    for c in range(nchunk):
        P.mm(ps[:, :ntok], ones_bf[:], sq_bf[:, c, :], c == 0, c == nchunk - 1, r=[sq_bf, ones_bf], w=[ps])
    P.ts(rstd[:, :ntok], ps[:, :ntok], 1.0 / dim, eps, ALU.mult, ALU.add, r=[ps], w=[rstd])
    P.act(rstd[:, :ntok], rstd[:, :ntok], AF.Sqrt, r=[rstd], w=[rstd])
    P.op("dve", lambda e: e.reciprocal(out=rstd[:, :ntok], in_=rstd[:, :ntok]), [Prog._k(rstd)], [Prog._k(rstd)])


def phase_ffn(P, A, l, hin, hout):
    nc = P.nc
    TB = 256
    with ExitStack() as es:
        P.es = es
        wup = P.sb([128, 8, 2 * DFF], BF16)
        wdn = P.sb([128, 22, D], BF16)
        ones_bf = P.sb([128, 128], BF16)
        cw = P.sb([128, 3, 44], F32)
        cb = P.sb([128, 44], F32)
        gpre = P.sb([128, 8], F32)
        gpost = P.sb([128, 8], F32)
        P.dma("pool", ones_bf[:], A["ones"], writes=[ones_bf])
        for c in range(8):
            P.dma("pool", wup[:, c, :], A["w_up"][l, c * 128:(c + 1) * 128, :], writes=[wup])
        for c in range(22):
            P.dma("pool", wdn[:, c, :], A["w_down"][l, c * 128:(c + 1) * 128, :], writes=[wdn])
        P.dma("sp", cw[:], A["conv_wT"][l], writes=[cw])
        P.dma("sp", cb[:], A["conv_bT"][l], writes=[cb])
        P.dma("sp", gpre[:], A["pre_ffn_normT"][l], writes=[gpre])
        P.dma("sp", gpost[:], A["post_ffn_normT"][l], writes=[gpost])

        hb512 = P.sb([128, 8, 512], F32)
        sq = P.sb([128, 8, TB], BF16)
        xn = P.sb([128, 8, TB], BF16)
        rstd = P.sb([128, TB], F32)
        hu = [P.sb([128, 2 + TB], F32, name=f"hu{i}") for i in range(4)]
        carry = P.sb([128, 44, 2], F32)
        cv = [P.sb([128, TB], F32, name=f"cv{i}") for i in range(4)]
        tmp = [P.sb([128, TB], F32, name=f"tmpf{i}") for i in range(2)]
        actT = P.sb([128, 22, TB], BF16)
        fo = P.sb([128, 8, TB], F32)
        pss = [P.ps([128, 512], F32, name=f"psf{i}") for i in range(6)]
        psst = P.ps([128, 512], F32, name="psfst")
        P.memset(carry[:], 0.0, w=[carry])
        GC = 2.0 * math.sqrt(2.0 / math.pi)
        pi = 0
        for b in range(T // TB):
            if b % 2 == 0:
                P.dma("sp", hb512[:], hin[b // 2], writes=[hb512])
            hb = hb512[:, :, (b % 2) * TB:(b % 2 + 1) * TB]
            P.act(sq[:], hb, AF.Square, r=[hb512], w=[sq])
            rms_stats(P, sq, 8, ones_bf, psst, rstd, TB)
            for c in range(8):
                P.stt(xn[:, c, :], hb[:, c, :], gpre[:, c:c + 1], rstd[:], ALU.mult, ALU.mult, r=[hb512, rstd, gpre], w=[xn])
            for i in range(22):
                res = []
                for half in range(2):
                    m = i + 22 * half
                    ps = pss[pi % 6]
                    pi += 1
                    for c in range(8):
                        P.mm(ps[:, :TB], wup[:, c, m * 128:(m + 1) * 128], xn[:, c, :], c == 0, c == 7, r=[wup, xn], w=[ps])
                    h = hu[(2 * i + half) % 4]
                    o = cv[(2 * i + half) % 4]
                    P.cp(h[:, 0:2], carry[:, m, :], r=[carry], w=[h], eng="pool")
                    P.act(h[:, 2:2 + TB], ps[:, :TB], AF.Copy, r=[ps], w=[h])
                    P.cp(carry[:, m, :], h[:, TB:TB + 2], r=[h], w=[carry], eng="pool")
                    P.ts(o[:], h[:, 0:TB], cw[:, 0, m:m + 1], cb[:, m:m + 1], ALU.mult, ALU.add, r=[h, cw, cb], w=[o])
                    P.stt(o[:], h[:, 1:1 + TB], cw[:, 1, m:m + 1], o[:], ALU.mult, ALU.add, r=[h, o], w=[o])
                    P.stt(o[:], h[:, 2:2 + TB], cw[:, 2, m:m + 1], o[:], ALU.mult, ALU.add, r=[h, o], w=[o])
                    res.append(o)
                gte, up = res
                t0_, t1_ = tmp
                P.tt(t0_[:], gte[:], gte[:], ALU.mult, r=[gte], w=[t0_], eng="pool")
                P.ts(t0_[:], t0_[:], 0.044715, 1.0, ALU.mult, ALU.add, r=[t0_], w=[t0_], eng="pool")
                P.tt(t0_[:], t0_[:], gte[:], ALU.mult, r=[t0_, gte], w=[t0_], eng="pool")
                P.act(t0_[:], t0_[:], AF.Sigmoid, scale=GC, r=[t0_], w=[t0_])
                P.tt(t1_[:], gte[:], up[:], ALU.mult, r=[gte, up], w=[t1_], eng="pool")
                P.tt(actT[:, i, :], t0_[:], t1_[:], ALU.mult, r=[t0_, t1_], w=[actT])
            for n in range(8):
                ps = pss[pi % 6]
                pi += 1
                for c in range(22):
                    P.mm(ps[:, :TB], wdn[:, c, n * 128:(n + 1) * 128], actT[:, c, :], c == 0, c == 21, r=[wdn, actT], w=[ps])
                P.act(fo[:, n, :], ps[:, :TB], AF.Copy, r=[ps], w=[fo])
            P.act(sq[:], fo[:], AF.Square, r=[fo], w=[sq])
            rms_stats(P, sq, 8, ones_bf, psst, rstd, TB)
            for c in range(8):
                P.stt(fo[:, c, :], fo[:, c, :], gpost[:, c:c + 1], rstd[:], ALU.mult, ALU.mult, r=[fo, rstd, gpost], w=[fo])
            P.tt(hb, hb, fo[:], ALU.add, r=[hb512, fo], w=[hb512])
            if b % 2 == 1:
                P.dma("sp", hout[b // 2], hb512[:], reads=[hb512], writes=["hout"])
        P.barrier()
        P.emit()
        P.renew_sems()
    P.es = P.ges


def phase_ple(P, A, l, hin, hout):
    TB = 512
    with ExitStack() as es:
        P.es = es
        wg = P.sb([128, 8, D], BF16)
        wpl = P.sb([128, 2, D], BF16)
        ones_bf = P.sb([128, 128], BF16)
        gple = P.sb([128, 8], F32)
        P.dma("pool", ones_bf[:], A["ones"], writes=[ones_bf])
        for c in range(8):
            P.dma("pool", wg[:, c, :], A["w_ple_gate"][l, c * 128:(c + 1) * 128, :], writes=[wg])
        for c in range(2):
            P.dma("pool", wpl[:, c, :], A["w_ple"][l, c * 128:(c + 1) * 128, :], writes=[wpl])
        P.dma("sp", gple[:], A["ple_normT"][l], writes=[gple])
        hb = [P.sb([128, 8, TB], F32, name=f"phb{i}") for i in range(2)]
        sq = P.sb([128, 8, TB], BF16)
        rstd = P.sb([128, TB], F32)
        fo = P.sb([128, 8, TB], F32)
        pb = P.sb([128, 2, TB], BF16)
        eo = P.sb([128, 8, TB], F32)
        h2b = P.sb([128, 8, TB], BF16)
        pss = [P.ps([128, 512], F32, name=f"psp{i}") for i in range(6)]
        psst = P.ps([128, 512], F32, name="pspst")
        pi = 0
        for b in range(T // TB):
            h = hb[b % 2]
            P.dma("sp", h[:], hin[b], writes=[h])
            P.dma("pool", pb[:], A["pT"][l, b], writes=[pb])
            P.cp(h2b[:], h[:], r=[h], w=[h2b], eng="act")
            for n in range(8):
                ps = pss[pi % 6]
                pi += 1
                for c in range(2):
                    P.mm(ps[:, :TB], wpl[:, c, n * 128:(n + 1) * 128], pb[:, c, :], c == 0, c == 1, r=[wpl, pb], w=[ps])
                P.act(eo[:, n, :], ps[:, :TB], AF.Copy, r=[ps], w=[eo])
            P.act(sq[:], eo[:], AF.Square, r=[eo], w=[sq])
            rms_stats(P, sq, 8, ones_bf, psst, rstd, TB)
            for c in range(8):
                P.stt(eo[:, c, :], eo[:, c, :], gple[:, c:c + 1], rstd[:], ALU.mult, ALU.mult, r=[eo, rstd, gple], w=[eo])
            for n in range(8):
                ps = pss[pi % 6]
                pi += 1
                for c in range(8):
                    P.mm(ps[:, :TB], wg[:, c, n * 128:(n + 1) * 128], h2b[:, c, :], c == 0, c == 7, r=[wg, h2b], w=[ps])
                P.act(fo[:, n, :], ps[:, :TB], AF.Sigmoid, r=[ps], w=[fo])
            P.tt(eo[:], eo[:], fo[:], ALU.mult, r=[eo, fo], w=[eo])
            P.tt(h[:], h[:], eo[:], ALU.add, r=[h, eo], w=[h])
            P.dma("sp", hout[b], h[:], reads=[h], writes=["hout"])
        P.barrier()
        P.emit()
        P.renew_sems()
    P.es = P.ges


def normT(g):
    L = g.shape[0]
    return np.ascontiguousarray(g.reshape(L, -1, 128).transpose(0, 2, 1))


def prep_shared(inp):
    f = lambda a: np.ascontiguousarray(np.asarray(a, dtype=np.float32))
    S = {}
    S.update(host_consts())
    for k in ("w_in", "w_out", "w_up", "w_down", "w_ple", "w_ple_gate"):
        S[k] = f(inp[k])
    L = DEPTH
    S["conv_wT"] = f(inp["conv_w"].reshape(L, 3, 44, 128).transpose(0, 3, 1, 2))
    S["conv_bT"] = f(inp["conv_b"].reshape(L, 44, 128).transpose(0, 2, 1))
    for k in ("pre_mix_norm", "post_mix_norm", "pre_ffn_norm", "post_ffn_norm", "ple_norm"):
        S[k + "T"] = f(normT(np.asarray(inp[k])))
    for k in ("shift_mu", "rw_w2", "rw_a2", "rw_g2", "rw_w0", "rw_a0", "rw_k_k", "rw_k_a", "rw_r_k", "rw_gn_w", "rw_gn_b"):
        S[k] = f(inp[k])
    z64 = np.zeros((DEPTH, 64, 384), np.float32)
    S["rw_w2p"] = f(np.concatenate([np.asarray(inp["rw_w2"]), z64], axis=1))
    S["rw_a2p"] = f(np.concatenate([z64, np.asarray(inp["rw_a2"])], axis=1))
    prep_s5(inp, S)
    for kv in ("k", "v"):
        S["cmp_w1_" + kv] = f(inp["cmp_w1_" + kv])
        w2 = np.asarray(inp["cmp_w2_" + kv])
        S["cmp_w2p_" + kv] = f(np.concatenate([w2, np.zeros_like(w2)], axis=2))
        pe = np.asarray(inp["cmp_pe_" + kv]).reshape(DEPTH, 16, 128)
        S["cmp_pe2_" + kv] = f(np.repeat(pe.transpose(0, 2, 1)[:, :, :, None], 64, axis=3))
    return S


def build(shared, mode="full"):
    nc = bass.Bass("TRN2", target_bir_lowering=False)
    A = {}
    for k, v in shared.items():
        A[k] = _dram_in(nc, k, v.shape)
    A["xT"] = _dram_in(nc, "xT", [8, 128, 8, 512])
    A["pT"] = _dram_in(nc, "pT", [DEPTH, 8, 128, 2, 512])
    yT = nc.dram_tensor("yT", [8, 128, 8, 512], F32, kind="ExternalOutput").ap()
    dbg = mode != "full" and not mode.startswith("layer")
    kind = "ExternalOutput" if dbg else "Internal"
    scr = {}

    def scratch(name, shape, dt=F32):
        scr[name] = nc.dram_tensor(name, list(shape), dt, kind=kind).ap()
        return scr[name]

    hA = scratch("hA", [8, 128, 8, 512])
    hB = scratch("hB", [8, 128, 8, 512])
    S = {}
    S["rkv"] = scratch("rkv", [T, 1152])
    S["lor"] = scratch("lor", [T, 1152])
    for nm in ("q0", "q1", "q2", "kc", "vc", "ks", "kw"):
        S[nm] = scratch(nm, [128, T], BF16)
    S["vsw"] = scratch("vsw", [T, 256], BF16)
    S["gates"] = scratch("gates", [T, 18])
    S["uT"] = scratch("uT", [256, T])
    mixT = scratch("mixT", [8, 128, 8, 512], BF16)
    if dbg:
        S["dbg"] = scratch("dbg", [128, 128])
        S["dbgO"] = scratch("dbgO", [128, 2400])
    with ExitStack() as es:
        P = Prog(nc, es)
        if mode.startswith("layer"):
            l = int(mode[5:])
            phase_proj(P, A, l, A["xT"], S)
            phase_rwkv(P, A, l, S, mixT)
            phase_nsa(P, A, l, S, mixT)
            phase_s5(P, A, l, S, mixT)
            phase_out(P, A, l, A["xT"], mixT, hA)
            phase_ffn(P, A, l, hA, hB)
            phase_ple(P, A, l, hB, yT)
        if mode == "full" or mode.startswith("ph:"):
            import os
            sel = mode[3:].split(",") if mode.startswith("ph:") else os.environ.get("FULLSEL", "proj,rwkv,nsa,s5,out,ffn,ple").split(",")
            nl = int(os.environ.get("NLAYERS", DEPTH))
            hcur = A["xT"]
            for l in range(nl):
                if "proj" in sel:
                    phase_proj(P, A, l, hcur, S)
                if "rwkv" in sel:
                    phase_rwkv(P, A, l, S, mixT, ntiles=int(os.environ.get("RWT", "32")), stage=int(os.environ.get("STAGE", "9")))
                if "nsa" in sel:
                    phase_nsa(P, A, l, S, mixT, ntiles=int(os.environ.get("NST", "32")))
                if "s5" in sel:
                    phase_s5(P, A, l, S, mixT)
                if "out" in sel:
                    phase_out(P, A, l, hcur, mixT, hA)
                if "ffn" in sel:
                    phase_ffn(P, A, l, hA, hB)
                if "ple" in sel:
                    phase_ple(P, A, l, hB, yT if l == nl - 1 else hA)
                hcur = hA
        if mode == "proj":
            phase_proj(P, A, 0, A["xT"], S)
            phase_s5(P, A, 0, S, mixT)
        if mode == "rwkv":
            phase_proj(P, A, 0, A["xT"], S)
            phase_rwkv(P, A, 0, S, mixT, ntiles=NT_DBG)
        if mode == "rwkv_only":
            import os
            phase_rwkv(P, A, 0, S, mixT, ntiles=1, stage=int(os.environ.get("STAGE", "9")))
        if mode == "s5":
            phase_s5(P, A, 0, S, mixT)
        if mode == "ffn":
            phase_ffn(P, A, 0, A["xT"], hA)
            phase_ple(P, A, 0, hA, yT)
        P.barrier()
        P.emit()
    return nc, list(scr.keys())


def kernel(**inputs):
    return run_layers(inputs)


def run_layers(inputs, cores=8):
    import os
    shared = prep_shared(inputs)
    x = np.asarray(inputs["x"], dtype=np.float32)
    p = np.asarray(inputs["p"], dtype=np.float32)
    hs = [np.ascontiguousarray(x[b].reshape(8, 512, 8, 128).transpose(0, 3, 2, 1)) for b in range(cores)]
    pTs = [np.ascontiguousarray(p[:, b].reshape(DEPTH, 8, 512, 2, 128).transpose(0, 1, 4, 3, 2)) for b in range(cores)]
    for l in range(DEPTH):
        nc, _ = build(shared, "layer%d" % l)
        in_maps = []
        for b in range(cores):
            m = dict(shared)
            m["xT"] = hs[b]
            m["pT"] = pTs[b]
            in_maps.append(m)
        res = run_bass_kernel_spmd(nc, in_maps, core_ids=list(range(cores)))
        hs = [np.ascontiguousarray(r["yT"]) for r in res.results]
    out = np.stack([np.ascontiguousarray(h.transpose(0, 3, 2, 1)).reshape(T, D) for h in hs], axis=0)
    return out.astype(np.float32)


def run(inputs, mode="full", cores=8):
    shared = prep_shared(inputs)
    nc, scr = build(shared, mode)
    x = np.asarray(inputs["x"], dtype=np.float32)
    p = np.asarray(inputs["p"], dtype=np.float32)
    in_maps = []
    import os
    boff = int(os.environ.get("BOFF", "0"))
    for b in range(boff, boff + cores):
        m = dict(shared)
        m["xT"] = np.ascontiguousarray(x[b].reshape(8, 512, 8, 128).transpose(0, 3, 2, 1))
        m["pT"] = np.ascontiguousarray(p[:, b].reshape(DEPTH, 8, 512, 2, 128).transpose(0, 1, 4, 3, 2))
        in_maps.append(m)
    cids = [int(c) for c in os.environ["CIDS"].split(",")] if "CIDS" in os.environ else list(range(cores))
    res = run_bass_kernel_spmd(nc, in_maps, core_ids=cids)
    if mode != "full":
        return res.results
    out = np.stack([np.ascontiguousarray(r["yT"].transpose(0, 3, 2, 1)).reshape(T, D) for r in res.results], axis=0)
    return out.astype(np.float32)


def phase_proj(P, A, l, hin, S):
    TB = 512
    with ExitStack() as es:
        P.es = es
        W1 = P.sb([128, 8, 1408], BF16)
        W2 = P.sb([128, 8, 1408], BF16)
        wn = P.sb([128, 8, 1426], BF16)
        ones_bf = P.sb([128, 128], BF16)
        gpre = P.sb([128, 8], F32)
        mu = P.sb([128, 1408], F32)
        stg = [P.sb([128, 1408], F32, name=f"stg{i}") for i in range(2)]
        w2b = P.sb([128, 384], BF16)
        a2b = P.sb([128, 384], BF16)
        g2b = P.sb([128, 384], BF16)
        P.dma("pool", ones_bf[:], A["ones"], writes=[ones_bf])
        P.dma("sp", gpre[:], A["pre_mix_normT"][l], writes=[gpre])
        load_bcast(P, "sp", mu[:], A["shift_mu"][l], 1408)
        P.dma("pool", w2b[:], A["rw_w2p"][l], writes=[w2b])
        P.dma("pool", a2b[:], A["rw_a2p"][l], writes=[a2b])
        P.dma("pool", g2b[:], A["rw_g2"][l], writes=[g2b])
        for c in range(8):
            P.dma("pool", wn[:, c, :], A["w_in"][l, c * 128:(c + 1) * 128, 1408:2834], writes=[wn])
            st = stg[c % 2]
            P.dma("sp", st[:], A["w_in"][l, c * 128:(c + 1) * 128, 0:1408], writes=[st])
            P.tt(W2[:, c, :], st[:], mu[:], ALU.mult, r=[st, mu], w=[W2])
            P.tt(W1[:, c, :], st[:], W2[:, c, :], ALU.subtract, r=[st, W2], w=[W1], eng="pool")

        hb = P.sb([128, 8, TB], F32)
        sq = P.sb([128, 8, TB], BF16)
        xn = P.sb([128, 8, 1 + TB], BF16)
        rstd = P.sb([128, TB], F32)
        rkv = [P.sb([128, 1152], F32, name=f"rkv{i}") for i in range(2)]
        lor = [P.sb([128, 1152], F32, name=f"lor{i}") for i in range(2)]
        twa = P.sb([128, TB], BF16)
        tg = P.sb([128, TB], BF16)
        fmo = [P.sb([128, TB], BF16, name=f"fmo{i}") for i in range(3)]
        uo = [P.sb([128, TB], F32, name=f"uo{i}") for i in range(2)]
        vsw = [P.sb([128, 256], BF16, name=f"vsw{i}") for i in range(2)]
        gts = [P.sb([128, 18], F32, name=f"gts{i}") for i in range(2)]
        pss = [P.ps([128, 512], F32, name=f"psj{i}") for i in range(6)]
        psst = P.ps([128, 512], F32, name="psjst")
        P.memset(xn[:], 0.0, w=[xn])
        pi = 0

        def shifted_group(ps_ap, cols, tok_lo, tok_n, fm):
            for c in range(8):
                for sh, W in ((1, W1), (0, W2)):
                    xs = xn[:, c, sh + tok_lo: sh + tok_lo + tok_n]
                    ws = W[:, c, cols]
                    first = (c == 0 and sh == 1)
                    last = (c == 7 and sh == 0)
                    if fm:
                        P.mm(ps_ap, ws, xs, first, last, r=[W1, W2, xn], w=[ps_ap.tensor])
                    else:
                        P.mm(ps_ap, xs, ws, first, last, r=[W1, W2, xn], w=[ps_ap.tensor])

        for b in range(T // TB):
            t0 = b * TB
            ts_ = slice(t0, t0 + TB)
            P.dma("sp", hb[:], hin[b], writes=[hb])
            P.act(sq[:], hb[:], AF.Square, r=[hb], w=[sq])
            rms_stats(P, sq, 8, ones_bf, psst, rstd, TB)
            if b > 0:
                P.cp(xn[:, :, 0:1], xn[:, :, TB:TB + 1], r=[xn], w=[xn])
            for c in range(8):
                P.stt(xn[:, c, 1:1 + TB], hb[:, c, :], gpre[:, c:c + 1], rstd[:], ALU.mult, ALU.mult, r=[hb, rstd, gpre], w=[xn])
            ps = pss[pi % 6]
            pi += 1
            shifted_group(ps[:, :], slice(1152, 1280), 0, TB, True)
            P.act(twa[0:64, :], ps[0:64, :], AF.Tanh, r=[ps], w=[twa])
            P.act(twa[64:128, :], ps[64:128, :], AF.Copy, r=[ps], w=[twa])
            ps = pss[pi % 6]
            pi += 1
            shifted_group(ps[:, :], slice(1280, 1408), 0, TB, True)
            P.act(tg[:, :], ps[:, :], AF.Sigmoid, r=[ps], w=[tg])
            for tt in range(4):
                tsl = slice(tt * 128, (tt + 1) * 128)
                rk = rkv[tt % 2]
                for j in range(3):
                    ps = pss[pi % 6]
                    pi += 1
                    shifted_group(ps[:, 0:384], slice(j * 384, (j + 1) * 384), tt * 128, 128, False)
                    if j == 1:
                        P.cp(rk[:, j * 384:(j + 1) * 384], ps[:, 0:384], r=[ps], w=[rk])
                    else:
                        P.act(rk[:, j * 384:(j + 1) * 384], ps[:, 0:384], AF.Copy, r=[ps], w=[rk])
                P.dma("sp", S["rkv"][t0 + tt * 128: t0 + (tt + 1) * 128, :], rk[:], reads=[rk], writes=["rkv_d"])
                lo = lor[tt % 2]
                for j, (src, wgt) in enumerate(((twa, w2b), (twa, a2b), (tg, g2b))):
                    ps = pss[pi % 6]
                    pi += 1
                    P.mm(ps[:, 0:384], src[:, tsl], wgt[:, :], True, True, r=[src, wgt], w=[ps])
                    P.cp(lo[:, j * 384:(j + 1) * 384], ps[:, 0:384], r=[ps], w=[lo])
                P.dma("sp", S["lor"][t0 + tt * 128: t0 + (tt + 1) * 128, :], lo[:], reads=[lo], writes=["lor_d"])
                ps = pss[pi % 6]
                pi += 1
                for (c0, n, o0) in ((2176 - 1408, 128, 0), (2432 - 1408, 128, 128), (2560 - 1408, 18, 256)):
                    for c in range(8):
                        P.mm(ps[:, o0:o0 + n], xn[:, c, 1 + tt * 128: 1 + (tt + 1) * 128], wn[:, c, c0:c0 + n], c == 0, c == 7,
                             r=[xn, wn], w=[ps])
                vv = vsw[tt % 2]
                gg = gts[tt % 2]
                P.cp(vv[:], ps[:, 0:256], r=[ps], w=[vv])
                P.act(gg[:], ps[:, 256:274], AF.Sigmoid, r=[ps], w=[gg])
                P.dma("sp", S["vsw"][t0 + tt * 128: t0 + (tt + 1) * 128, :], vv[:], reads=[vv], writes=["vsw_d"])
                P.dma("sp", S["gates"][t0 + tt * 128: t0 + (tt + 1) * 128, :], gg[:], reads=[gg], writes=["gates_d"])
            for k, (c0, dname, scale) in enumerate(((0, "q0", 0.125), (128, "q1", 0.125), (256, "q2", 0.125),
                                                    (1792 - 1408, "kc", 1.0), (1920 - 1408, "vc", 1.0),
                                                    (2048 - 1408, "ks", 1.0), (2304 - 1408, "kw", 1.0))):
                ps = pss[pi % 6]
                pi += 1
                for c in range(8):
                    P.mm(ps[:, :], wn[:, c, c0:c0 + 128], xn[:, c, 1:1 + TB], c == 0, c == 7, r=[wn, xn], w=[ps])
                o = fmo[k % 3]
                P.act(o[:], ps[:], AF.Copy, scale=scale, r=[ps], w=[o])
                P.dma("sp", S[dname][:, ts_], o[:], reads=[o], writes=[dname + "_d"])
            for k in range(2):
                ps = pss[pi % 6]
                pi += 1
                c0 = 2578 - 1408 + k * 128
                for c in range(8):
                    P.mm(ps[:, :], wn[:, c, c0:c0 + 128], xn[:, c, 1:1 + TB], c == 0, c == 7, r=[wn, xn], w=[ps])
                o = uo[k]
                P.cp(o[:], ps[:], r=[ps], w=[o])
                P.dma("sp", S["uT"][k * 128:(k + 1) * 128, ts_], o[:], reads=[o], writes=["uT_d"])
        P.barrier()
        P.emit()
        P.renew_sems()
    P.es = P.ges


def phase_out(P, A, l, hin, mixT, hout):
    TB = 512
    with ExitStack() as es:
        P.es = es
        wo = P.sb([128, 8, D], BF16)
        ones_bf = P.sb([128, 128], BF16)
        gpost = P.sb([128, 8], F32)
        P.dma("pool", ones_bf[:], A["ones"], writes=[ones_bf])
        P.dma("sp", gpost[:], A["post_mix_normT"][l], writes=[gpost])
        for c in range(8):
            P.dma("pool", wo[:, c, :], A["w_out"][l, c * 128:(c + 1) * 128, :], writes=[wo])
        hb = [P.sb([128, 8, TB], F32, name=f"ohb{i}") for i in range(2)]
        mx = [P.sb([128, 8, TB], BF16, name=f"omx{i}") for i in range(2)]
        fo = P.sb([128, 8, TB], F32)
        sq = P.sb([128, 8, TB], BF16)
        rstd = P.sb([128, TB], F32)
        pss = [P.ps([128, 512], F32, name=f"pso{i}") for i in range(6)]
        psst = P.ps([128, 512], F32, name="psost")
        pi = 0
        for b in range(T // TB):
            ts_ = slice(b * TB, (b + 1) * TB)
            h = hb[b % 2]
            m = mx[b % 2]
            P.dma("sp", h[:], hin[b], writes=[h])
            P.dma("sp", m[:], mixT[b], writes=[m])
            for n in range(8):
                ps = pss[pi % 6]
                pi += 1
                for c in range(8):
                    P.mm(ps[:], wo[:, c, n * 128:(n + 1) * 128], m[:, c, :], c == 0, c == 7, r=[wo, m], w=[ps])
                P.act(fo[:, n, :], ps[:], AF.Copy, r=[ps], w=[fo])
            P.act(sq[:], fo[:], AF.Square, r=[fo], w=[sq])
            rms_stats(P, sq, 8, ones_bf, psst, rstd, TB)
            for c in range(8):
                P.stt(fo[:, c, :], fo[:, c, :], gpost[:, c:c + 1], rstd[:], ALU.mult, ALU.mult, r=[fo, rstd, gpost], w=[fo])
            P.tt(h[:], h[:], fo[:], ALU.add, r=[h, fo], w=[h])
            P.dma("sp", hout[b], h[:], reads=[h], writes=["hout"])
        P.barrier()
        P.emit()
        P.renew_sems()
    P.es = P.ges


def prep_s5(inp, S):
    f = lambda a: np.ascontiguousarray(np.asarray(a, dtype=np.float32))
    L = DEPTH
    st = lambda a: f(np.asarray(a).reshape(L, 8, 128).transpose(0, 2, 1))
    S["s5_lam_reT"] = st(inp["s5_lam_re"])
    S["s5_lam_imT"] = st(inp["s5_lam_im"])
    S["s5_logdtT"] = st(np.repeat(np.asarray(inp["s5_log_dt"])[:, :, None], 64, axis=2))
    bre = np.zeros((L, 256, 1024), np.float32)
    bim = np.zeros((L, 256, 1024), np.float32)
    cre = np.zeros((L, 1024, 256), np.float32)
    cim = np.zeros((L, 1024, 256), np.float32)
    for g in range(16):
        bre[:, g * 16:(g + 1) * 16, g * 64:(g + 1) * 64] = np.asarray(inp["s5_b_re"])[:, g].transpose(0, 2, 1)
        bim[:, g * 16:(g + 1) * 16, g * 64:(g + 1) * 64] = np.asarray(inp["s5_b_im"])[:, g].transpose(0, 2, 1)
        cre[:, g * 64:(g + 1) * 64, g * 16:(g + 1) * 16] = np.asarray(inp["s5_c_re"])[:, g].transpose(0, 2, 1)
        cim[:, g * 64:(g + 1) * 64, g * 16:(g + 1) * 16] = np.asarray(inp["s5_c_im"])[:, g].transpose(0, 2, 1)
    S["s5_bre"], S["s5_bim"], S["s5_cre"], S["s5_cim"] = bre, bim, cre, cim
    S["s5_dT"] = f(np.asarray(inp["s5_d"]).reshape(L, 2, 128).transpose(0, 2, 1))
    S["s5_w_glu"] = f(inp["s5_w_glu"])


def phase_s5(P, A, l, S, mixT):
    TB = 512
    NL = 9
    PI = math.pi
    with ExitStack() as es:
        P.es = es
        lre = P.sb([128, 8], F32)
        lim = P.sb([128, 8], F32)
        ldt = P.sb([128, 8], F32)
        bre = P.sb([128, 2, 1024], BF16)
        bim = P.sb([128, 2, 1024], BF16)
        cre = P.sb([128, 8, 256], BF16)
        cim = P.sb([128, 8, 256], BF16)
        dsk = P.sb([128, 2], F32)
        wgl = P.sb([128, 2, 512], BF16)
        P.dma("sp", lre[:], A["s5_lam_reT"][l], writes=[lre])
        P.dma("sp", lim[:], A["s5_lam_imT"][l], writes=[lim])
        P.dma("sp", ldt[:], A["s5_logdtT"][l], writes=[ldt])
        P.dma("sp", dsk[:], A["s5_dT"][l], writes=[dsk])
        for c in range(2):
            P.dma("pool", bre[:, c, :], A["s5_bre"][l, c * 128:(c + 1) * 128, :], writes=[bre])
            P.dma("pool", bim[:, c, :], A["s5_bim"][l, c * 128:(c + 1) * 128, :], writes=[bim])
            P.dma("pool", wgl[:, c, :], A["s5_w_glu"][l, c * 128:(c + 1) * 128, :], writes=[wgl])
        for c in range(8):
            P.dma("pool", cre[:, c, :], A["s5_cre"][l, c * 128:(c + 1) * 128, :], writes=[cre])
            P.dma("pool", cim[:, c, :], A["s5_cim"][l, c * 128:(c + 1) * 128, :], writes=[cim])
        sm = lambda n: P.sb([128, 8], F32, name=n)
        dt, mag, ang, x, acc, tmp = sm("dt"), sm("mag"), sm("ang"), sm("x5"), sm("acc5"), sm("tmp5")
        abr, abi, fre, fim, den, t2 = sm("abr"), sm("abi"), sm("fre"), sm("fim"), sm("den"), sm("t25")
        nfim = sm("nfim")
        P.act(dt[:], ldt[:], AF.Exp, r=[ldt], w=[dt])
        P.tt(mag[:], lre[:], dt[:], ALU.mult, r=[lre, dt], w=[mag])
        P.act(mag[:], mag[:], AF.Exp, r=[mag], w=[mag])
        P.tt(ang[:], lim[:], dt[:], ALU.mult, r=[lim, dt], w=[ang])

        def sin_of(dst, shift):
            P.ts(x[:], ang[:], shift + PI, None, ALU.add, r=[ang], w=[x])
            P.cp(acc[:], x[:], r=[x], w=[acc])
            for k in (1, 2, 3):
                P.ts(tmp[:], x[:], 2 * PI * k, -2 * PI, ALU.is_ge, ALU.mult, r=[x], w=[tmp])
                P.tt(acc[:], acc[:], tmp[:], ALU.add, r=[acc, tmp], w=[acc])
            P.ts(acc[:], acc[:], -PI, None, ALU.add, r=[acc], w=[acc])
            P.ts(tmp[:], acc[:], -1.0, PI, ALU.mult, ALU.add, r=[acc], w=[tmp])
            P.tt(tmp[:], tmp[:], acc[:], ALU.min, r=[tmp, acc], w=[tmp])
            P.ts(acc[:], acc[:], -1.0, -PI, ALU.mult, ALU.add, r=[acc], w=[acc])
            P.tt(acc[:], acc[:], tmp[:], ALU.max, r=[tmp, acc], w=[acc])
            P.tt(t2[:], acc[:], acc[:], ALU.mult, r=[acc], w=[t2])
            P.ts(tmp[:], t2[:], 1.0 / 6227020800.0, None, ALU.mult, r=[t2], w=[tmp])
            for cf in (-1.0 / 39916800.0, 1.0 / 362880.0, -1.0 / 5040.0, 1.0 / 120.0, -1.0 / 6.0):
                P.stt(tmp[:], tmp[:], cf, t2[:], ALU.add, ALU.mult, r=[tmp, t2], w=[tmp])
            P.stt(dst[:], tmp[:], 1.0, acc[:], ALU.add, ALU.mult, r=[tmp, acc], w=[dst])

        sin_of(abi, 0.0)
        sin_of(abr, PI / 2)
        P.tt(abr[:], abr[:], mag[:], ALU.mult, r=[abr, mag], w=[abr])
        P.tt(abi[:], abi[:], mag[:], ALU.mult, r=[abi, mag], w=[abi])
        P.tt(den[:], lre[:], lre[:], ALU.mult, r=[lre], w=[den])
        P.tt(t2[:], lim[:], lim[:], ALU.mult, r=[lim], w=[t2])
        P.tt(den[:], den[:], t2[:], ALU.add, r=[den, t2], w=[den])
        P.op("dve", lambda e: e.reciprocal(out=den[:], in_=den[:]), [Prog._k(den)], [Prog._k(den)])
        P.ts(tmp[:], abr[:], -1.0, None, ALU.add, r=[abr], w=[tmp])
        P.tt(fre[:], tmp[:], lre[:], ALU.mult, r=[tmp, lre], w=[fre])
        P.tt(t2[:], abi[:], lim[:], ALU.mult, r=[abi, lim], w=[t2])
        P.tt(fre[:], fre[:], t2[:], ALU.add, r=[fre, t2], w=[fre])
        P.tt(fre[:], fre[:], den[:], ALU.mult, r=[fre, den], w=[fre])
        P.tt(fim[:], abi[:], lre[:], ALU.mult, r=[abi, lre], w=[fim])
        P.tt(t2[:], tmp[:], lim[:], ALU.mult, r=[tmp, lim], w=[t2])
        P.tt(fim[:], fim[:], t2[:], ALU.subtract, r=[fim, t2], w=[fim])
        P.tt(fim[:], fim[:], den[:], ALU.mult, r=[fim, den], w=[fim])
        P.ts(nfim[:], fim[:], -1.0, None, ALU.mult, r=[fim], w=[nfim])
        pwr = [abr] + [sm(f"pwr{k}") for k in range(1, NL)]
        pwi = [abi] + [sm(f"pwi{k}") for k in range(1, NL)]
        npwi = [sm(f"npwi{k}") for k in range(NL)]
        for k in range(NL):
            P.ts(npwi[k][:], pwi[k][:], -1.0, None, ALU.mult, r=[pwi[k]], w=[npwi[k]])
            if k + 1 < NL:
                P.tt(pwr[k + 1][:], pwr[k][:], pwr[k][:], ALU.mult, r=[pwr[k]], w=[pwr[k + 1]])
                P.tt(t2[:], pwi[k][:], pwi[k][:], ALU.mult, r=[pwi[k]], w=[t2])
                P.tt(pwr[k + 1][:], pwr[k + 1][:], t2[:], ALU.subtract, r=[pwr[k + 1], t2], w=[pwr[k + 1]])
                P.tt(pwi[k + 1][:], pwr[k][:], pwi[k][:], ALU.mult, r=[pwr[k], pwi[k]], w=[pwi[k + 1]])
                P.ts(pwi[k + 1][:], pwi[k + 1][:], 2.0, None, ALU.mult, r=[pwi[k + 1]], w=[pwi[k + 1]])
        if "dbg" in S:
            for i_, t_ in enumerate((dt, mag, ang, abr, abi, fre, fim, den, pwr[NL - 1], pwi[NL - 1])):
                P.dma("sp", S["dbg"][:, i_ * 8:(i_ + 1) * 8], t_[:], reads=[t_])
        cst_re = P.sb([128, 8], F32)
        cst_im = P.sb([128, 8], F32)
        P.memset(cst_re[:], 0.0, w=[cst_re])
        P.memset(cst_im[:], 0.0, w=[cst_im])
        uf = P.sb([128, 2, TB], F32)
        ub = P.sb([128, 2, TB], BF16)
        Are = [P.sb([128, TB], F32, name=f"Are{i}") for i in range(2)]
        Aim = [P.sb([128, TB], F32, name=f"Aim{i}") for i in range(2)]
        sre = P.sb([128, 8, TB], BF16)
        sim = P.sb([128, 8, TB], BF16)
        yv = P.sb([128, 2, TB], F32)
        yt = P.sb([128, TB], F32)
        yg = P.sb([128, 2, TB], BF16)
        gl = P.sb([128, 4, TB], F32)
        ob = P.sb([128, 2, TB], BF16)
        c4 = P.sb([128, 4], F32)
        psx = [P.ps([128, 512], F32, name=f"ps5{i}") for i in range(4)]
        psy = [P.ps([128, 512], F32, name=f"ps5y{i}") for i in range(2)]
        GC = 2.0 * math.sqrt(2.0 / math.pi)
        uT = S["uT"].rearrange("(c p) t -> p c t", p=128)
        for b in range(T // TB):
            ts_ = slice(b * TB, (b + 1) * TB)
            P.dma("sp", uf[:], uT[:, :, ts_], writes=[uf])
            P.cp(ub[:], uf[:], r=[uf], w=[ub], eng="act")
            for m in range(8):
                kc = m // 4
                pr, pim = psx[(2 * m) % 4], psx[(2 * m + 1) % 4]
                P.mm(pr[:], bre[:, kc, m * 128:(m + 1) * 128], ub[:, kc, :], True, True, r=[bre, ub], w=[pr])
                P.mm(pim[:], bim[:, kc, m * 128:(m + 1) * 128], ub[:, kc, :], True, True, r=[bim, ub], w=[pim])
                a_re, a_im = Are[0], Aim[0]
                mc = slice(m, m + 1)
                P.ts(a_re[:], pr[:], fre[:, mc], None, ALU.mult, r=[pr, fre], w=[a_re])
                P.stt(a_re[:], pim[:], nfim[:, mc], a_re[:], ALU.mult, ALU.add, r=[pim, nfim, a_re], w=[a_re])
                P.ts(a_im[:], pim[:], fre[:, mc], None, ALU.mult, r=[pim, fre], w=[a_im])
                P.stt(a_im[:], pr[:], fim[:, mc], a_im[:], ALU.mult, ALU.add, r=[pr, fim, a_im], w=[a_im])
                P.tt(c4[:, 0:1], abr[:, mc], cst_re[:, mc], ALU.mult, r=[abr, cst_re], w=[c4])
                P.tt(c4[:, 1:2], abi[:, mc], cst_im[:, mc], ALU.mult, r=[abi, cst_im], w=[c4])
                P.tt(c4[:, 2:3], abr[:, mc], cst_im[:, mc], ALU.mult, r=[abr, cst_im], w=[c4])
                P.tt(c4[:, 3:4], abi[:, mc], cst_re[:, mc], ALU.mult, r=[abi, cst_re], w=[c4])
                P.tt(a_re[:, 0:1], a_re[:, 0:1], c4[:, 0:1], ALU.add, r=[a_re, c4], w=[a_re])
                P.tt(a_re[:, 0:1], a_re[:, 0:1], c4[:, 1:2], ALU.subtract, r=[a_re, c4], w=[a_re])
                P.tt(a_im[:, 0:1], a_im[:, 0:1], c4[:, 2:3], ALU.add, r=[a_im, c4], w=[a_im])
                P.tt(a_im[:, 0:1], a_im[:, 0:1], c4[:, 3:4], ALU.add, r=[a_im, c4], w=[a_im])
                cur = 0
                for k in range(NL):
                    d = 1 << k
                    sr, si = Are[cur], Aim[cur]
                    dr, di = Are[1 - cur], Aim[1 - cur]
                    P.stt(dr[:, d:], sr[:, :TB - d], pwr[k][:, mc], sr[:, d:], ALU.mult, ALU.add, r=[sr, pwr[k]], w=[dr])
                    P.stt(dr[:, d:], si[:, :TB - d], npwi[k][:, mc], dr[:, d:], ALU.mult, ALU.add, r=[si, npwi[k], dr], w=[dr])
                    P.stt(di[:, d:], si[:, :TB - d], pwr[k][:, mc], si[:, d:], ALU.mult, ALU.add, r=[si, pwr[k]], w=[di])
                    P.stt(di[:, d:], sr[:, :TB - d], pwi[k][:, mc], di[:, d:], ALU.mult, ALU.add, r=[sr, pwi[k], di], w=[di])
                    P.cp(dr[:, :d], sr[:, :d], r=[sr], w=[dr], eng="pool")
                    P.cp(di[:, :d], si[:, :d], r=[si], w=[di], eng="pool")
                    cur = 1 - cur
                fr, fi = Are[cur], Aim[cur]
                P.cp(cst_re[:, mc], fr[:, TB - 1:TB], r=[fr], w=[cst_re])
                P.cp(cst_im[:, mc], fi[:, TB - 1:TB], r=[fi], w=[cst_im])
                P.cp(sre[:, m, :], fr[:], r=[fr], w=[sre], eng="act")
                P.act(sim[:, m, :], fi[:], AF.Copy, scale=-1.0, r=[fi], w=[sim])
                if cur != 0:
                    pass
            for j in range(2):
                py = psy[j]
                n = 0
                for kc in range(4 * j, 4 * j + 4):
                    P.mm(py[:], cre[:, kc, j * 128:(j + 1) * 128], sre[:, kc, :], n == 0, False, r=[cre, sre], w=[py])
                    n += 1
                    P.mm(py[:], cim[:, kc, j * 128:(j + 1) * 128], sim[:, kc, :], False, kc == 4 * j + 3, r=[cim, sim], w=[py])
                P.stt(yv[:, j, :], uf[:, j, :], dsk[:, j:j + 1], py[:], ALU.mult, ALU.add, r=[uf, dsk, py], w=[yv])
                P.tt(yt[:], yv[:, j, :], yv[:, j, :], ALU.mult, r=[yv], w=[yt])
                P.ts(yt[:], yt[:], 0.044715, 1.0, ALU.mult, ALU.add, r=[yt], w=[yt])
                P.tt(yt[:], yt[:], yv[:, j, :], ALU.mult, r=[yt, yv], w=[yt])
                P.act(yt[:], yt[:], AF.Sigmoid, scale=GC, r=[yt], w=[yt])
                P.tt(yg[:, j, :], yt[:], yv[:, j, :], ALU.mult, r=[yt, yv], w=[yg])
            for n in range(4):
                pg = psx[n]
                for kc in range(2):
                    P.mm(pg[:], wgl[:, kc, n * 128:(n + 1) * 128], yg[:, kc, :], kc == 0, kc == 1, r=[wgl, yg], w=[pg])
                if n < 2:
                    P.cp(gl[:, n, :], pg[:], r=[pg], w=[gl])
                else:
                    P.act(gl[:, n, :], pg[:], AF.Sigmoid, r=[pg], w=[gl])
            P.tt(ob[:], gl[:, 0:2, :], gl[:, 2:4, :], ALU.mult, r=[gl], w=[ob])
            P.dma("sp", mixT[b][:, 6:8, :], ob[:], reads=[ob], writes=["mix_s5"])
        P.barrier()
        P.emit()
        P.renew_sems()
    P.es = P.ges


def phase_rwkv(P, A, l, S, mixT, ntiles=T // 128, stage=9):
    C = C_DEC
    with ExitStack() as es:
        P.es = es

        def cst(name, shape):
            t_ = P.sb(shape, F32, name="c_" + name)
            P.dma("sp", t_[:], A[name], writes=[t_])
            return t_

        tri = cst("tri_incl", [128, 128])
        same = cst("same", [128, 128])
        cind = cst("chunkind", [128, 2])
        cind64 = cst("chunkind64", [128, 64])
        mask4 = cst("mask4", [128, 512])
        masklt = cst("mask_lt", [128, 128])
        ident = cst("ident", [128, 128])
        identb = P.sb([128, 128], BF16)
        P.dma("pool", identb[:], A["ident"], writes=[identb])
        par = {}
        for nm in ("rw_w0", "rw_a0", "rw_k_k", "rw_k_a", "rw_r_k", "rw_gn_w", "rw_gn_b"):
            t_ = P.sb([128, 384], F32, name="p_" + nm)
            load_bcast(P, "sp", t_[:], A[nm][l], 384)
            par[nm] = t_
        ST = P.sb([128, 3, 2, 64], F32)
        P.memset(ST[:], 0.0, w=[ST])
        fmz = [[P.sb([128, 512], F32, name=f"fmz{g}{e}") for e in range(2)] for g in range(3)]
        Bdz = [P.sb([128, 384], F32, name=f"Bdz{c}") for c in range(2)]
        Kdz = [P.sb([128, 384], F32, name=f"Kdz{c}") for c in range(2)]
        rkvb = [P.sb([128, 1152], F32, name=f"rkvb{i}") for i in range(2)]
        lorb = [P.sb([128, 1152], F32, name=f"lorb{i}") for i in range(2)]
        w = lambda n: P.sb([128, 384], F32, name="w_" + n)
        sig, a, kk, kp, cs, tmpx, tmpe, Ep, Em, Ex, Eend, ka, Bd, Kd, t4, On = [w(n) for n in (
            "sig", "a", "kk", "kp", "cs", "tmpx", "tmpe", "Ep", "Em", "Ex", "Eend", "ka", "Bd", "Kd", "t4", "On")]
        Q4 = P.sb([128, 4, 384], F32)
        fm = [P.sb([128, 512], F32, name=f"fm{g}") for g in range(3)]
        Gs = P.sb([128, 6, 512], F32)
        MA = [P.sb([128, 6, 128], F32, name=f"MA{i}") for i in range(2)]
        MTA = [P.sb([128, 6, 128], F32, name=f"MTA{i}") for i in range(2)]
        Rall = P.sb([128, 6, 128], F32)
        XT = P.sb([128, 384], F32)
        WT = P.sb([128, 384], F32)
        P.memset(XT[:], 0.0, w=[XT])
        P.memset(WT[:], 0.0, w=[WT])
        O = P.sb([128, 384], F32)
        Ob = P.sb([128, 384], BF16)
        OT = [P.sb([128, 3, 512], BF16, name=f"OT{i}") for i in range(2)]
        ss = P.sb([128, 6], F32)
        bsum = P.sb([128, 6], F32)
        s1 = P.sb([128, 6], F32)
        s2 = P.sb([128, 6], F32)
        m2 = P.sb([128, 6], F32)
        pcs = P.sb([128, 6], F32)
        pool_ps = [P.ps([128, 512], F32, name=f"psr{i}") for i in range(5)]
        psM_fixed = [P.ps([128, 512], F32, name=f"psrM{i}") for i in range(2)]
        rr = [0]

        def nps():
            p_ = pool_ps[rr[0] % 5]
            rr[0] += 1
            return p_

        v3 = lambda t_: t_[:].rearrange("p (h j) -> p h j", j=64)
        b3 = lambda t_: t_[:].unsqueeze(2).to_broadcast([128, 6, 64])
        recip = lambda t_: P.op("dve", lambda e: e.reciprocal(out=t_[:], in_=t_[:]), [Prog._k(t_)], [Prog._k(t_)])

        for ti in range(ntiles):
            t0 = ti * 128
            RK = rkvb[ti % 2]
            LO = lorb[ti % 2]
            P.dma("sp", RK[:], S["rkv"][t0:t0 + 128, :], writes=[RK])
            P.dma("sp", LO[:], S["lor"][t0:t0 + 128, :], writes=[LO])
            R, Kx, V = RK[:, 0:384], RK[:, 384:768], RK[:, 768:1152]
            XW, XA, G = LO[:, 0:384], LO[:, 384:768], LO[:, 768:1152]
            P.tt(sig[:], XW, par["rw_w0"][:], ALU.add, r=[LO, par["rw_w0"]], w=[sig])
            P.act(sig[:], sig[:], AF.Sigmoid, r=[sig], w=[sig])
            P.tt(a[:], XA, par["rw_a0"][:], ALU.add, r=[LO, par["rw_a0"]], w=[a])
            P.act(a[:], a[:], AF.Sigmoid, r=[a], w=[a])
            P.tt(kk[:], Kx, par["rw_k_k"][:], ALU.mult, r=[RK, par["rw_k_k"]], w=[kk])
            P.tt(t4[:], kk[:], kk[:], ALU.mult, r=[kk], w=[t4])
            P.red(ss[:], v3(t4), ALU.add, r=[t4], w=[ss])
            P.ts(ss[:], ss[:], 1e-24, None, ALU.max, r=[ss], w=[ss])
            P.act(ss[:], ss[:], AF.Sqrt, r=[ss], w=[ss])
            recip(ss)
            P.tt(v3(kk), v3(kk), b3(ss), ALU.mult, r=[kk, ss], w=[kk])
            P.stt(t4[:], a[:], -1.0, par["rw_k_a"][:], ALU.add, ALU.mult, r=[a, par["rw_k_a"]], w=[t4])
            P.stt(kp[:], t4[:], 1.0, Kx, ALU.add, ALU.mult, r=[t4, RK], w=[kp])
            psA, psB = nps(), nps()
            P.mm(psA[:, 0:384], tri[:], sig[:], True, True, r=[tri, sig], w=[psA])
            P.mm(psB[:, 0:384], same[:], sig[:], True, True, r=[same, sig], w=[psB])
            P.cp(cs[:], psA[:, 0:384], r=[psA], w=[cs], eng="act")
            P.act(Ep[:], cs[:], AF.Exp, scale=-C, r=[cs], w=[Ep])
            P.act(Em[:], cs[:], AF.Exp, scale=C, r=[cs], w=[Em])
            P.tt(tmpx[:], cs[:], sig[:], ALU.subtract, r=[cs, sig], w=[tmpx])
            P.act(Ex[:], tmpx[:], AF.Exp, scale=-C, r=[tmpx], w=[Ex])
            P.tt(tmpe[:], psB[:, 0:384], cs[:], ALU.subtract, r=[psB, cs], w=[tmpe])
            P.act(Eend[:], tmpe[:], AF.Exp, scale=-C, r=[tmpe], w=[Eend])
            P.tt(ka[:], kk[:], a[:], ALU.mult, r=[kk, a], w=[ka])
            P.stt(Q4[:, 0, :], kk[:], -1.0, Ex[:], ALU.mult, ALU.mult, r=[kk, Ex], w=[Q4])
            P.tt(Q4[:, 1, :], R, Ep[:], ALU.mult, r=[RK, Ep], w=[Q4])
            P.tt(Q4[:, 2, :], ka[:], Em[:], ALU.mult, r=[ka, Em], w=[Q4])
            P.tt(Q4[:, 3, :], kp[:], Em[:], ALU.mult, r=[kp, Em], w=[Q4])
            P.tt(Bd[:], ka[:], Eend[:], ALU.mult, r=[ka, Eend], w=[Bd])
            P.tt(Kd[:], kp[:], Eend[:], ALU.mult, r=[kp, Eend], w=[Kd])
            for c_ in range(2):
                P.ts(Bdz[c_][:], Bd[:], cind[:, c_:c_ + 1], None, ALU.mult, r=[Bd, cind], w=[Bdz[c_]], eng="pool")
                P.ts(Kdz[c_][:], Kd[:], cind[:, c_:c_ + 1], None, ALU.mult, r=[Kd, cind], w=[Kdz[c_]], eng="pool")
            P.tt(t4[:], R, kp[:], ALU.mult, r=[RK, kp], w=[t4])
            P.tt(t4[:], t4[:], par["rw_r_k"][:], ALU.mult, r=[t4, par["rw_r_k"]], w=[t4])
            P.red(bsum[:], v3(t4), ALU.add, r=[t4], w=[bsum])
            if stage <= 1:
                continue
            psP = nps()
            for g in range(3):
                psT = nps()
                for q in range(4):
                    P.mm(psT[:, q * 128:(q + 1) * 128], Q4[:, q, g * 128:(g + 1) * 128], ident[:], True, True, r=[Q4, ident], w=[psT])
                P.cp(fm[g][:], psT[:], r=[psT], w=[fm[g]], eng="act" if g % 2 else "dve")
                for e_ in range(2):
                    P.ts(fmz[g][e_][:], fm[g][:], cind[:, e_:e_ + 1], None, ALU.mult, r=[fm[g], cind], w=[fmz[g][e_]],
                         eng="pool" if e_ else "dve")
                P.mm(psP[:, g * 64:(g + 1) * 64], sig[:, g * 128:(g + 1) * 128], cind64[:], True, True, r=[sig, cind64], w=[psP])
            P.act(pcs[:].rearrange("p (g c) -> p g c", c=2), psP[:, 0:192].rearrange("p (g c) -> p g c", c=64)[:, :, 0:2], AF.Exp, scale=-C,
                  r=[psP], w=[pcs])
            if stage <= 2:
                continue
            psM = psM_fixed
            for h in range(6):
                g, e_ = h // 2, h % 2
                f_, fz = fm[g], fmz[g][e_]
                psG = nps()
                P.mm(psG[:, 0:256], f_[:, 256:384], fz[:, 0:256], True, True, r=[f_, fz], w=[psG])
                P.mm(psG[:, 256:512], f_[:, 384:512], fz[:, 0:256], True, True, r=[f_, fz], w=[psG])
                P.tt(Gs[:, h, :], psG[:], mask4[:], ALU.mult, r=[psG, mask4], w=[Gs])
                pm = psM[h // 3]
                P.mm(pm[:, (h % 3) * 128:(h % 3 + 1) * 128], f_[:, 0:128], fz[:, 256:384], True, True, r=[f_, fz], w=[pm])
            for half in range(2):
                P.tt(MTA[0][:, 3 * half:3 * half + 3, :], psM[half][:, 0:384].rearrange("p (h u) -> p h u", u=128),
                     masklt[:].unsqueeze(1).to_broadcast([128, 3, 128]), ALU.mult, r=[psM[half], masklt], w=[MTA[0]])
            P.cp(MA[0][:], Gs[:, :, 0:128], r=[Gs], w=[MA[0]], eng="pool")
            P.tt(Rall[:], Gs[:, :, 0:128], ident[:].unsqueeze(1).to_broadcast([128, 6, 128]), ALU.add, r=[Gs, ident], w=[Rall])
            if stage <= 3:
                continue
            cur = 0
            for lvl in range(1, 6):
                last = lvl == 5
                Mc, MTc, Mn, MTn = MA[cur], MTA[cur], MA[1 - cur], MTA[1 - cur]
                for half in range(2):
                    hs = slice(3 * half, 3 * half + 3)
                    psa, psb, psc = nps(), nps(), nps()
                    for hh in range(3):
                        h = 3 * half + hh
                        cs_ = slice(hh * 128, (hh + 1) * 128)
                        if not last:
                            P.mm(psa[:, cs_], MTc[:, h, :], Mc[:, h, :], True, True, r=[MTc, Mc], w=[psa])
                        P.mm(psb[:, cs_], Mc[:, h, :], MTc[:, h, :], True, True, r=[MTc, Mc], w=[psb])
                    if not last:
                        P.cp(Mn[:, hs, :], psa[:, 0:384].rearrange("p (h u) -> p h u", u=128), r=[psa], w=[Mn], eng="act")
                    P.cp(MTn[:, hs, :], psb[:, 0:384].rearrange("p (h u) -> p h u", u=128), r=[psb], w=[MTn])
                    for hh in range(3):
                        h = 3 * half + hh
                        P.mm(psc[:, hh * 128:(hh + 1) * 128], MTn[:, h, :], Rall[:, h, :], True, True, r=[MTn, Rall], w=[psc])
                    P.tt(Rall[:, hs, :], Rall[:, hs, :], psc[:, 0:384].rearrange("p (h u) -> p h u", u=128), ALU.add,
                         r=[Rall, psc], w=[Rall])
                cur = 1 - cur
            if stage <= 4:
                continue
            for cc in range(2):
                cb = cc * 64
                psX, psW, psO, psS = nps(), nps(), nps(), nps()
                for h in range(6):
                    g, e_ = h // 2, h % 2
                    hc = slice(h * 64, (h + 1) * 64)
                    P.mm(psX[:, hc], fm[g][:, 0:128], ST[:, g, e_, :], True, False, r=[fm[g], ST], w=[psX])
                    P.mm(psX[:, hc], Gs[:, h, 256:384], RK[:, 768 + h * 64:768 + (h + 1) * 64], False, True,
                         r=[Gs, RK], w=[psX])
                P.cp(XT[cb:cb + 64, :], psX[cb:cb + 64, 0:384], r=[psX], w=[XT], eng="act")
                for h in range(6):
                    hc = slice(h * 64, (h + 1) * 64)
                    P.mm(psW[:, hc], Rall[:, h, :], XT[:, hc], True, True, r=[Rall, XT], w=[psW])
                P.cp(WT[cb:cb + 64, :], psW[cb:cb + 64, 0:384], r=[psW], w=[WT])
                for h in range(6):
                    g, e_ = h // 2, h % 2
                    hc = slice(h * 64, (h + 1) * 64)
                    Vh = RK[:, 768 + h * 64:768 + (h + 1) * 64]
                    P.mm(psO[:, hc], fm[g][:, 128:256], ST[:, g, e_, :], True, False, r=[fm[g], ST], w=[psO])
                    P.mm(psO[:, hc], Gs[:, h, 128:256], WT[:, hc], False, False, r=[Gs, WT], w=[psO])
                    P.mm(psO[:, hc], Gs[:, h, 384:512], Vh, False, True, r=[Gs, RK], w=[psO])
                    P.mm(psS[:, hc], Bdz[cc][:, g * 128:(g + 1) * 128], WT[:, hc], True, False, r=[Bdz[cc], WT], w=[psS])
                    P.mm(psS[:, hc], Kdz[cc][:, g * 128:(g + 1) * 128], Vh, False, True, r=[Kdz[cc], RK], w=[psS])
                P.cp(O[cb:cb + 64, :], psO[cb:cb + 64, 0:384], r=[psO], w=[O], eng="act")
                for h in range(6):
                    g, e_ = h // 2, h % 2
                    jb = e_ * 64
                    P.stt(ST[jb:jb + 64, g, e_, :], ST[jb:jb + 64, g, e_, :], pcs[jb:jb + 64, 2 * g + cc:2 * g + cc + 1],
                          psS[jb:jb + 64, h * 64:(h + 1) * 64], ALU.mult, ALU.add, r=[ST, pcs, psS], w=[ST])
            if stage <= 5:
                continue
            if "dbgO" in S and ti == 0:
                P.dma("sp", S["dbgO"][:, 0:384], O[:], reads=[O])
                P.dma("sp", S["dbgO"][:, 384:768], XT[:], reads=[XT])
                P.dma("sp", S["dbgO"][:, 768:1152], WT[:], reads=[WT])
                P.dma("sp", S["dbgO"][:, 1152:1664], Gs[:, 0, :], reads=[Gs])
                P.dma("sp", S["dbgO"][:, 1664:1792], Rall[:, 0, :], reads=[Rall])
                P.dma("sp", S["dbgO"][:, 1792:2304], fm[0][:], reads=[fm[0]])
                P.dma("sp", S["dbgO"][:, 2304:2310], pcs[:], reads=[pcs])
            P.red(s1[:], v3(O), ALU.add, r=[O], w=[s1])
            P.tt(t4[:], O[:], O[:], ALU.mult, r=[O], w=[t4])
            P.red(s2[:], v3(t4), ALU.add, r=[t4], w=[s2])
            P.ts(s1[:], s1[:], 1.0 / 64, None, ALU.mult, r=[s1], w=[s1])
            P.ts(s2[:], s2[:], 1.0 / 64, None, ALU.mult, r=[s2], w=[s2])
            P.tt(m2[:], s1[:], s1[:], ALU.mult, r=[s1], w=[m2])
            P.tt(s2[:], s2[:], m2[:], ALU.subtract, r=[s2, m2], w=[s2])
            P.ts(s2[:], s2[:], 64e-5, None, ALU.add, r=[s2], w=[s2])
            P.act(s2[:], s2[:], AF.Sqrt, r=[s2], w=[s2])
            recip(s2)
            P.tt(v3(On), v3(O), b3(s1), ALU.subtract, r=[O, s1], w=[On])
            P.tt(v3(On), v3(On), b3(s2), ALU.mult, r=[On, s2], w=[On])
            P.tt(On[:], On[:], par["rw_gn_w"][:], ALU.mult, r=[On, par["rw_gn_w"]], w=[On])
            P.tt(On[:], On[:], par["rw_gn_b"][:], ALU.add, r=[On, par["rw_gn_b"]], w=[On])
            P.tt(v3(t4), V.rearrange("p (h j) -> p h j", j=64), b3(bsum), ALU.mult, r=[RK, bsum], w=[t4])
            P.tt(On[:], On[:], t4[:], ALU.add, r=[On, t4], w=[On])
            P.tt(Ob[:], On[:], G, ALU.mult, r=[On, LO], w=[Ob])
            psTo = nps()
            for g in range(3):
                P.mm(psTo[:, g * 128:(g + 1) * 128], Ob[:, g * 128:(g + 1) * 128], identb[:], True, True, r=[Ob, identb], w=[psTo])
            ot = OT[(ti // 4) % 2]
            P.cp(ot[:, :, (ti % 4) * 128:(ti % 4 + 1) * 128], psTo[:, 0:384].rearrange("p (g t) -> p g t", t=128), r=[psTo], w=[ot])
            if ti % 4 == 3 or ti == ntiles - 1:
                P.dma("sp", mixT[ti // 4][:, 0:3, :], ot[:], reads=[ot], writes=["mix_rw"])
        P.barrier()
        P.emit()
        P.renew_sems()
    P.es = P.ges


def phase_nsa(P, A, l, S, mixT, ntiles=T // 128):
    GC = 2.0 * math.sqrt(2.0 / math.pi)
    with ExitStack() as es:
        P.es = es

        def cst(name, shape, dt=F32, q="sp", src=None):
            t_ = P.sb(shape, dt, name="n_" + name)
            P.dma(q, t_[:], A[name] if src is None else src, writes=[t_])
            return t_

        identb = cst("ident", [128, 128], BF16, "pool")
        causal = cst("causal", [128, 128])
        winlo = cst("winlo", [128, 128])
        cmask = cst("cmask", [128, 8])
        rowv0 = cst("rowvalid0", [128, 1])
        ovl = P.sb([128, 2, 64], BF16)
        for ch in range(2):
            P.dma("pool", ovl[:, ch, :], A["overlap"][ch], writes=[ovl])
        KA = {}
        for nm in ("ks", "kw"):
            for g in range(2):
                t_ = P.sb([128, T], BF16, name=f"KA{nm}{g}")
                P.memset(t_[:], 0.0, w=[t_], eng="pool")
                P.dma("sp", t_[0:64, :], S[nm][g * 64:(g + 1) * 64, :], writes=[t_])
                P.dma("pool", t_[64:68, :], A["kaug"], writes=[t_])
                KA[(nm, g)] = t_
        vsw = P.sb([128, 32, 256], BF16)
        P.dma("sp", vsw[:], S["vsw"].rearrange("(b p) c -> p b c", p=128), writes=[vsw])
        kcmpA = [P.sb([128, 256], BF16, name=f"kcmpA{g}") for g in range(2)]
        vcmp = P.sb([128, 2, 2, 64], BF16)
        with ExitStack() as es2:
            P.es = es2
            w1 = {}
            w2 = {}
            pe2 = {}
            for kv in ("k", "v"):
                w1[kv] = P.sb([128, 16, 128], BF16, name="w1" + kv)
                P.dma("pool", w1[kv][:], A["cmp_w1_" + kv][l].rearrange("(a p) m -> p a m", p=128), writes=[w1[kv]])
                w2[kv] = P.sb([128, 128], BF16, name="w2" + kv)
                P.dma("pool", w2[kv][:], A["cmp_w2p_" + kv][l], writes=[w2[kv]])
                pe2[kv] = P.sb([128, 16, 64], BF16, name="pe2" + kv)
                P.dma("pool", pe2[kv][:], A["cmp_pe2_" + kv][l], writes=[pe2[kv]])
            kc2 = P.sb([128, T], BF16)
            hg = P.sb([128, 256], BF16)
            hf = P.sb([128, 256], F32)
            ht = P.sb([128, 256], F32)
            bias = P.sb([128, 64], F32)
            psa = P.ps([128, 512], F32, name="pscA")
            psb = P.ps([128, 512], F32, name="pscB")
            psc = P.ps([128, 512], F32, name="pscC")
            P.memset(kc2[:], 0.0, w=[kc2])
            P.memset(hg[:], 0.0, w=[hg])
            for kv in ("k", "v"):
                for a_ in range(16):
                    P.mm(psb[:, 0:64], w1[kv][:, a_, :], pe2[kv][:, a_, :], a_ == 0, a_ == 15, r=[w1[kv], pe2[kv]], w=[psb])
                P.cp(bias[:], psb[:, 0:64], r=[psb], w=[bias])
                for g in range(2):
                    src = S["kc" if kv == "k" else "vc"]
                    P.dma("sp", kc2[0:64, :], src[g * 64:(g + 1) * 64, :], writes=[kc2])
                    P.dma("sp", kc2[64:128, 0:T - 1], src[g * 64:(g + 1) * 64, 1:T], writes=[kc2])
                    for a_ in range(16):
                        P.mm(psa[:, 0:255], w1[kv][:, a_, :], kc2[:, 2 * a_: 2 * a_ + 16 * 254 + 1: 16], a_ == 0, a_ == 15,
                             r=[w1[kv], kc2], w=[psa])
                    P.ts(hf[:, 0:255], psa[:, 0:255], bias[:, 0:1], None, ALU.add, r=[psa, bias], w=[hf])
                    P.tt(ht[:, 0:255], hf[:, 0:255], hf[:, 0:255], ALU.mult, r=[hf], w=[ht])
                    P.ts(ht[:, 0:255], ht[:, 0:255], 0.044715, 1.0, ALU.mult, ALU.add, r=[ht], w=[ht])
                    P.tt(ht[:, 0:255], ht[:, 0:255], hf[:, 0:255], ALU.mult, r=[ht, hf], w=[ht])
                    P.act(ht[:, 0:255], ht[:, 0:255], AF.Sigmoid, scale=GC, r=[ht], w=[ht])
                    P.tt(hg[:, 0:255], ht[:, 0:255], hf[:, 0:255], ALU.mult, r=[ht, hf], w=[hg])
                    if kv == "k":
                        P.mm(psc[:, 0:256], w2[kv][:], hg[:], True, True, r=[w2[kv], hg], w=[psc])
                        P.cp(kcmpA[g][:], psc[:, 0:256], r=[psc], w=[kcmpA[g]])
                        P.dma("pool", kcmpA[g][64:68, :], A["caug"], writes=[kcmpA[g]])
                    else:
                        for ch in range(2):
                            P.mm(psc[:, ch * 64:(ch + 1) * 64], hg[:, ch * 128:(ch + 1) * 128], w2[kv][:, 0:64], True, True,
                                 r=[w2[kv], hg], w=[psc])
                        P.cp(vcmp[:, :, g, :], psc[:, 0:128].rearrange("p (c d) -> p c d", d=64), r=[psc], w=[vcmp])
            P.barrier()
            P.emit()
        P.es = es
        qA = [P.sb([128, 128], BF16, name=f"qA{i}") for i in range(4)]
        for t_ in qA:
            P.memset(t_[:], 0.0, w=[t_], eng="pool")
        gts = [P.sb([128, 18], F32, name=f"ngt{i}") for i in range(2)]
        selb = [P.sb([128, 64], F32, name=f"selb{i}") for i in range(2)]
        scs = [P.sb([128, T], F32, name=f"nsc{i}") for i in range(2)]
        pbs = [P.sb([128, T], BF16, name=f"npb{i}") for i in range(2)]
        pT = [P.sb([128, 4, 128], BF16, name=f"npT{i}") for i in range(2)]
        pn = [P.sb([128, 256], BF16, name=f"pn{i}") for i in range(3)]
        for t_ in pn:
            P.memset(t_[:], 0.0, w=[t_])
        pnT = [P.sb([128, 2, 128], BF16, name=f"pnT{i}") for i in range(3)]
        acc = P.sb([128, 384], F32)
        accb = P.sb([128, 384], BF16)
        OT = [P.sb([128, 3, 512], BF16, name=f"nOT{i}") for i in range(2)]
        sm = lambda n: P.sb([128, 1], F32, name=n)
        rmaxs = [sm(f"rmax{i}") for i in range(2)]
        ssums = [sm(f"ssum{i}") for i in range(2)]
        gss = [sm(f"gs{i}") for i in range(2)]
        sc64 = P.sb([128, 64], F32)
        sc64b = P.sb([128, 64], F32)
        selneg = P.sb([128, 64], F32)
        m8a = P.sb([128, 8], F32)
        m8b = P.sb([128, 8], F32)
        psS = [P.ps([128, 512], F32, name=f"psn{i}") for i in range(3)]
        psTT = [P.ps([128, 512], F32, name=f"psnT{i}") for i in range(2)]
        psO = P.ps([128, 512], F32, name="psnO")
        psI = P.ps([128, 512], F32, name="psnI")
        cnt = {"s": 0, "t": 0, "q": 0, "p": 0, "b": 0}

        def softmax_pv(ncols, Vfn, nblk0, gate_ap, first, bi):
            sc, pb, rmax, ssum, gs = scs[bi], pbs[bi], rmaxs[bi], ssums[bi], gss[bi]
            P.red(rmax[:], sc[:, 0:ncols], ALU.max, r=[sc], w=[rmax])
            P.ts(rmax[:], rmax[:], -1.0, None, ALU.mult, r=[rmax], w=[rmax])
            P.act(pb[:, 0:ncols], sc[:, 0:ncols], AF.Exp, bias=rmax[:], r=[sc, rmax], w=[pb])
            P.red(ssum[:], pb[:, 0:ncols], ALU.add, r=[pb], w=[ssum])
            P.op("dve", lambda e: e.reciprocal(out=ssum[:], in_=ssum[:]), [Prog._k(ssum)], [Prog._k(ssum)])
            P.tt(gs[:], ssum[:], gate_ap, ALU.mult, r=[ssum, "gates"], w=[gs])
            nb = ncols // 128
            for c0 in range(0, nb, 4):
                n4 = min(4, nb - c0)
                pst = psTT[cnt["t"] % 2]
                ptt = pT[cnt["t"] % 2]
                cnt["t"] += 1
                for k in range(n4):
                    P.mm(pst[:, k * 128:(k + 1) * 128], pb[:, (c0 + k) * 128:(c0 + k + 1) * 128], identb[:], True, True,
                         r=[pb, identb], w=[pst])
                P.cp(ptt[:, 0:n4, :], pst[:, 0:n4 * 128].rearrange("p (k q) -> p k q", q=128), r=[pst], w=[ptt],
                     eng="act" if cnt["t"] % 2 else "dve")
                for k in range(n4):
                    P.mm(psO[:, 0:64], ptt[:, k, :], Vfn(nblk0 + c0 + k), c0 + k == 0, c0 + k == nb - 1, r=[ptt, vsw], w=[psO])
            return gs

        for qi in range(ntiles):
            t0 = qi * 128
            G = gts[qi % 2]
            P.dma("sp", G[:], S["gates"][t0:t0 + 128, :], writes=[G, "gates"])
            for g in range(2):
                sbias = selb[g]
                if g == 0:
                    P.dma("sp", selb[0][:], A["selbias"][qi], writes=[selb[0]])
                qts = []
                n0 = 8 * qi
                ncol = min(255, n0 + 7)
                nch = (ncol + 127) // 128
                for r_ in range(3):
                    h = 3 * g + r_
                    qt = qA[cnt["q"] % 4]
                    cnt["q"] += 1
                    qsrc = S[f"q{h // 2}"]
                    P.dma("sp", qt[0:64, :], qsrc[(h % 2) * 64:(h % 2 + 1) * 64, t0:t0 + 128], writes=[qt])
                    P.dma("pool", qt[64:68, :], A["qaug"][h, :, t0:t0 + 128], writes=[qt])
                    qts.append(qt)
                    bi = cnt["b"] % 2
                    cnt["b"] += 1
                    sc, pb, rmax, ssum = scs[bi], pbs[bi], rmaxs[bi], ssums[bi]
                    ps = psS[cnt["s"] % 3]
                    cnt["s"] += 1
                    P.mm(ps[:, 0:ncol], qt[:], kcmpA[g][:, 0:ncol], True, True, r=[qt, kcmpA[g]], w=[ps])
                    P.cp(sc[:, 0:ncol], ps[:, 0:ncol], r=[ps], w=[sc], eng="act")
                    lo = max(0, n0 - 1)
                    mlo = lo - (n0 - 1)
                    P.tt(sc[:, lo:ncol], sc[:, lo:ncol], cmask[:, mlo:mlo + (ncol - lo)], ALU.add, r=[sc, cmask], w=[sc])
                    P.red(rmax[:], sc[:, 0:ncol], ALU.max, r=[sc], w=[rmax])
                    P.ts(rmax[:], rmax[:], -1.0, None, ALU.mult, r=[rmax], w=[rmax])
                    P.act(pb[:, 0:ncol], sc[:, 0:ncol], AF.Exp, bias=rmax[:], r=[sc, rmax], w=[pb])
                    P.red(ssum[:], pb[:, 0:ncol], ALU.add, r=[pb], w=[ssum])
                    P.op("dve", lambda e, ssum=ssum: e.reciprocal(out=ssum[:], in_=ssum[:]), [Prog._k(ssum)], [Prog._k(ssum)])
                    if qi == 0:
                        P.tt(ssum[:], ssum[:], rowv0[:], ALU.mult, r=[ssum, rowv0], w=[ssum])
                    pnr = pn[r_]
                    P.ts(pnr[:, 0:ncol], pb[:, 0:ncol], ssum[:, 0:1], None, ALU.mult, r=[pb, ssum], w=[pnr])
                    pst = psTT[cnt["t"] % 2]
                    cnt["t"] += 1
                    for ch in range(nch):
                        P.mm(pst[:, ch * 128:(ch + 1) * 128], pnr[:, ch * 128:(ch + 1) * 128], identb[:], True, True,
                             r=[pnr, identb], w=[pst])
                    pnt = pnT[r_]
                    P.cp(pnt[:, 0:nch, :], pst[:, 0:nch * 128].rearrange("p (k q) -> p k q", q=128), r=[pst], w=[pnt])
                    for ch in range(nch):
                        P.mm(psO[:, 0:64], pnt[:, ch, :], vcmp[:, ch, g, :], ch == 0, ch == nch - 1, r=[pnt, vcmp], w=[psO])
                        P.mm(psI[:, 0:64], pnt[:, ch, :], ovl[:, ch, :], r_ == 0 and ch == 0, r_ == 2 and ch == nch - 1,
                             r=[pnt, ovl], w=[psI])
                    P.ts(acc[:, h * 64:(h + 1) * 64], psO[:, 0:64], G[:, 3 * h:3 * h + 1], None, ALU.mult, r=[psO, G], w=[acc])
                P.tt(sc64[:], psI[:, 0:64], selb[0][:], ALU.add, r=[psI, selb[0]], w=[sc64])
                P.op("dve", lambda e: e.max(out=m8a[:], in_=sc64[:]), [Prog._k(sc64)], [Prog._k(m8a)])
                P.op("dve", lambda e: e.match_replace(out=sc64b[:], in_to_replace=m8a[:], in_values=sc64[:], imm_value=-3.0e38),
                     [Prog._k(sc64), Prog._k(m8a)], [Prog._k(sc64b)])
                P.op("dve", lambda e: e.max(out=m8b[:], in_=sc64b[:]), [Prog._k(sc64b)], [Prog._k(m8b)])
                P.ts(selneg[:], sc64[:], m8b[:, 7:8], NEG, ALU.is_lt, ALU.mult, r=[sc64, m8b], w=[selneg])
                for r_ in range(3):
                    h = 3 * g + r_
                    qt = qts[r_]
                    bi = cnt["b"] % 2
                    cnt["b"] += 1
                    sc = scs[bi]
                    nk = (qi + 1) * 128
                    for c0 in range(0, nk, 512):
                        wdt = min(512, nk - c0)
                        ps = psS[cnt["s"] % 3]
                        cnt["s"] += 1
                        P.mm(ps[:, 0:wdt], qt[:], KA[("ks", g)][:, c0:c0 + wdt], True, True, r=[qt, KA[("ks", g)]], w=[ps])
                        nj = wdt // 64
                        P.tt(sc[:, c0:c0 + wdt].rearrange("p (j k) -> p j k", k=64), ps[:, 0:wdt].rearrange("p (j k) -> p j k", k=64),
                             selneg[:, c0 // 64:c0 // 64 + nj].unsqueeze(2).to_broadcast([128, nj, 64]), ALU.add,
                             r=[ps, selneg], w=[sc])
                    P.tt(sc[:, nk - 128:nk], sc[:, nk - 128:nk], causal[:], ALU.add, r=[sc, causal], w=[sc])
                    gs_ = softmax_pv(nk, lambda kb, g=g: vsw[:, kb, g * 64:(g + 1) * 64], 0, G[:, 3 * h + 1:3 * h + 2], False, bi)
                    P.stt(acc[:, h * 64:(h + 1) * 64], psO[:, 0:64], gs_[:, 0:1], acc[:, h * 64:(h + 1) * 64], ALU.mult, ALU.add,
                          r=[psO, gs_, acc], w=[acc])
                    bi = cnt["b"] % 2
                    cnt["b"] += 1
                    sc = scs[bi]
                    kb0 = max(0, qi - 4)
                    nk = (qi - kb0 + 1) * 128
                    for c0 in range(0, nk, 512):
                        wdt = min(512, nk - c0)
                        ps = psS[cnt["s"] % 3]
                        cnt["s"] += 1
                        P.mm(ps[:, 0:wdt], qt[:], KA[("kw", g)][:, kb0 * 128 + c0:kb0 * 128 + c0 + wdt], True, True,
                             r=[qt, KA[("kw", g)]], w=[ps])
                        P.cp(sc[:, c0:c0 + wdt], ps[:, 0:wdt], r=[ps], w=[sc], eng="act")
                    P.tt(sc[:, nk - 128:nk], sc[:, nk - 128:nk], causal[:], ALU.add, r=[sc, causal], w=[sc])
                    if qi >= 4:
                        P.tt(sc[:, 0:128], sc[:, 0:128], winlo[:], ALU.add, r=[sc, winlo], w=[sc])
                    gs_ = softmax_pv(nk, lambda kb, g=g: vsw[:, kb, 128 + g * 64:128 + (g + 1) * 64], kb0, G[:, 3 * h + 2:3 * h + 3], False, bi)
                    P.stt(acc[:, h * 64:(h + 1) * 64], psO[:, 0:64], gs_[:, 0:1], acc[:, h * 64:(h + 1) * 64], ALU.mult, ALU.add,
                          r=[psO, gs_, acc], w=[acc])
            P.cp(accb[:], acc[:], r=[acc], w=[accb])
            pst = psTT[cnt["t"] % 2]
            cnt["t"] += 1
            for k in range(3):
                P.mm(pst[:, k * 128:(k + 1) * 128], accb[:, k * 128:(k + 1) * 128], identb[:], True, True, r=[accb, identb], w=[pst])
            ot = OT[(qi // 4) % 2]
            P.cp(ot[:, :, (qi % 4) * 128:(qi % 4 + 1) * 128], pst[:, 0:384].rearrange("p (g t) -> p g t", t=128), r=[pst], w=[ot])
            if qi % 4 == 3 or qi == ntiles - 1:
                P.dma("sp", mixT[qi // 4][:, 3:6, :], ot[:], reads=[ot], writes=["mix_nsa"])
        P.barrier()
        P.emit()
        P.renew_sems()
    P.es = P.ges
```

```python
import math
import numpy as np
from contextlib import ExitStack
import concourse.bass as bass
import concourse.mybir as mybir
from concourse.bass_utils import run_bass_kernel_spmd

F32 = mybir.dt.float32
BF16 = mybir.dt.bfloat16
AF = mybir.ActivationFunctionType
ALU = mybir.AluOpType
AX = mybir.AxisListType

T = 4096
D = 1024
DEPTH = 2
NIN = 2834
DFF = 2816
C_DEC = math.exp(-0.5)
NEG = -1.0e30
NT_DBG = 4
SLOPES = [0.25, 0.0625, 0.015625, 0.00390625, 0.5, 0.125]


class Prog:
    COMPUTE = ("pe", "act", "dve", "pool")
    QUEUES = ("sp", "act", "pool")
    NSLOT = 8

    def __init__(self, nc, es):
        self.nc = nc
        self.ges = es
        self.es = es
        self.ops = {e: [] for e in ("pe", "act", "dve", "pool", "sp")}
        self.sem = {}
        self.cnt = {}
        for e in self.COMPUTE:
            self.sem[e] = es.enter_context(nc.semaphore("s_" + e))
            self.cnt[e] = 0
        self.slots = {}
        for q in self.QUEUES:
            self.slots[q] = [[es.enter_context(nc.semaphore(f"d_{q}{i}")), 0] for i in range(self.NSLOT)]
        self.slot_rr = {q: 0 for q in self.QUEUES}
        self.waited = {e: {} for e in self.ops}
        self.lastw = {}
        self.readers = {}
        self.nsb = 0
        self.ndram = 0

    def sb(self, shape, dt=F32, name=None):
        self.nsb += 1
        return self.es.enter_context(self.nc.sbuf_tensor(f"{name or 'sb'}_{self.nsb}", list(shape), dt))

    def ps(self, shape, dt=F32, name=None):
        self.nsb += 1
        return self.es.enter_context(self.nc.psum_tensor(f"{name or 'ps'}_{self.nsb}", list(shape), dt))

    def _need(self, eng, ev, waits):
        if ev is None:
            return
        sem_key, sem, val, src = ev
        if src == eng and sem_key == src and eng == "pe":
            return
        w = self.waited[eng]
        if w.get(sem_key, 0) >= val:
            return
        w[sem_key] = val
        waits.append((sem, val))

    @staticmethod
    def _k(r):
        if isinstance(r, (str, int)):
            return r
        if isinstance(r, tuple):
            return tuple(Prog._k(x) for x in r)
        return "T:" + str(getattr(r, "name", id(r)))

    def _deps(self, eng, reads, writes):
        waits = []
        for r in reads:
            self._need(eng, self.lastw.get(r), waits)
        for w in writes:
            self._need(eng, self.lastw.get(w), waits)
            for ev in self.readers.get(w, []):
                self._need(eng, ev, waits)
        return waits

    def _commit(self, ev, reads, writes):
        for r in reads:
            lst = self.readers.setdefault(r, [])
            lst.append(ev)
            if len(lst) > 32:
                del lst[0]
        for w in writes:
            self.lastw[w] = ev
            self.readers[w] = []

    def op(self, eng, fn, reads=(), writes=()):
        reads = [self._k(r) for r in reads]
        writes = [self._k(r) for r in writes]
        waits = self._deps(eng, reads, writes)
        self.cnt[eng] += 1
        ev = (eng, self.sem[eng], self.cnt[eng], eng)
        self.ops[eng].append((waits, fn, (self.sem[eng], 1)))
        self._commit(ev, reads, writes)

    def dma(self, q, out, in_, reads=(), writes=(), **kw):
        reads = [self._k(r) for r in reads]
        writes = [self._k(r) for r in writes]
        slots = self.slots[q]
        i = self.slot_rr[q]
        self.slot_rr[q] = (i + 1) % self.NSLOT
        sem, val = slots[i]
        waits = self._deps(q, reads, writes)
        key = f"d_{q}{i}"
        if val > 0 and self.waited[q].get(key, 0) < val:
            self.waited[q][key] = val
            waits.append((sem, val))
        slots[i][1] = val + 16
        ev = (key, sem, val + 16, q)

        def fn(e, out=out, in_=in_, kw=kw):
            return e.dma_start(out=out, in_=in_, **kw)

        self.ops[q].append((waits, fn, (sem, 16)))
        self._commit(ev, reads, writes)

    def barrier(self):
        for eng in self.ops:
            waits = []
            for e in self.COMPUTE:
                if e != eng and self.cnt[e] > self.waited[eng].get(e, 0):
                    self.waited[eng][e] = self.cnt[e]
                    waits.append((self.sem[e], self.cnt[e]))
            for q in self.QUEUES:
                for i, (sem, val) in enumerate(self.slots[q]):
                    key = f"d_{q}{i}"
                    if val > self.waited[eng].get(key, 0):
                        self.waited[eng][key] = val
                        waits.append((sem, val))
            if waits:
                self.ops[eng].append((waits, None, None))
        self.lastw = {}
        self.readers = {}
        self._fresh = True

    def renew_sems(self):
        self.gen = getattr(self, "gen", 0) + 1
        for e in self.COMPUTE:
            self.sem[e] = self.ges.enter_context(self.nc.semaphore(f"s_{e}_{self.gen}"))
            self.cnt[e] = 0
            for eng in self.waited:
                self.waited[eng].pop(e, None)

    def emit(self):
        nc = self.nc
        ops = self.ops
        with nc.Block() as block:
            def play(name):
                def run(e):
                    for waits, fn, inc in ops[name]:
                        for sem, val in waits:
                            e.wait_ge(sem, val)
                        if fn is not None:
                            fn(e).then_inc(inc[0], inc[1])
                return run
            block.tensor(play("pe"))
            block.scalar(play("act"))
            block.vector(play("dve"))
            block.gpsimd(play("pool"))
            block.sync(play("sp"))
        self.ops = {e: [] for e in ops}

    def mm(self, out, lhsT, rhs, start, stop, r=(), w=()):
        self.op("pe", lambda e: e.matmul(out, lhsT=lhsT, rhs=rhs, start=start, stop=stop), r, w)

    def tr(self, out, in_, ident, r=(), w=()):
        self.op("pe", lambda e: e.transpose(out, in_, ident), r, w)

    def tt(self, out, in0, in1, op, r=(), w=(), eng="dve"):
        self.op(eng, lambda e: e.tensor_tensor(out=out, in0=in0, in1=in1, op=op), r, w)

    def ts(self, out, in0, s1, s2, op0, op1=None, r=(), w=(), eng="dve"):
        if op1 is None:
            self.op(eng, lambda e: e.tensor_scalar(out=out, in0=in0, scalar1=s1, scalar2=None, op0=op0), r, w)
        else:
            self.op(eng, lambda e: e.tensor_scalar(out=out, in0=in0, scalar1=s1, scalar2=s2, op0=op0, op1=op1), r, w)

    def stt(self, out, in0, scalar, in1, op0, op1, r=(), w=()):
        self.op("dve", lambda e: e.scalar_tensor_tensor(out=out, in0=in0, scalar=scalar, in1=in1, op0=op0, op1=op1), r, w)

    def act(self, out, in_, func, bias=None, scale=None, accum=None, r=(), w=()):
        kw = {}
        if bias is not None:
            kw["bias"] = bias
        if scale is not None:
            kw["scale"] = scale
        if accum is not None:
            kw["accum_out"] = accum
        self.op("act", lambda e: e.activation(out=out, in_=in_, func=func, **kw), r, w)

    def cp(self, out, in_, r=(), w=(), eng="dve"):
        if eng == "act":
            self.op("act", lambda e: e.activation(out=out, in_=in_, func=AF.Copy), r, w)
        else:
            self.op(eng, lambda e: e.tensor_copy(out=out, in_=in_), r, w)

    def red(self, out, in_, op, r=(), w=(), axis=AX.X):
        self.op("dve", lambda e: e.tensor_reduce(out=out, in_=in_, axis=axis, op=op), r, w)

    def memset(self, ap, val, w=(), eng="dve"):
        self.op(eng, lambda e: e.memset(ap, val), (), w)


def bc(ap, shape):
    return ap.to_broadcast(list(shape))


def host_consts():
    c = {}
    t = np.arange(128)
    same = (t[:, None] // 64) == (t[None, :] // 64)
    c["tri_incl"] = (same & (t[:, None] <= t[None, :])).astype(np.float32)
    c["same"] = same.astype(np.float32)
    ci = np.zeros((128, 2), np.float32)
    ci[:64, 0] = 1
    ci[64:, 1] = 1
    c["chunkind"] = ci
    ci64 = np.zeros((128, 64), np.float32)
    ci64[:, 0:2] = ci
    c["chunkind64"] = ci64
    su = (same & (t[:, None] < t[None, :])).astype(np.float32)
    iu = (same & (t[:, None] <= t[None, :])).astype(np.float32)
    c["mask4"] = np.concatenate([su, iu, su, iu], axis=1)
    c["mask_lt"] = su.T.copy()
    c["ident"] = np.eye(128, dtype=np.float32)
    c["ones"] = np.ones((128, 128), np.float32)
    pos = np.arange(T)
    kaug = np.stack([np.ones(T), np.ones(T), pos // 64, pos % 64]).astype(np.float32)
    c["kaug"] = kaug
    qaug = np.zeros((6, 4, T), np.float32)
    for h, s in enumerate(SLOPES):
        qaug[h, 0] = -s * 64 * (pos // 64)
        qaug[h, 1] = -s * (pos % 64)
        qaug[h, 2] = s * 64
        qaug[h, 3] = s
    c["qaug"] = qaug
    ce = np.arange(256) * 16 + 31
    c["caug"] = np.stack([np.ones(256), np.ones(256), ce // 64, ce % 64]).astype(np.float32)
    p = np.arange(128)
    c["causal"] = np.where(p[None, :] <= p[:, None], 0.0, NEG).astype(np.float32)
    c["winlo"] = np.where(p[None, :] > p[:, None], 0.0, NEG).astype(np.float32)
    m = np.arange(-1, 7)
    c["cmask"] = np.where(16 * m[None, :] + 31 <= p[:, None], 0.0, NEG).astype(np.float32)
    rv = np.ones((128, 1), np.float32)
    rv[:31] = 0
    c["rowvalid0"] = rv
    n = np.arange(256)
    j = np.arange(64)
    ov = ((16 * n[:, None] < 64 * j[None, :] + 64) & (16 * n[:, None] + 31 >= 64 * j[None, :])).astype(np.float32)
    ov[255] = 0
    c["overlap"] = ov.reshape(2, 128, 64)
    sb = np.zeros((32, 128, 64), np.float32)
    for qi in range(32):
        tt = qi * 128 + p
        cur = tt // 64
        forced = (j[None, :] == 0) | (j[None, :] == cur[:, None]) | (j[None, :] == cur[:, None] - 1)
        valid = (64 * j[None, :]) <= tt[:, None]
        sb[qi] = np.where(valid, 1000.0 * forced, NEG)
    c["selbias"] = sb
    return c


CONST_SHAPES = None


def _dram_in(nc, name, arr_shape, dt=F32):
    return nc.dram_tensor(name, list(arr_shape), dt, kind="ExternalInput").ap()


def load_bcast(P, q, dst, src_row, n):
    P.dma(q, dst, src_row.unsqueeze(0).to_broadcast([128, n]), writes=[dst.tensor])


def rms_stats(P, sq_bf, nchunk, ones_bf, ps, rstd, ntok, eps=1e-6, dim=D):
    for c in range(nchunk):
        P.mm(ps[:, :ntok], ones_bf[:], sq_bf[:, c, :], c == 0, c == nchunk - 1, r=[sq_bf, ones_bf], w=[ps])
    P.ts(rstd[:, :ntok], ps[:, :ntok], 1.0 / dim, eps, ALU.mult, ALU.add, r=[ps], w=[rstd])
    P.act(rstd[:, :ntok], rstd[:, :ntok], AF.Sqrt, r=[rstd], w=[rstd])
    P.op("dve", lambda e: e.reciprocal(out=rstd[:, :ntok], in_=rstd[:, :ntok]), [Prog._k(rstd)], [Prog._k(rstd)])


def phase_ffn(P, A, l, hin, hout):
    nc = P.nc
    TB = 256
    with ExitStack() as es:
        P.es = es
        wup = P.sb([128, 8, 2 * DFF], BF16)
        wdn = P.sb([128, 22, D], BF16)
        ones_bf = P.sb([128, 128], BF16)
        cw = P.sb([128, 3, 44], F32)
        cb = P.sb([128, 44], F32)
        gpre = P.sb([128, 8], F32)
        gpost = P.sb([128, 8], F32)
        P.dma("pool", ones_bf[:], A["ones"], writes=[ones_bf])
        for c in range(8):
            P.dma("pool", wup[:, c, :], A["w_up"][l, c * 128:(c + 1) * 128, :], writes=[wup])
        for c in range(22):
            P.dma("pool", wdn[:, c, :], A["w_down"][l, c * 128:(c + 1) * 128, :], writes=[wdn])
        P.dma("sp", cw[:], A["conv_wT"][l], writes=[cw])
        P.dma("sp", cb[:], A["conv_bT"][l], writes=[cb])
        P.dma("sp", gpre[:], A["pre_ffn_normT"][l], writes=[gpre])
        P.dma("sp", gpost[:], A["post_ffn_normT"][l], writes=[gpost])

        hb512 = P.sb([128, 8, 512], F32)
        sq = P.sb([128, 8, TB], BF16)
        xn = P.sb([128, 8, TB], BF16)
        rstd = P.sb([128, TB], F32)
        hu = [P.sb([128, 2 + TB], F32, name=f"hu{i}") for i in range(4)]
        carry = P.sb([128, 44, 2], F32)
        cv = [P.sb([128, TB], F32, name=f"cv{i}") for i in range(4)]
        tmp = [P.sb([128, TB], F32, name=f"tmpf{i}") for i in range(2)]
        actT = P.sb([128, 22, TB], BF16)
        fo = P.sb([128, 8, TB], F32)
        pss = [P.ps([128, 512], F32, name=f"psf{i}") for i in range(6)]
        psst = P.ps([128, 512], F32, name="psfst")
        P.memset(carry[:], 0.0, w=[carry])
        GC = 2.0 * math.sqrt(2.0 / math.pi)
        pi = 0
        for b in range(T // TB):
            if b % 2 == 0:
                P.dma("sp", hb512[:], hin[b // 2], writes=[hb512])
            hb = hb512[:, :, (b % 2) * TB:(b % 2 + 1) * TB]
            P.act(sq[:], hb, AF.Square, r=[hb512], w=[sq])
            rms_stats(P, sq, 8, ones_bf, psst, rstd, TB)
            for c in range(8):
                P.stt(xn[:, c, :], hb[:, c, :], gpre[:, c:c + 1], rstd[:], ALU.mult, ALU.mult, r=[hb512, rstd, gpre], w=[xn])
            for i in range(22):
                res = []
                for half in range(2):
                    m = i + 22 * half
                    ps = pss[pi % 6]
                    pi += 1
                    for c in range(8):
                        P.mm(ps[:, :TB], wup[:, c, m * 128:(m + 1) * 128], xn[:, c, :], c == 0, c == 7, r=[wup, xn], w=[ps])
                    h = hu[(2 * i + half) % 4]
                    o = cv[(2 * i + half) % 4]
                    P.cp(h[:, 0:2], carry[:, m, :], r=[carry], w=[h], eng="pool")
                    P.act(h[:, 2:2 + TB], ps[:, :TB], AF.Copy, r=[ps], w=[h])
                    P.cp(carry[:, m, :], h[:, TB:TB + 2], r=[h], w=[carry], eng="pool")
                    P.ts(o[:], h[:, 0:TB], cw[:, 0, m:m + 1], cb[:, m:m + 1], ALU.mult, ALU.add, r=[h, cw, cb], w=[o])
                    P.stt(o[:], h[:, 1:1 + TB], cw[:, 1, m:m + 1], o[:], ALU.mult, ALU.add, r=[h, o], w=[o])
                    P.stt(o[:], h[:, 2:2 + TB], cw[:, 2, m:m + 1], o[:], ALU.mult, ALU.add, r=[h, o], w=[o])
                    res.append(o)
                gte, up = res
                t0_, t1_ = tmp
                P.tt(t0_[:], gte[:], gte[:], ALU.mult, r=[gte], w=[t0_], eng="pool")
                P.ts(t0_[:], t0_[:], 0.044715, 1.0, ALU.mult, ALU.add, r=[t0_], w=[t0_], eng="pool")
                P.tt(t0_[:], t0_[:], gte[:], ALU.mult, r=[t0_, gte], w=[t0_], eng="pool")
                P.act(t0_[:], t0_[:], AF.Sigmoid, scale=GC, r=[t0_], w=[t0_])
                P.tt(t1_[:], gte[:], up[:], ALU.mult, r=[gte, up], w=[t1_], eng="pool")
                P.tt(actT[:, i, :], t0_[:], t1_[:], ALU.mult, r=[t0_, t1_], w=[actT])
            for n in range(8):
                ps = pss[pi % 6]
                pi += 1
                for c in range(22):
                    P.mm(ps[:, :TB], wdn[:, c, n * 128:(n + 1) * 128], actT[:, c, :], c == 0, c == 21, r=[wdn, actT], w=[ps])
                P.act(fo[:, n, :], ps[:, :TB], AF.Copy, r=[ps], w=[fo])
            P.act(sq[:], fo[:], AF.Square, r=[fo], w=[sq])
            rms_stats(P, sq, 8, ones_bf, psst, rstd, TB)
            for c in range(8):
                P.stt(fo[:, c, :], fo[:, c, :], gpost[:, c:c + 1], rstd[:], ALU.mult, ALU.mult, r=[fo, rstd, gpost], w=[fo])
            P.tt(hb, hb, fo[:], ALU.add, r=[hb512, fo], w=[hb512])
            if b % 2 == 1:
                P.dma("sp", hout[b // 2], hb512[:], reads=[hb512], writes=["hout"])
        P.barrier()
        P.emit()
        P.renew_sems()
    P.es = P.ges


def phase_ple(P, A, l, hin, hout):
    TB = 512
    with ExitStack() as es:
        P.es = es
        wg = P.sb([128, 8, D], BF16)
        wpl = P.sb([128, 2, D], BF16)
        ones_bf = P.sb([128, 128], BF16)
        gple = P.sb([128, 8], F32)
        P.dma("pool", ones_bf[:], A["ones"], writes=[ones_bf])
        for c in range(8):
            P.dma("pool", wg[:, c, :], A["w_ple_gate"][l, c * 128:(c + 1) * 128, :], writes=[wg])
        for c in range(2):
            P.dma("pool", wpl[:, c, :], A["w_ple"][l, c * 128:(c + 1) * 128, :], writes=[wpl])
        P.dma("sp", gple[:], A["ple_normT"][l], writes=[gple])
        hb = [P.sb([128, 8, TB], F32, name=f"phb{i}") for i in range(2)]
        sq = P.sb([128, 8, TB], BF16)
        rstd = P.sb([128, TB], F32)
        fo = P.sb([128, 8, TB], F32)
        pb = P.sb([128, 2, TB], BF16)
        eo = P.sb([128, 8, TB], F32)
        h2b = P.sb([128, 8, TB], BF16)
        pss = [P.ps([128, 512], F32, name=f"psp{i}") for i in range(6)]
        psst = P.ps([128, 512], F32, name="pspst")
        pi = 0
        for b in range(T // TB):
            h = hb[b % 2]
            P.dma("sp", h[:], hin[b], writes=[h])
            P.dma("pool", pb[:], A["pT"][l, b], writes=[pb])
            P.cp(h2b[:], h[:], r=[h], w=[h2b], eng="act")
            for n in range(8):
                ps = pss[pi % 6]
                pi += 1
                for c in range(2):
                    P.mm(ps[:, :TB], wpl[:, c, n * 128:(n + 1) * 128], pb[:, c, :], c == 0, c == 1, r=[wpl, pb], w=[ps])
                P.act(eo[:, n, :], ps[:, :TB], AF.Copy, r=[ps], w=[eo])
            P.act(sq[:], eo[:], AF.Square, r=[eo], w=[sq])
            rms_stats(P, sq, 8, ones_bf, psst, rstd, TB)
            for c in range(8):
                P.stt(eo[:, c, :], eo[:, c, :], gple[:, c:c + 1], rstd[:], ALU.mult, ALU.mult, r=[eo, rstd, gple], w=[eo])
            for n in range(8):
                ps = pss[pi % 6]
                pi += 1
                for c in range(8):
                    P.mm(ps[:, :TB], wg[:, c, n * 128:(n + 1) * 128], h2b[:, c, :], c == 0, c == 7, r=[wg, h2b], w=[ps])
                P.act(fo[:, n, :], ps[:, :TB], AF.Sigmoid, r=[ps], w=[fo])
            P.tt(eo[:], eo[:], fo[:], ALU.mult, r=[eo, fo], w=[eo])
            P.tt(h[:], h[:], eo[:], ALU.add, r=[h, eo], w=[h])
            P.dma("sp", hout[b], h[:], reads=[h], writes=["hout"])
        P.barrier()
        P.emit()
        P.renew_sems()
    P.es = P.ges


def normT(g):
    L = g.shape[0]
    return np.ascontiguousarray(g.reshape(L, -1, 128).transpose(0, 2, 1))


def prep_shared(inp):
    f = lambda a: np.ascontiguousarray(np.asarray(a, dtype=np.float32))
    S = {}
    S.update(host_consts())
    for k in ("w_in", "w_out", "w_up", "w_down", "w_ple", "w_ple_gate"):
        S[k] = f(inp[k])
    L = DEPTH
    S["conv_wT"] = f(inp["conv_w"].reshape(L, 3, 44, 128).transpose(0, 3, 1, 2))
    S["conv_bT"] = f(inp["conv_b"].reshape(L, 44, 128).transpose(0, 2, 1))
    for k in ("pre_mix_norm", "post_mix_norm", "pre_ffn_norm", "post_ffn_norm", "ple_norm"):
        S[k + "T"] = f(normT(np.asarray(inp[k])))
    for k in ("shift_mu", "rw_w2", "rw_a2", "rw_g2", "rw_w0", "rw_a0", "rw_k_k", "rw_k_a", "rw_r_k", "rw_gn_w", "rw_gn_b"):
        S[k] = f(inp[k])
    z64 = np.zeros((DEPTH, 64, 384), np.float32)
    S["rw_w2p"] = f(np.concatenate([np.asarray(inp["rw_w2"]), z64], axis=1))
    S["rw_a2p"] = f(np.concatenate([z64, np.asarray(inp["rw_a2"])], axis=1))
    prep_s5(inp, S)
    for kv in ("k", "v"):
        S["cmp_w1_" + kv] = f(inp["cmp_w1_" + kv])
        w2 = np.asarray(inp["cmp_w2_" + kv])
        S["cmp_w2p_" + kv] = f(np.concatenate([w2, np.zeros_like(w2)], axis=2))
        pe = np.asarray(inp["cmp_pe_" + kv]).reshape(DEPTH, 16, 128)
        S["cmp_pe2_" + kv] = f(np.repeat(pe.transpose(0, 2, 1)[:, :, :, None], 64, axis=3))
    return S


def build(shared, mode="full"):
    nc = bass.Bass("TRN2", target_bir_lowering=False)
    A = {}
    for k, v in shared.items():
        A[k] = _dram_in(nc, k, v.shape)
    A["xT"] = _dram_in(nc, "xT", [8, 128, 8, 512])
    A["pT"] = _dram_in(nc, "pT", [DEPTH, 8, 128, 2, 512])
    yT = nc.dram_tensor("yT", [8, 128, 8, 512], F32, kind="ExternalOutput").ap()
    dbg = mode != "full" and not mode.startswith("layer")
    kind = "ExternalOutput" if dbg else "Internal"
    scr = {}

    def scratch(name, shape, dt=F32):
        scr[name] = nc.dram_tensor(name, list(shape), dt, kind=kind).ap()
        return scr[name]

    hA = scratch("hA", [8, 128, 8, 512])
    hB = scratch("hB", [8, 128, 8, 512])
    S = {}
    S["rkv"] = scratch("rkv", [T, 1152])
    S["lor"] = scratch("lor", [T, 1152])
    for nm in ("q0", "q1", "q2", "kc", "vc", "ks", "kw"):
        S[nm] = scratch(nm, [128, T], BF16)
    S["vsw"] = scratch("vsw", [T, 256], BF16)
    S["gates"] = scratch("gates", [T, 18])
    S["uT"] = scratch("uT", [256, T])
    mixT = scratch("mixT", [8, 128, 8, 512], BF16)
    if dbg:
        S["dbg"] = scratch("dbg", [128, 128])
        S["dbgO"] = scratch("dbgO", [128, 2400])
    with ExitStack() as es:
        P = Prog(nc, es)
        if mode.startswith("layer"):
            l = int(mode[5:])
            phase_proj(P, A, l, A["xT"], S)
            phase_rwkv(P, A, l, S, mixT)
            phase_nsa(P, A, l, S, mixT)
            phase_s5(P, A, l, S, mixT)
            phase_out(P, A, l, A["xT"], mixT, hA)
            phase_ffn(P, A, l, hA, hB)
            phase_ple(P, A, l, hB, yT)
        if mode == "full" or mode.startswith("ph:"):
            import os
            sel = mode[3:].split(",") if mode.startswith("ph:") else os.environ.get("FULLSEL", "proj,rwkv,nsa,s5,out,ffn,ple").split(",")
            nl = int(os.environ.get("NLAYERS", DEPTH))
            hcur = A["xT"]
            for l in range(nl):
                if "proj" in sel:
                    phase_proj(P, A, l, hcur, S)
                if "rwkv" in sel:
                    phase_rwkv(P, A, l, S, mixT, ntiles=int(os.environ.get("RWT", "32")), stage=int(os.environ.get("STAGE", "9")))
                if "nsa" in sel:
                    phase_nsa(P, A, l, S, mixT, ntiles=int(os.environ.get("NST", "32")))
                if "s5" in sel:
                    phase_s5(P, A, l, S, mixT)
                if "out" in sel:
                    phase_out(P, A, l, hcur, mixT, hA)
                if "ffn" in sel:
                    phase_ffn(P, A, l, hA, hB)
                if "ple" in sel:
                    phase_ple(P, A, l, hB, yT if l == nl - 1 else hA)
                hcur = hA
        if mode == "proj":
            phase_proj(P, A, 0, A["xT"], S)
            phase_s5(P, A, 0, S, mixT)
        if mode == "rwkv":
            phase_proj(P, A, 0, A["xT"], S)
            phase_rwkv(P, A, 0, S, mixT, ntiles=NT_DBG)
        if mode == "rwkv_only":
            import os
            phase_rwkv(P, A, 0, S, mixT, ntiles=1, stage=int(os.environ.get("STAGE", "9")))
        if mode == "s5":
            phase_s5(P, A, 0, S, mixT)
        if mode == "ffn":
            phase_ffn(P, A, 0, A["xT"], hA)
            phase_ple(P, A, 0, hA, yT)
        P.barrier()
        P.emit()
    return nc, list(scr.keys())


def kernel(**inputs):
    return run_layers(inputs)


def run_layers(inputs, cores=8):
    import os
    shared = prep_shared(inputs)
    x = np.asarray(inputs["x"], dtype=np.float32)
    p = np.asarray(inputs["p"], dtype=np.float32)
    hs = [np.ascontiguousarray(x[b].reshape(8, 512, 8, 128).transpose(0, 3, 2, 1)) for b in range(cores)]
    pTs = [np.ascontiguousarray(p[:, b].reshape(DEPTH, 8, 512, 2, 128).transpose(0, 1, 4, 3, 2)) for b in range(cores)]
    for l in range(DEPTH):
        nc, _ = build(shared, "layer%d" % l)
        in_maps = []
        for b in range(cores):
            m = dict(shared)
            m["xT"] = hs[b]
            m["pT"] = pTs[b]
            in_maps.append(m)
        res = run_bass_kernel_spmd(nc, in_maps, core_ids=list(range(cores)))
        hs = [np.ascontiguousarray(r["yT"]) for r in res.results]
    out = np.stack([np.ascontiguousarray(h.transpose(0, 3, 2, 1)).reshape(T, D) for h in hs], axis=0)
    return out.astype(np.float32)


def run(inputs, mode="full", cores=8):
    shared = prep_shared(inputs)
    nc, scr = build(shared, mode)
    x = np.asarray(inputs["x"], dtype=np.float32)
    p = np.asarray(inputs["p"], dtype=np.float32)
    in_maps = []
    import os
    boff = int(os.environ.get("BOFF", "0"))
    for b in range(boff, boff + cores):
        m = dict(shared)
        m["xT"] = np.ascontiguousarray(x[b].reshape(8, 512, 8, 128).transpose(0, 3, 2, 1))
        m["pT"] = np.ascontiguousarray(p[:, b].reshape(DEPTH, 8, 512, 2, 128).transpose(0, 1, 4, 3, 2))
        in_maps.append(m)
    cids = [int(c) for c in os.environ["CIDS"].split(",")] if "CIDS" in os.environ else list(range(cores))
    res = run_bass_kernel_spmd(nc, in_maps, core_ids=cids)
    if mode != "full":
        return res.results
    out = np.stack([np.ascontiguousarray(r["yT"].transpose(0, 3, 2, 1)).reshape(T, D) for r in res.results], axis=0)
    return out.astype(np.float32)


def phase_proj(P, A, l, hin, S):
    TB = 512
    with ExitStack() as es:
        P.es = es
        W1 = P.sb([128, 8, 1408], BF16)
        W2 = P.sb([128, 8, 1408], BF16)
        wn = P.sb([128, 8, 1426], BF16)
        ones_bf = P.sb([128, 128], BF16)
        gpre = P.sb([128, 8], F32)
        mu = P.sb([128, 1408], F32)
        stg = [P.sb([128, 1408], F32, name=f"stg{i}") for i in range(2)]
        w2b = P.sb([128, 384], BF16)
        a2b = P.sb([128, 384], BF16)
        g2b = P.sb([128, 384], BF16)
        P.dma("pool", ones_bf[:], A["ones"], writes=[ones_bf])
        P.dma("sp", gpre[:], A["pre_mix_normT"][l], writes=[gpre])
        load_bcast(P, "sp", mu[:], A["shift_mu"][l], 1408)
        P.dma("pool", w2b[:], A["rw_w2p"][l], writes=[w2b])
        P.dma("pool", a2b[:], A["rw_a2p"][l], writes=[a2b])
        P.dma("pool", g2b[:], A["rw_g2"][l], writes=[g2b])
        for c in range(8):
            P.dma("pool", wn[:, c, :], A["w_in"][l, c * 128:(c + 1) * 128, 1408:2834], writes=[wn])
            st = stg[c % 2]
            P.dma("sp", st[:], A["w_in"][l, c * 128:(c + 1) * 128, 0:1408], writes=[st])
            P.tt(W2[:, c, :], st[:], mu[:], ALU.mult, r=[st, mu], w=[W2])
            P.tt(W1[:, c, :], st[:], W2[:, c, :], ALU.subtract, r=[st, W2], w=[W1], eng="pool")

        hb = P.sb([128, 8, TB], F32)
        sq = P.sb([128, 8, TB], BF16)
        xn = P.sb([128, 8, 1 + TB], BF16)
        rstd = P.sb([128, TB], F32)
        rkv = [P.sb([128, 1152], F32, name=f"rkv{i}") for i in range(2)]
        lor = [P.sb([128, 1152], F32, name=f"lor{i}") for i in range(2)]
        twa = P.sb([128, TB], BF16)
        tg = P.sb([128, TB], BF16)
        fmo = [P.sb([128, TB], BF16, name=f"fmo{i}") for i in range(3)]
        uo = [P.sb([128, TB], F32, name=f"uo{i}") for i in range(2)]
        vsw = [P.sb([128, 256], BF16, name=f"vsw{i}") for i in range(2)]
        gts = [P.sb([128, 18], F32, name=f"gts{i}") for i in range(2)]
        pss = [P.ps([128, 512], F32, name=f"psj{i}") for i in range(6)]
        psst = P.ps([128, 512], F32, name="psjst")
        P.memset(xn[:], 0.0, w=[xn])
        pi = 0

        def shifted_group(ps_ap, cols, tok_lo, tok_n, fm):
            for c in range(8):
                for sh, W in ((1, W1), (0, W2)):
                    xs = xn[:, c, sh + tok_lo: sh + tok_lo + tok_n]
                    ws = W[:, c, cols]
                    first = (c == 0 and sh == 1)
                    last = (c == 7 and sh == 0)
                    if fm:
                        P.mm(ps_ap, ws, xs, first, last, r=[W1, W2, xn], w=[ps_ap.tensor])
                    else:
                        P.mm(ps_ap, xs, ws, first, last, r=[W1, W2, xn], w=[ps_ap.tensor])

        for b in range(T // TB):
            t0 = b * TB
            ts_ = slice(t0, t0 + TB)
            P.dma("sp", hb[:], hin[b], writes=[hb])
            P.act(sq[:], hb[:], AF.Square, r=[hb], w=[sq])
            rms_stats(P, sq, 8, ones_bf, psst, rstd, TB)
            if b > 0:
                P.cp(xn[:, :, 0:1], xn[:, :, TB:TB + 1], r=[xn], w=[xn])
            for c in range(8):
                P.stt(xn[:, c, 1:1 + TB], hb[:, c, :], gpre[:, c:c + 1], rstd[:], ALU.mult, ALU.mult, r=[hb, rstd, gpre], w=[xn])
            ps = pss[pi % 6]
            pi += 1
            shifted_group(ps[:, :], slice(1152, 1280), 0, TB, True)
            P.act(twa[0:64, :], ps[0:64, :], AF.Tanh, r=[ps], w=[twa])
            P.act(twa[64:128, :], ps[64:128, :], AF.Copy, r=[ps], w=[twa])
            ps = pss[pi % 6]
            pi += 1
            shifted_group(ps[:, :], slice(1280, 1408), 0, TB, True)
            P.act(tg[:, :], ps[:, :], AF.Sigmoid, r=[ps], w=[tg])
            for tt in range(4):
                tsl = slice(tt * 128, (tt + 1) * 128)
                rk = rkv[tt % 2]
                for j in range(3):
                    ps = pss[pi % 6]
                    pi += 1
                    shifted_group(ps[:, 0:384], slice(j * 384, (j + 1) * 384), tt * 128, 128, False)
                    if j == 1:
                        P.cp(rk[:, j * 384:(j + 1) * 384], ps[:, 0:384], r=[ps], w=[rk])
                    else:
                        P.act(rk[:, j * 384:(j + 1) * 384], ps[:, 0:384], AF.Copy, r=[ps], w=[rk])
                P.dma("sp", S["rkv"][t0 + tt * 128: t0 + (tt + 1) * 128, :], rk[:], reads=[rk], writes=["rkv_d"])
                lo = lor[tt % 2]
                for j, (src, wgt) in enumerate(((twa, w2b), (twa, a2b), (tg, g2b))):
                    ps = pss[pi % 6]
                    pi += 1
                    P.mm(ps[:, 0:384], src[:, tsl], wgt[:, :], True, True, r=[src, wgt], w=[ps])
                    P.cp(lo[:, j * 384:(j + 1) * 384], ps[:, 0:384], r=[ps], w=[lo])
                P.dma("sp", S["lor"][t0 + tt * 128: t0 + (tt + 1) * 128, :], lo[:], reads=[lo], writes=["lor_d"])
                ps = pss[pi % 6]
                pi += 1
                for (c0, n, o0) in ((2176 - 1408, 128, 0), (2432 - 1408, 128, 128), (2560 - 1408, 18, 256)):
                    for c in range(8):
                        P.mm(ps[:, o0:o0 + n], xn[:, c, 1 + tt * 128: 1 + (tt + 1) * 128], wn[:, c, c0:c0 + n], c == 0, c == 7,
                             r=[xn, wn], w=[ps])
                vv = vsw[tt % 2]
                gg = gts[tt % 2]
                P.cp(vv[:], ps[:, 0:256], r=[ps], w=[vv])
                P.act(gg[:], ps[:, 256:274], AF.Sigmoid, r=[ps], w=[gg])
                P.dma("sp", S["vsw"][t0 + tt * 128: t0 + (tt + 1) * 128, :], vv[:], reads=[vv], writes=["vsw_d"])
                P.dma("sp", S["gates"][t0 + tt * 128: t0 + (tt + 1) * 128, :], gg[:], reads=[gg], writes=["gates_d"])
            for k, (c0, dname, scale) in enumerate(((0, "q0", 0.125), (128, "q1", 0.125), (256, "q2", 0.125),
                                                    (1792 - 1408, "kc", 1.0), (1920 - 1408, "vc", 1.0),
                                                    (2048 - 1408, "ks", 1.0), (2304 - 1408, "kw", 1.0))):
                ps = pss[pi % 6]
                pi += 1
                for c in range(8):
                    P.mm(ps[:, :], wn[:, c, c0:c0 + 128], xn[:, c, 1:1 + TB], c == 0, c == 7, r=[wn, xn], w=[ps])
                o = fmo[k % 3]
                P.act(o[:], ps[:], AF.Copy, scale=scale, r=[ps], w=[o])
                P.dma("sp", S[dname][:, ts_], o[:], reads=[o], writes=[dname + "_d"])
            for k in range(2):
                ps = pss[pi % 6]
                pi += 1
                c0 = 2578 - 1408 + k * 128
                for c in range(8):
                    P.mm(ps[:, :], wn[:, c, c0:c0 + 128], xn[:, c, 1:1 + TB], c == 0, c == 7, r=[wn, xn], w=[ps])
                o = uo[k]
                P.cp(o[:], ps[:], r=[ps], w=[o])
                P.dma("sp", S["uT"][k * 128:(k + 1) * 128, ts_], o[:], reads=[o], writes=["uT_d"])
        P.barrier()
        P.emit()
        P.renew_sems()
    P.es = P.ges


def phase_out(P, A, l, hin, mixT, hout):
    TB = 512
    with ExitStack() as es:
        P.es = es
        wo = P.sb([128, 8, D], BF16)
        ones_bf = P.sb([128, 128], BF16)
        gpost = P.sb([128, 8], F32)
        P.dma("pool", ones_bf[:], A["ones"], writes=[ones_bf])
        P.dma("sp", gpost[:], A["post_mix_normT"][l], writes=[gpost])
        for c in range(8):
            P.dma("pool", wo[:, c, :], A["w_out"][l, c * 128:(c + 1) * 128, :], writes=[wo])
        hb = [P.sb([128, 8, TB], F32, name=f"ohb{i}") for i in range(2)]
        mx = [P.sb([128, 8, TB], BF16, name=f"omx{i}") for i in range(2)]
        fo = P.sb([128, 8, TB], F32)
        sq = P.sb([128, 8, TB], BF16)
        rstd = P.sb([128, TB], F32)
        pss = [P.ps([128, 512], F32, name=f"pso{i}") for i in range(6)]
        psst = P.ps([128, 512], F32, name="psost")
        pi = 0
        for b in range(T // TB):
            ts_ = slice(b * TB, (b + 1) * TB)
            h = hb[b % 2]
            m = mx[b % 2]
            P.dma("sp", h[:], hin[b], writes=[h])
            P.dma("sp", m[:], mixT[b], writes=[m])
            for n in range(8):
                ps = pss[pi % 6]
                pi += 1
                for c in range(8):
                    P.mm(ps[:], wo[:, c, n * 128:(n + 1) * 128], m[:, c, :], c == 0, c == 7, r=[wo, m], w=[ps])
                P.act(fo[:, n, :], ps[:], AF.Copy, r=[ps], w=[fo])
            P.act(sq[:], fo[:], AF.Square, r=[fo], w=[sq])
            rms_stats(P, sq, 8, ones_bf, psst, rstd, TB)
            for c in range(8):
                P.stt(fo[:, c, :], fo[:, c, :], gpost[:, c:c + 1], rstd[:], ALU.mult, ALU.mult, r=[fo, rstd, gpost], w=[fo])
            P.tt(h[:], h[:], fo[:], ALU.add, r=[h, fo], w=[h])
            P.dma("sp", hout[b], h[:], reads=[h], writes=["hout"])
        P.barrier()
        P.emit()
        P.renew_sems()
    P.es = P.ges


def prep_s5(inp, S):
    f = lambda a: np.ascontiguousarray(np.asarray(a, dtype=np.float32))
    L = DEPTH
    st = lambda a: f(np.asarray(a).reshape(L, 8, 128).transpose(0, 2, 1))
    S["s5_lam_reT"] = st(inp["s5_lam_re"])
    S["s5_lam_imT"] = st(inp["s5_lam_im"])
    S["s5_logdtT"] = st(np.repeat(np.asarray(inp["s5_log_dt"])[:, :, None], 64, axis=2))
    bre = np.zeros((L, 256, 1024), np.float32)
    bim = np.zeros((L, 256, 1024), np.float32)
    cre = np.zeros((L, 1024, 256), np.float32)
    cim = np.zeros((L, 1024, 256), np.float32)
    for g in range(16):
        bre[:, g * 16:(g + 1) * 16, g * 64:(g + 1) * 64] = np.asarray(inp["s5_b_re"])[:, g].transpose(0, 2, 1)
        bim[:, g * 16:(g + 1) * 16, g * 64:(g + 1) * 64] = np.asarray(inp["s5_b_im"])[:, g].transpose(0, 2, 1)
        cre[:, g * 64:(g + 1) * 64, g * 16:(g + 1) * 16] = np.asarray(inp["s5_c_re"])[:, g].transpose(0, 2, 1)
        cim[:, g * 64:(g + 1) * 64, g * 16:(g + 1) * 16] = np.asarray(inp["s5_c_im"])[:, g].transpose(0, 2, 1)
    S["s5_bre"], S["s5_bim"], S["s5_cre"], S["s5_cim"] = bre, bim, cre, cim
    S["s5_dT"] = f(np.asarray(inp["s5_d"]).reshape(L, 2, 128).transpose(0, 2, 1))
    S["s5_w_glu"] = f(inp["s5_w_glu"])


def phase_s5(P, A, l, S, mixT):
    TB = 512
    NL = 9
    PI = math.pi
    with ExitStack() as es:
        P.es = es
        lre = P.sb([128, 8], F32)
        lim = P.sb([128, 8], F32)
        ldt = P.sb([128, 8], F32)
        bre = P.sb([128, 2, 1024], BF16)
        bim = P.sb([128, 2, 1024], BF16)
        cre = P.sb([128, 8, 256], BF16)
        cim = P.sb([128, 8, 256], BF16)
        dsk = P.sb([128, 2], F32)
        wgl = P.sb([128, 2, 512], BF16)
        P.dma("sp", lre[:], A["s5_lam_reT"][l], writes=[lre])
        P.dma("sp", lim[:], A["s5_lam_imT"][l], writes=[lim])
        P.dma("sp", ldt[:], A["s5_logdtT"][l], writes=[ldt])
        P.dma("sp", dsk[:], A["s5_dT"][l], writes=[dsk])
        for c in range(2):
            P.dma("pool", bre[:, c, :], A["s5_bre"][l, c * 128:(c + 1) * 128, :], writes=[bre])
            P.dma("pool", bim[:, c, :], A["s5_bim"][l, c * 128:(c + 1) * 128, :], writes=[bim])
            P.dma("pool", wgl[:, c, :], A["s5_w_glu"][l, c * 128:(c + 1) * 128, :], writes=[wgl])
        for c in range(8):
            P.dma("pool", cre[:, c, :], A["s5_cre"][l, c * 128:(c + 1) * 128, :], writes=[cre])
            P.dma("pool", cim[:, c, :], A["s5_cim"][l, c * 128:(c + 1) * 128, :], writes=[cim])
        sm = lambda n: P.sb([128, 8], F32, name=n)
        dt, mag, ang, x, acc, tmp = sm("dt"), sm("mag"), sm("ang"), sm("x5"), sm("acc5"), sm("tmp5")
        abr, abi, fre, fim, den, t2 = sm("abr"), sm("abi"), sm("fre"), sm("fim"), sm("den"), sm("t25")
        nfim = sm("nfim")
        P.act(dt[:], ldt[:], AF.Exp, r=[ldt], w=[dt])
        P.tt(mag[:], lre[:], dt[:], ALU.mult, r=[lre, dt], w=[mag])
        P.act(mag[:], mag[:], AF.Exp, r=[mag], w=[mag])
        P.tt(ang[:], lim[:], dt[:], ALU.mult, r=[lim, dt], w=[ang])

        def sin_of(dst, shift):
            P.ts(x[:], ang[:], shift + PI, None, ALU.add, r=[ang], w=[x])
            P.cp(acc[:], x[:], r=[x], w=[acc])
            for k in (1, 2, 3):
                P.ts(tmp[:], x[:], 2 * PI * k, -2 * PI, ALU.is_ge, ALU.mult, r=[x], w=[tmp])
                P.tt(acc[:], acc[:], tmp[:], ALU.add, r=[acc, tmp], w=[acc])
            P.ts(acc[:], acc[:], -PI, None, ALU.add, r=[acc], w=[acc])
            P.ts(tmp[:], acc[:], -1.0, PI, ALU.mult, ALU.add, r=[acc], w=[tmp])
            P.tt(tmp[:], tmp[:], acc[:], ALU.min, r=[tmp, acc], w=[tmp])
            P.ts(acc[:], acc[:], -1.0, -PI, ALU.mult, ALU.add, r=[acc], w=[acc])
            P.tt(acc[:], acc[:], tmp[:], ALU.max, r=[tmp, acc], w=[acc])
            P.tt(t2[:], acc[:], acc[:], ALU.mult, r=[acc], w=[t2])
            P.ts(tmp[:], t2[:], 1.0 / 6227020800.0, None, ALU.mult, r=[t2], w=[tmp])
            for cf in (-1.0 / 39916800.0, 1.0 / 362880.0, -1.0 / 5040.0, 1.0 / 120.0, -1.0 / 6.0):
                P.stt(tmp[:], tmp[:], cf, t2[:], ALU.add, ALU.mult, r=[tmp, t2], w=[tmp])
            P.stt(dst[:], tmp[:], 1.0, acc[:], ALU.add, ALU.mult, r=[tmp, acc], w=[dst])

        sin_of(abi, 0.0)
        sin_of(abr, PI / 2)
        P.tt(abr[:], abr[:], mag[:], ALU.mult, r=[abr, mag], w=[abr])
        P.tt(abi[:], abi[:], mag[:], ALU.mult, r=[abi, mag], w=[abi])
        P.tt(den[:], lre[:], lre[:], ALU.mult, r=[lre], w=[den])
        P.tt(t2[:], lim[:], lim[:], ALU.mult, r=[lim], w=[t2])
        P.tt(den[:], den[:], t2[:], ALU.add, r=[den, t2], w=[den])
        P.op("dve", lambda e: e.reciprocal(out=den[:], in_=den[:]), [Prog._k(den)], [Prog._k(den)])
        P.ts(tmp[:], abr[:], -1.0, None, ALU.add, r=[abr], w=[tmp])
        P.tt(fre[:], tmp[:], lre[:], ALU.mult, r=[tmp, lre], w=[fre])
        P.tt(t2[:], abi[:], lim[:], ALU.mult, r=[abi, lim], w=[t2])
        P.tt(fre[:], fre[:], t2[:], ALU.add, r=[fre, t2], w=[fre])
        P.tt(fre[:], fre[:], den[:], ALU.mult, r=[fre, den], w=[fre])
        P.tt(fim[:], abi[:], lre[:], ALU.mult, r=[abi, lre], w=[fim])
        P.tt(t2[:], tmp[:], lim[:], ALU.mult, r=[tmp, lim], w=[t2])
        P.tt(fim[:], fim[:], t2[:], ALU.subtract, r=[fim, t2], w=[fim])
        P.tt(fim[:], fim[:], den[:], ALU.mult, r=[fim, den], w=[fim])
        P.ts(nfim[:], fim[:], -1.0, None, ALU.mult, r=[fim], w=[nfim])
        pwr = [abr] + [sm(f"pwr{k}") for k in range(1, NL)]
        pwi = [abi] + [sm(f"pwi{k}") for k in range(1, NL)]
        npwi = [sm(f"npwi{k}") for k in range(NL)]
        for k in range(NL):
            P.ts(npwi[k][:], pwi[k][:], -1.0, None, ALU.mult, r=[pwi[k]], w=[npwi[k]])
            if k + 1 < NL:
                P.tt(pwr[k + 1][:], pwr[k][:], pwr[k][:], ALU.mult, r=[pwr[k]], w=[pwr[k + 1]])
                P.tt(t2[:], pwi[k][:], pwi[k][:], ALU.mult, r=[pwi[k]], w=[t2])
                P.tt(pwr[k + 1][:], pwr[k + 1][:], t2[:], ALU.subtract, r=[pwr[k + 1], t2], w=[pwr[k + 1]])
                P.tt(pwi[k + 1][:], pwr[k][:], pwi[k][:], ALU.mult, r=[pwr[k], pwi[k]], w=[pwi[k + 1]])
                P.ts(pwi[k + 1][:], pwi[k + 1][:], 2.0, None, ALU.mult, r=[pwi[k + 1]], w=[pwi[k + 1]])
        if "dbg" in S:
            for i_, t_ in enumerate((dt, mag, ang, abr, abi, fre, fim, den, pwr[NL - 1], pwi[NL - 1])):
                P.dma("sp", S["dbg"][:, i_ * 8:(i_ + 1) * 8], t_[:], reads=[t_])
        cst_re = P.sb([128, 8], F32)
        cst_im = P.sb([128, 8], F32)
        P.memset(cst_re[:], 0.0, w=[cst_re])
        P.memset(cst_im[:], 0.0, w=[cst_im])
        uf = P.sb([128, 2, TB], F32)
        ub = P.sb([128, 2, TB], BF16)
        Are = [P.sb([128, TB], F32, name=f"Are{i}") for i in range(2)]
        Aim = [P.sb([128, TB], F32, name=f"Aim{i}") for i in range(2)]
        sre = P.sb([128, 8, TB], BF16)
        sim = P.sb([128, 8, TB], BF16)
        yv = P.sb([128, 2, TB], F32)
        yt = P.sb([128, TB], F32)
        yg = P.sb([128, 2, TB], BF16)
        gl = P.sb([128, 4, TB], F32)
        ob = P.sb([128, 2, TB], BF16)
        c4 = P.sb([128, 4], F32)
        psx = [P.ps([128, 512], F32, name=f"ps5{i}") for i in range(4)]
        psy = [P.ps([128, 512], F32, name=f"ps5y{i}") for i in range(2)]
        GC = 2.0 * math.sqrt(2.0 / math.pi)
        uT = S["uT"].rearrange("(c p) t -> p c t", p=128)
        for b in range(T // TB):
            ts_ = slice(b * TB, (b + 1) * TB)
            P.dma("sp", uf[:], uT[:, :, ts_], writes=[uf])
            P.cp(ub[:], uf[:], r=[uf], w=[ub], eng="act")
            for m in range(8):
                kc = m // 4
                pr, pim = psx[(2 * m) % 4], psx[(2 * m + 1) % 4]
                P.mm(pr[:], bre[:, kc, m * 128:(m + 1) * 128], ub[:, kc, :], True, True, r=[bre, ub], w=[pr])
                P.mm(pim[:], bim[:, kc, m * 128:(m + 1) * 128], ub[:, kc, :], True, True, r=[bim, ub], w=[pim])
                a_re, a_im = Are[0], Aim[0]
                mc = slice(m, m + 1)
                P.ts(a_re[:], pr[:], fre[:, mc], None, ALU.mult, r=[pr, fre], w=[a_re])
                P.stt(a_re[:], pim[:], nfim[:, mc], a_re[:], ALU.mult, ALU.add, r=[pim, nfim, a_re], w=[a_re])
                P.ts(a_im[:], pim[:], fre[:, mc], None, ALU.mult, r=[pim, fre], w=[a_im])
                P.stt(a_im[:], pr[:], fim[:, mc], a_im[:], ALU.mult, ALU.add, r=[pr, fim, a_im], w=[a_im])
                P.tt(c4[:, 0:1], abr[:, mc], cst_re[:, mc], ALU.mult, r=[abr, cst_re], w=[c4])
                P.tt(c4[:, 1:2], abi[:, mc], cst_im[:, mc], ALU.mult, r=[abi, cst_im], w=[c4])
                P.tt(c4[:, 2:3], abr[:, mc], cst_im[:, mc], ALU.mult, r=[abr, cst_im], w=[c4])
                P.tt(c4[:, 3:4], abi[:, mc], cst_re[:, mc], ALU.mult, r=[abi, cst_re], w=[c4])
                P.tt(a_re[:, 0:1], a_re[:, 0:1], c4[:, 0:1], ALU.add, r=[a_re, c4], w=[a_re])
                P.tt(a_re[:, 0:1], a_re[:, 0:1], c4[:, 1:2], ALU.subtract, r=[a_re, c4], w=[a_re])
                P.tt(a_im[:, 0:1], a_im[:, 0:1], c4[:, 2:3], ALU.add, r=[a_im, c4], w=[a_im])
                P.tt(a_im[:, 0:1], a_im[:, 0:1], c4[:, 3:4], ALU.add, r=[a_im, c4], w=[a_im])
                cur = 0
                for k in range(NL):
                    d = 1 << k
                    sr, si = Are[cur], Aim[cur]
                    dr, di = Are[1 - cur], Aim[1 - cur]
                    P.stt(dr[:, d:], sr[:, :TB - d], pwr[k][:, mc], sr[:, d:], ALU.mult, ALU.add, r=[sr, pwr[k]], w=[dr])
                    P.stt(dr[:, d:], si[:, :TB - d], npwi[k][:, mc], dr[:, d:], ALU.mult, ALU.add, r=[si, npwi[k], dr], w=[dr])
                    P.stt(di[:, d:], si[:, :TB - d], pwr[k][:, mc], si[:, d:], ALU.mult, ALU.add, r=[si, pwr[k]], w=[di])
                    P.stt(di[:, d:], sr[:, :TB - d], pwi[k][:, mc], di[:, d:], ALU.mult, ALU.add, r=[sr, pwi[k], di], w=[di])
                    P.cp(dr[:, :d], sr[:, :d], r=[sr], w=[dr], eng="pool")
                    P.cp(di[:, :d], si[:, :d], r=[si], w=[di], eng="pool")
                    cur = 1 - cur
                fr, fi = Are[cur], Aim[cur]
                P.cp(cst_re[:, mc], fr[:, TB - 1:TB], r=[fr], w=[cst_re])
                P.cp(cst_im[:, mc], fi[:, TB - 1:TB], r=[fi], w=[cst_im])
                P.cp(sre[:, m, :], fr[:], r=[fr], w=[sre], eng="act")
                P.act(sim[:, m, :], fi[:], AF.Copy, scale=-1.0, r=[fi], w=[sim])
                if cur != 0:
                    pass
            for j in range(2):
                py = psy[j]
                n = 0
                for kc in range(4 * j, 4 * j + 4):
                    P.mm(py[:], cre[:, kc, j * 128:(j + 1) * 128], sre[:, kc, :], n == 0, False, r=[cre, sre], w=[py])
                    n += 1
                    P.mm(py[:], cim[:, kc, j * 128:(j + 1) * 128], sim[:, kc, :], False, kc == 4 * j + 3, r=[cim, sim], w=[py])
                P.stt(yv[:, j, :], uf[:, j, :], dsk[:, j:j + 1], py[:], ALU.mult, ALU.add, r=[uf, dsk, py], w=[yv])
                P.tt(yt[:], yv[:, j, :], yv[:, j, :], ALU.mult, r=[yv], w=[yt])
                P.ts(yt[:], yt[:], 0.044715, 1.0, ALU.mult, ALU.add, r=[yt], w=[yt])
                P.tt(yt[:], yt[:], yv[:, j, :], ALU.mult, r=[yt, yv], w=[yt])
                P.act(yt[:], yt[:], AF.Sigmoid, scale=GC, r=[yt], w=[yt])
                P.tt(yg[:, j, :], yt[:], yv[:, j, :], ALU.mult, r=[yt, yv], w=[yg])
            for n in range(4):
                pg = psx[n]
                for kc in range(2):
                    P.mm(pg[:], wgl[:, kc, n * 128:(n + 1) * 128], yg[:, kc, :], kc == 0, kc == 1, r=[wgl, yg], w=[pg])
                if n < 2:
                    P.cp(gl[:, n, :], pg[:], r=[pg], w=[gl])
                else:
                    P.act(gl[:, n, :], pg[:], AF.Sigmoid, r=[pg], w=[gl])
            P.tt(ob[:], gl[:, 0:2, :], gl[:, 2:4, :], ALU.mult, r=[gl], w=[ob])
            P.dma("sp", mixT[b][:, 6:8, :], ob[:], reads=[ob], writes=["mix_s5"])
        P.barrier()
        P.emit()
        P.renew_sems()
    P.es = P.ges


def phase_rwkv(P, A, l, S, mixT, ntiles=T // 128, stage=9):
    C = C_DEC
    with ExitStack() as es:
        P.es = es

        def cst(name, shape):
            t_ = P.sb(shape, F32, name="c_" + name)
            P.dma("sp", t_[:], A[name], writes=[t_])
            return t_

        tri = cst("tri_incl", [128, 128])
        same = cst("same", [128, 128])
        cind = cst("chunkind", [128, 2])
        cind64 = cst("chunkind64", [128, 64])
        mask4 = cst("mask4", [128, 512])
        masklt = cst("mask_lt", [128, 128])
        ident = cst("ident", [128, 128])
        identb = P.sb([128, 128], BF16)
        P.dma("pool", identb[:], A["ident"], writes=[identb])
        par = {}
        for nm in ("rw_w0", "rw_a0", "rw_k_k", "rw_k_a", "rw_r_k", "rw_gn_w", "rw_gn_b"):
            t_ = P.sb([128, 384], F32, name="p_" + nm)
            load_bcast(P, "sp", t_[:], A[nm][l], 384)
            par[nm] = t_
        ST = P.sb([128, 3, 2, 64], F32)
        P.memset(ST[:], 0.0, w=[ST])
        fmz = [[P.sb([128, 512], F32, name=f"fmz{g}{e}") for e in range(2)] for g in range(3)]
        Bdz = [P.sb([128, 384], F32, name=f"Bdz{c}") for c in range(2)]
        Kdz = [P.sb([128, 384], F32, name=f"Kdz{c}") for c in range(2)]
        rkvb = [P.sb([128, 1152], F32, name=f"rkvb{i}") for i in range(2)]
        lorb = [P.sb([128, 1152], F32, name=f"lorb{i}") for i in range(2)]
        w = lambda n: P.sb([128, 384], F32, name="w_" + n)
        sig, a, kk, kp, cs, tmpx, tmpe, Ep, Em, Ex, Eend, ka, Bd, Kd, t4, On = [w(n) for n in (
            "sig", "a", "kk", "kp", "cs", "tmpx", "tmpe", "Ep", "Em", "Ex", "Eend", "ka", "Bd", "Kd", "t4", "On")]
        Q4 = P.sb([128, 4, 384], F32)
        fm = [P.sb([128, 512], F32, name=f"fm{g}") for g in range(3)]
        Gs = P.sb([128, 6, 512], F32)
        MA = [P.sb([128, 6, 128], F32, name=f"MA{i}") for i in range(2)]
        MTA = [P.sb([128, 6, 128], F32, name=f"MTA{i}") for i in range(2)]
        Rall = P.sb([128, 6, 128], F32)
        XT = P.sb([128, 384], F32)
        WT = P.sb([128, 384], F32)
        P.memset(XT[:], 0.0, w=[XT])
        P.memset(WT[:], 0.0, w=[WT])
        O = P.sb([128, 384], F32)
        Ob = P.sb([128, 384], BF16)
        OT = [P.sb([128, 3, 512], BF16, name=f"OT{i}") for i in range(2)]
        ss = P.sb([128, 6], F32)
        bsum = P.sb([128, 6], F32)
        s1 = P.sb([128, 6], F32)
        s2 = P.sb([128, 6], F32)
        m2 = P.sb([128, 6], F32)
        pcs = P.sb([128, 6], F32)
        pool_ps = [P.ps([128, 512], F32, name=f"psr{i}") for i in range(5)]
        psM_fixed = [P.ps([128, 512], F32, name=f"psrM{i}") for i in range(2)]
        rr = [0]

        def nps():
            p_ = pool_ps[rr[0] % 5]
            rr[0] += 1
            return p_

        v3 = lambda t_: t_[:].rearrange("p (h j) -> p h j", j=64)
        b3 = lambda t_: t_[:].unsqueeze(2).to_broadcast([128, 6, 64])
        recip = lambda t_: P.op("dve", lambda e: e.reciprocal(out=t_[:], in_=t_[:]), [Prog._k(t_)], [Prog._k(t_)])

        for ti in range(ntiles):
            t0 = ti * 128
            RK = rkvb[ti % 2]
            LO = lorb[ti % 2]
            P.dma("sp", RK[:], S["rkv"][t0:t0 + 128, :], writes=[RK])
            P.dma("sp", LO[:], S["lor"][t0:t0 + 128, :], writes=[LO])
            R, Kx, V = RK[:, 0:384], RK[:, 384:768], RK[:, 768:1152]
            XW, XA, G = LO[:, 0:384], LO[:, 384:768], LO[:, 768:1152]
            P.tt(sig[:], XW, par["rw_w0"][:], ALU.add, r=[LO, par["rw_w0"]], w=[sig])
            P.act(sig[:], sig[:], AF.Sigmoid, r=[sig], w=[sig])
            P.tt(a[:], XA, par["rw_a0"][:], ALU.add, r=[LO, par["rw_a0"]], w=[a])
            P.act(a[:], a[:], AF.Sigmoid, r=[a], w=[a])
            P.tt(kk[:], Kx, par["rw_k_k"][:], ALU.mult, r=[RK, par["rw_k_k"]], w=[kk])
            P.tt(t4[:], kk[:], kk[:], ALU.mult, r=[kk], w=[t4])
            P.red(ss[:], v3(t4), ALU.add, r=[t4], w=[ss])
            P.ts(ss[:], ss[:], 1e-24, None, ALU.max, r=[ss], w=[ss])
            P.act(ss[:], ss[:], AF.Sqrt, r=[ss], w=[ss])
            recip(ss)
            P.tt(v3(kk), v3(kk), b3(ss), ALU.mult, r=[kk, ss], w=[kk])
            P.stt(t4[:], a[:], -1.0, par["rw_k_a"][:], ALU.add, ALU.mult, r=[a, par["rw_k_a"]], w=[t4])
            P.stt(kp[:], t4[:], 1.0, Kx, ALU.add, ALU.mult, r=[t4, RK], w=[kp])
            psA, psB = nps(), nps()
            P.mm(psA[:, 0:384], tri[:], sig[:], True, True, r=[tri, sig], w=[psA])
            P.mm(psB[:, 0:384], same[:], sig[:], True, True, r=[same, sig], w=[psB])
            P.cp(cs[:], psA[:, 0:384], r=[psA], w=[cs], eng="act")
            P.act(Ep[:], cs[:], AF.Exp, scale=-C, r=[cs], w=[Ep])
            P.act(Em[:], cs[:], AF.Exp, scale=C, r=[cs], w=[Em])
            P.tt(tmpx[:], cs[:], sig[:], ALU.subtract, r=[cs, sig], w=[tmpx])
            P.act(Ex[:], tmpx[:], AF.Exp, scale=-C, r=[tmpx], w=[Ex])
            P.tt(tmpe[:], psB[:, 0:384], cs[:], ALU.subtract, r=[psB, cs], w=[tmpe])
            P.act(Eend[:], tmpe[:], AF.Exp, scale=-C, r=[tmpe], w=[Eend])
            P.tt(ka[:], kk[:], a[:], ALU.mult, r=[kk, a], w=[ka])
            P.stt(Q4[:, 0, :], kk[:], -1.0, Ex[:], ALU.mult, ALU.mult, r=[kk, Ex], w=[Q4])
            P.tt(Q4[:, 1, :], R, Ep[:], ALU.mult, r=[RK, Ep], w=[Q4])
            P.tt(Q4[:, 2, :], ka[:], Em[:], ALU.mult, r=[ka, Em], w=[Q4])
            P.tt(Q4[:, 3, :], kp[:], Em[:], ALU.mult, r=[kp, Em], w=[Q4])
            P.tt(Bd[:], ka[:], Eend[:], ALU.mult, r=[ka, Eend], w=[Bd])
            P.tt(Kd[:], kp[:], Eend[:], ALU.mult, r=[kp, Eend], w=[Kd])
            for c_ in range(2):
                P.ts(Bdz[c_][:], Bd[:], cind[:, c_:c_ + 1], None, ALU.mult, r=[Bd, cind], w=[Bdz[c_]], eng="pool")
                P.ts(Kdz[c_][:], Kd[:], cind[:, c_:c_ + 1], None, ALU.mult, r=[Kd, cind], w=[Kdz[c_]], eng="pool")
            P.tt(t4[:], R, kp[:], ALU.mult, r=[RK, kp], w=[t4])
            P.tt(t4[:], t4[:], par["rw_r_k"][:], ALU.mult, r=[t4, par["rw_r_k"]], w=[t4])
            P.red(bsum[:], v3(t4), ALU.add, r=[t4], w=[bsum])
            if stage <= 1:
                continue
            psP = nps()
            for g in range(3):
                psT = nps()
                for q in range(4):
                    P.mm(psT[:, q * 128:(q + 1) * 128], Q4[:, q, g * 128:(g + 1) * 128], ident[:], True, True, r=[Q4, ident], w=[psT])
                P.cp(fm[g][:], psT[:], r=[psT], w=[fm[g]], eng="act" if g % 2 else "dve")
                for e_ in range(2):
                    P.ts(fmz[g][e_][:], fm[g][:], cind[:, e_:e_ + 1], None, ALU.mult, r=[fm[g], cind], w=[fmz[g][e_]],
                         eng="pool" if e_ else "dve")
                P.mm(psP[:, g * 64:(g + 1) * 64], sig[:, g * 128:(g + 1) * 128], cind64[:], True, True, r=[sig, cind64], w=[psP])
            P.act(pcs[:].rearrange("p (g c) -> p g c", c=2), psP[:, 0:192].rearrange("p (g c) -> p g c", c=64)[:, :, 0:2], AF.Exp, scale=-C,
                  r=[psP], w=[pcs])
            if stage <= 2:
                continue
            psM = psM_fixed
            for h in range(6):
                g, e_ = h // 2, h % 2
                f_, fz = fm[g], fmz[g][e_]
                psG = nps()
                P.mm(psG[:, 0:256], f_[:, 256:384], fz[:, 0:256], True, True, r=[f_, fz], w=[psG])
                P.mm(psG[:, 256:512], f_[:, 384:512], fz[:, 0:256], True, True, r=[f_, fz], w=[psG])
                P.tt(Gs[:, h, :], psG[:], mask4[:], ALU.mult, r=[psG, mask4], w=[Gs])
                pm = psM[h // 3]
                P.mm(pm[:, (h % 3) * 128:(h % 3 + 1) * 128], f_[:, 0:128], fz[:, 256:384], True, True, r=[f_, fz], w=[pm])
            for half in range(2):
                P.tt(MTA[0][:, 3 * half:3 * half + 3, :], psM[half][:, 0:384].rearrange("p (h u) -> p h u", u=128),
                     masklt[:].unsqueeze(1).to_broadcast([128, 3, 128]), ALU.mult, r=[psM[half], masklt], w=[MTA[0]])
            P.cp(MA[0][:], Gs[:, :, 0:128], r=[Gs], w=[MA[0]], eng="pool")
            P.tt(Rall[:], Gs[:, :, 0:128], ident[:].unsqueeze(1).to_broadcast([128, 6, 128]), ALU.add, r=[Gs, ident], w=[Rall])
            if stage <= 3:
                continue
            cur = 0
            for lvl in range(1, 6):
                last = lvl == 5
                Mc, MTc, Mn, MTn = MA[cur], MTA[cur], MA[1 - cur], MTA[1 - cur]
                for half in range(2):
                    hs = slice(3 * half, 3 * half + 3)
                    psa, psb, psc = nps(), nps(), nps()
                    for hh in range(3):
                        h = 3 * half + hh
                        cs_ = slice(hh * 128, (hh + 1) * 128)
                        if not last:
                            P.mm(psa[:, cs_], MTc[:, h, :], Mc[:, h, :], True, True, r=[MTc, Mc], w=[psa])
                        P.mm(psb[:, cs_], Mc[:, h, :], MTc[:, h, :], True, True, r=[MTc, Mc], w=[psb])
                    if not last:
                        P.cp(Mn[:, hs, :], psa[:, 0:384].rearrange("p (h u) -> p h u", u=128), r=[psa], w=[Mn], eng="act")
                    P.cp(MTn[:, hs, :], psb[:, 0:384].rearrange("p (h u) -> p h u", u=128), r=[psb], w=[MTn])
                    for hh in range(3):
                        h = 3 * half + hh
                        P.mm(psc[:, hh * 128:(hh + 1) * 128], MTn[:, h, :], Rall[:, h, :], True, True, r=[MTn, Rall], w=[psc])
                    P.tt(Rall[:, hs, :], Rall[:, hs, :], psc[:, 0:384].rearrange("p (h u) -> p h u", u=128), ALU.add,
                         r=[Rall, psc], w=[Rall])
                cur = 1 - cur
            if stage <= 4:
                continue
            for cc in range(2):
                cb = cc * 64
                psX, psW, psO, psS = nps(), nps(), nps(), nps()
                for h in range(6):
                    g, e_ = h // 2, h % 2
                    hc = slice(h * 64, (h + 1) * 64)
                    P.mm(psX[:, hc], fm[g][:, 0:128], ST[:, g, e_, :], True, False, r=[fm[g], ST], w=[psX])
                    P.mm(psX[:, hc], Gs[:, h, 256:384], RK[:, 768 + h * 64:768 + (h + 1) * 64], False, True,
                         r=[Gs, RK], w=[psX])
                P.cp(XT[cb:cb + 64, :], psX[cb:cb + 64, 0:384], r=[psX], w=[XT], eng="act")
                for h in range(6):
                    hc = slice(h * 64, (h + 1) * 64)
                    P.mm(psW[:, hc], Rall[:, h, :], XT[:, hc], True, True, r=[Rall, XT], w=[psW])
                P.cp(WT[cb:cb + 64, :], psW[cb:cb + 64, 0:384], r=[psW], w=[WT])
                for h in range(6):
                    g, e_ = h // 2, h % 2
                    hc = slice(h * 64, (h + 1) * 64)
                    Vh = RK[:, 768 + h * 64:768 + (h + 1) * 64]
                    P.mm(psO[:, hc], fm[g][:, 128:256], ST[:, g, e_, :], True, False, r=[fm[g], ST], w=[psO])
                    P.mm(psO[:, hc], Gs[:, h, 128:256], WT[:, hc], False, False, r=[Gs, WT], w=[psO])
                    P.mm(psO[:, hc], Gs[:, h, 384:512], Vh, False, True, r=[Gs, RK], w=[psO])
                    P.mm(psS[:, hc], Bdz[cc][:, g * 128:(g + 1) * 128], WT[:, hc], True, False, r=[Bdz[cc], WT], w=[psS])
                    P.mm(psS[:, hc], Kdz[cc][:, g * 128:(g + 1) * 128], Vh, False, True, r=[Kdz[cc], RK], w=[psS])
                P.cp(O[cb:cb + 64, :], psO[cb:cb + 64, 0:384], r=[psO], w=[O], eng="act")
                for h in range(6):
                    g, e_ = h // 2, h % 2
                    jb = e_ * 64
                    P.stt(ST[jb:jb + 64, g, e_, :], ST[jb:jb + 64, g, e_, :], pcs[jb:jb + 64, 2 * g + cc:2 * g + cc + 1],
                          psS[jb:jb + 64, h * 64:(h + 1) * 64], ALU.mult, ALU.add, r=[ST, pcs, psS], w=[ST])
            if stage <= 5:
                continue
            if "dbgO" in S and ti == 0:
                P.dma("sp", S["dbgO"][:, 0:384], O[:], reads=[O])
                P.dma("sp", S["dbgO"][:, 384:768], XT[:], reads=[XT])
                P.dma("sp", S["dbgO"][:, 768:1152], WT[:], reads=[WT])
                P.dma("sp", S["dbgO"][:, 1152:1664], Gs[:, 0, :], reads=[Gs])
                P.dma("sp", S["dbgO"][:, 1664:1792], Rall[:, 0, :], reads=[Rall])
                P.dma("sp", S["dbgO"][:, 1792:2304], fm[0][:], reads=[fm[0]])
                P.dma("sp", S["dbgO"][:, 2304:2310], pcs[:], reads=[pcs])
            P.red(s1[:], v3(O), ALU.add, r=[O], w=[s1])
            P.tt(t4[:], O[:], O[:], ALU.mult, r=[O], w=[t4])
            P.red(s2[:], v3(t4), ALU.add, r=[t4], w=[s2])
            P.ts(s1[:], s1[:], 1.0 / 64, None, ALU.mult, r=[s1], w=[s1])
            P.ts(s2[:], s2[:], 1.0 / 64, None, ALU.mult, r=[s2], w=[s2])
            P.tt(m2[:], s1[:], s1[:], ALU.mult, r=[s1], w=[m2])
            P.tt(s2[:], s2[:], m2[:], ALU.subtract, r=[s2, m2], w=[s2])
            P.ts(s2[:], s2[:], 64e-5, None, ALU.add, r=[s2], w=[s2])
            P.act(s2[:], s2[:], AF.Sqrt, r=[s2], w=[s2])
            recip(s2)
            P.tt(v3(On), v3(O), b3(s1), ALU.subtract, r=[O, s1], w=[On])
            P.tt(v3(On), v3(On), b3(s2), ALU.mult, r=[On, s2], w=[On])
            P.tt(On[:], On[:], par["rw_gn_w"][:], ALU.mult, r=[On, par["rw_gn_w"]], w=[On])
            P.tt(On[:], On[:], par["rw_gn_b"][:], ALU.add, r=[On, par["rw_gn_b"]], w=[On])
            P.tt(v3(t4), V.rearrange("p (h j) -> p h j", j=64), b3(bsum), ALU.mult, r=[RK, bsum], w=[t4])
            P.tt(On[:], On[:], t4[:], ALU.add, r=[On, t4], w=[On])
            P.tt(Ob[:], On[:], G, ALU.mult, r=[On, LO], w=[Ob])
            psTo = nps()
            for g in range(3):
                P.mm(psTo[:, g * 128:(g + 1) * 128], Ob[:, g * 128:(g + 1) * 128], identb[:], True, True, r=[Ob, identb], w=[psTo])
            ot = OT[(ti // 4) % 2]
            P.cp(ot[:, :, (ti % 4) * 128:(ti % 4 + 1) * 128], psTo[:, 0:384].rearrange("p (g t) -> p g t", t=128), r=[psTo], w=[ot])
            if ti % 4 == 3 or ti == ntiles - 1:
                P.dma("sp", mixT[ti // 4][:, 0:3, :], ot[:], reads=[ot], writes=["mix_rw"])
        P.barrier()
        P.emit()
        P.renew_sems()
    P.es = P.ges


def phase_nsa(P, A, l, S, mixT, ntiles=T // 128):
    GC = 2.0 * math.sqrt(2.0 / math.pi)
    with ExitStack() as es:
        P.es = es

        def cst(name, shape, dt=F32, q="sp", src=None):
            t_ = P.sb(shape, dt, name="n_" + name)
            P.dma(q, t_[:], A[name] if src is None else src, writes=[t_])
            return t_

        identb = cst("ident", [128, 128], BF16, "pool")
        causal = cst("causal", [128, 128])
        winlo = cst("winlo", [128, 128])
        cmask = cst("cmask", [128, 8])
        rowv0 = cst("rowvalid0", [128, 1])
        ovl = P.sb([128, 2, 64], BF16)
        for ch in range(2):
            P.dma("pool", ovl[:, ch, :], A["overlap"][ch], writes=[ovl])
        KA = {}
        for nm in ("ks", "kw"):
            for g in range(2):
                t_ = P.sb([128, T], BF16, name=f"KA{nm}{g}")
                P.memset(t_[:], 0.0, w=[t_], eng="pool")
                P.dma("sp", t_[0:64, :], S[nm][g * 64:(g + 1) * 64, :], writes=[t_])
                P.dma("pool", t_[64:68, :], A["kaug"], writes=[t_])
                KA[(nm, g)] = t_
        vsw = P.sb([128, 32, 256], BF16)
        P.dma("sp", vsw[:], S["vsw"].rearrange("(b p) c -> p b c", p=128), writes=[vsw])
        kcmpA = [P.sb([128, 256], BF16, name=f"kcmpA{g}") for g in range(2)]
        vcmp = P.sb([128, 2, 2, 64], BF16)
        with ExitStack() as es2:
            P.es = es2
            w1 = {}
            w2 = {}
            pe2 = {}
            for kv in ("k", "v"):
                w1[kv] = P.sb([128, 16, 128], BF16, name="w1" + kv)
                P.dma("pool", w1[kv][:], A["cmp_w1_" + kv][l].rearrange("(a p) m -> p a m", p=128), writes=[w1[kv]])
                w2[kv] = P.sb([128, 128], BF16, name="w2" + kv)
                P.dma("pool", w2[kv][:], A["cmp_w2p_" + kv][l], writes=[w2[kv]])
                pe2[kv] = P.sb([128, 16, 64], BF16, name="pe2" + kv)
                P.dma("pool", pe2[kv][:], A["cmp_pe2_" + kv][l], writes=[pe2[kv]])
            kc2 = P.sb([128, T], BF16)
            hg = P.sb([128, 256], BF16)
            hf = P.sb([128, 256], F32)
            ht = P.sb([128, 256], F32)
            bias = P.sb([128, 64], F32)
            psa = P.ps([128, 512], F32, name="pscA")
            psb = P.ps([128, 512], F32, name="pscB")
            psc = P.ps([128, 512], F32, name="pscC")
            P.memset(kc2[:], 0.0, w=[kc2])
            P.memset(hg[:], 0.0, w=[hg])
            for kv in ("k", "v"):
                for a_ in range(16):
                    P.mm(psb[:, 0:64], w1[kv][:, a_, :], pe2[kv][:, a_, :], a_ == 0, a_ == 15, r=[w1[kv], pe2[kv]], w=[psb])
                P.cp(bias[:], psb[:, 0:64], r=[psb], w=[bias])
                for g in range(2):
                    src = S["kc" if kv == "k" else "vc"]
                    P.dma("sp", kc2[0:64, :], src[g * 64:(g + 1) * 64, :], writes=[kc2])
                    P.dma("sp", kc2[64:128, 0:T - 1], src[g * 64:(g + 1) * 64, 1:T], writes=[kc2])
                    for a_ in range(16):
                        P.mm(psa[:, 0:255], w1[kv][:, a_, :], kc2[:, 2 * a_: 2 * a_ + 16 * 254 + 1: 16], a_ == 0, a_ == 15,
                             r=[w1[kv], kc2], w=[psa])
                    P.ts(hf[:, 0:255], psa[:, 0:255], bias[:, 0:1], None, ALU.add, r=[psa, bias], w=[hf])
                    P.tt(ht[:, 0:255], hf[:, 0:255], hf[:, 0:255], ALU.mult, r=[hf], w=[ht])
                    P.ts(ht[:, 0:255], ht[:, 0:255], 0.044715, 1.0, ALU.mult, ALU.add, r=[ht], w=[ht])
                    P.tt(ht[:, 0:255], ht[:, 0:255], hf[:, 0:255], ALU.mult, r=[ht, hf], w=[ht])
                    P.act(ht[:, 0:255], ht[:, 0:255], AF.Sigmoid, scale=GC, r=[ht], w=[ht])
                    P.tt(hg[:, 0:255], ht[:, 0:255], hf[:, 0:255], ALU.mult, r=[ht, hf], w=[hg])
                    if kv == "k":
                        P.mm(psc[:, 0:256], w2[kv][:], hg[:], True, True, r=[w2[kv], hg], w=[psc])
                        P.cp(kcmpA[g][:], psc[:, 0:256], r=[psc], w=[kcmpA[g]])
                        P.dma("pool", kcmpA[g][64:68, :], A["caug"], writes=[kcmpA[g]])
                    else:
                        for ch in range(2):
                            P.mm(psc[:, ch * 64:(ch + 1) * 64], hg[:, ch * 128:(ch + 1) * 128], w2[kv][:, 0:64], True, True,
                                 r=[w2[kv], hg], w=[psc])
                        P.cp(vcmp[:, :, g, :], psc[:, 0:128].rearrange("p (c d) -> p c d", d=64), r=[psc], w=[vcmp])
            P.barrier()
            P.emit()
        P.es = es
        qA = [P.sb([128, 128], BF16, name=f"qA{i}") for i in range(4)]
        for t_ in qA:
            P.memset(t_[:], 0.0, w=[t_], eng="pool")
        gts = [P.sb([128, 18], F32, name=f"ngt{i}") for i in range(2)]
        selb = [P.sb([128, 64], F32, name=f"selb{i}") for i in range(2)]
        scs = [P.sb([128, T], F32, name=f"nsc{i}") for i in range(2)]
        pbs = [P.sb([128, T], BF16, name=f"npb{i}") for i in range(2)]
        pT = [P.sb([128, 4, 128], BF16, name=f"npT{i}") for i in range(2)]
        pn = [P.sb([128, 256], BF16, name=f"pn{i}") for i in range(3)]
        for t_ in pn:
            P.memset(t_[:], 0.0, w=[t_])
        pnT = [P.sb([128, 2, 128], BF16, name=f"pnT{i}") for i in range(3)]
        acc = P.sb([128, 384], F32)
        accb = P.sb([128, 384], BF16)
        OT = [P.sb([128, 3, 512], BF16, name=f"nOT{i}") for i in range(2)]
        sm = lambda n: P.sb([128, 1], F32, name=n)
        rmaxs = [sm(f"rmax{i}") for i in range(2)]
        ssums = [sm(f"ssum{i}") for i in range(2)]
        gss = [sm(f"gs{i}") for i in range(2)]
        sc64 = P.sb([128, 64], F32)
        sc64b = P.sb([128, 64], F32)
        selneg = P.sb([128, 64], F32)
        m8a = P.sb([128, 8], F32)
        m8b = P.sb([128, 8], F32)
        psS = [P.ps([128, 512], F32, name=f"psn{i}") for i in range(3)]
        psTT = [P.ps([128, 512], F32, name=f"psnT{i}") for i in range(2)]
        psOs = [P.ps([128, 512], F32, name=f"psnO{i}") for i in range(2)]
        psO = psOs[0]
        psI = P.ps([128, 512], F32, name="psnI")
        cnt = {"s": 0, "t": 0, "q": 0, "p": 0, "b": 0}

        def softmax_pv(ncols, Vfn, nblk0, gate_ap, first, bi):
            sc, pb, rmax, ssum, gs = scs[bi], pbs[bi], rmaxs[bi], ssums[bi], gss[bi]
            P.red(rmax[:], sc[:, 0:ncols], ALU.max, r=[sc], w=[rmax])
            P.ts(rmax[:], rmax[:], -1.0, None, ALU.mult, r=[rmax], w=[rmax])
            P.act(pb[:, 0:ncols], sc[:, 0:ncols], AF.Exp, bias=rmax[:], r=[sc, rmax], w=[pb])
            P.red(ssum[:], pb[:, 0:ncols], ALU.add, r=[pb], w=[ssum])
            P.op("dve", lambda e: e.reciprocal(out=ssum[:], in_=ssum[:]), [Prog._k(ssum)], [Prog._k(ssum)])
            P.tt(gs[:], ssum[:], gate_ap, ALU.mult, r=[ssum, "gates"], w=[gs])
            nb = ncols // 128
            for c0 in range(0, nb, 4):
                n4 = min(4, nb - c0)
                pst = psTT[cnt["t"] % 2]
                ptt = pT[cnt["t"] % 2]
                cnt["t"] += 1
                for k in range(n4):
                    P.mm(pst[:, k * 128:(k + 1) * 128], pb[:, (c0 + k) * 128:(c0 + k + 1) * 128], identb[:], True, True,
                         r=[pb, identb], w=[pst])
                P.cp(ptt[:, 0:n4, :], pst[:, 0:n4 * 128].rearrange("p (k q) -> p k q", q=128), r=[pst], w=[ptt],
                     eng="act" if cnt["t"] % 2 else "dve")
                for k in range(n4):
                    P.mm(psO[:, 0:64], ptt[:, k, :], Vfn(nblk0 + c0 + k), c0 + k == 0, c0 + k == nb - 1, r=[ptt, vsw], w=[psO])
            return gs

        for qi in range(ntiles):
            t0 = qi * 128
            G = gts[qi % 2]
            P.dma("sp", G[:], S["gates"][t0:t0 + 128, :], writes=[G, "gates"])
            for g in range(2):
                sbias = selb[g]
                if g == 0:
                    P.dma("sp", selb[0][:], A["selbias"][qi], writes=[selb[0]])
                qts = []
                n0 = 8 * qi
                ncol = min(255, n0 + 7)
                nch = (ncol + 127) // 128
                for r_ in range(3):
                    h = 3 * g + r_
                    qt = qA[cnt["q"] % 4]
                    cnt["q"] += 1
                    qsrc = S[f"q{h // 2}"]
                    P.dma("sp", qt[0:64, :], qsrc[(h % 2) * 64:(h % 2 + 1) * 64, t0:t0 + 128], writes=[qt])
                    P.dma("pool", qt[64:68, :], A["qaug"][h, :, t0:t0 + 128], writes=[qt])
                    qts.append(qt)
                    bi = cnt["b"] % 2
                    cnt["b"] += 1
                    sc, pb, rmax, ssum = scs[bi], pbs[bi], rmaxs[bi], ssums[bi]
                    ps = psS[cnt["s"] % 3]
                    cnt["s"] += 1
                    P.mm(ps[:, 0:ncol], qt[:], kcmpA[g][:, 0:ncol], True, True, r=[qt, kcmpA[g]], w=[ps])
                    P.cp(sc[:, 0:ncol], ps[:, 0:ncol], r=[ps], w=[sc], eng="act")
                    lo = max(0, n0 - 1)
                    mlo = lo - (n0 - 1)
                    P.tt(sc[:, lo:ncol], sc[:, lo:ncol], cmask[:, mlo:mlo + (ncol - lo)], ALU.add, r=[sc, cmask], w=[sc])
                    P.red(rmax[:], sc[:, 0:ncol], ALU.max, r=[sc], w=[rmax])
                    P.ts(rmax[:], rmax[:], -1.0, None, ALU.mult, r=[rmax], w=[rmax])
                    P.act(pb[:, 0:ncol], sc[:, 0:ncol], AF.Exp, bias=rmax[:], r=[sc, rmax], w=[pb])
                    P.red(ssum[:], pb[:, 0:ncol], ALU.add, r=[pb], w=[ssum])
                    P.op("dve", lambda e, ssum=ssum: e.reciprocal(out=ssum[:], in_=ssum[:]), [Prog._k(ssum)], [Prog._k(ssum)])
                    if qi == 0:
                        P.tt(ssum[:], ssum[:], rowv0[:], ALU.mult, r=[ssum, rowv0], w=[ssum])
                    pnr = pn[r_]
                    P.ts(pnr[:, 0:ncol], pb[:, 0:ncol], ssum[:, 0:1], None, ALU.mult, r=[pb, ssum], w=[pnr])
                    pst = psTT[cnt["t"] % 2]
                    cnt["t"] += 1
                    for ch in range(nch):
                        P.mm(pst[:, ch * 128:(ch + 1) * 128], pnr[:, ch * 128:(ch + 1) * 128], identb[:], True, True,
                             r=[pnr, identb], w=[pst])
                    pnt = pnT[r_]
                    P.cp(pnt[:, 0:nch, :], pst[:, 0:nch * 128].rearrange("p (k q) -> p k q", q=128), r=[pst], w=[pnt])
                    for ch in range(nch):
                        P.mm(psO[:, 0:64], pnt[:, ch, :], vcmp[:, ch, g, :], ch == 0, ch == nch - 1, r=[pnt, vcmp], w=[psO])
                        P.mm(psI[:, 0:64], pnt[:, ch, :], ovl[:, ch, :], r_ == 0 and ch == 0, r_ == 2 and ch == nch - 1,
                             r=[pnt, ovl], w=[psI])
                    P.ts(acc[:, h * 64:(h + 1) * 64], psO[:, 0:64], G[:, 3 * h:3 * h + 1], None, ALU.mult, r=[psO, G], w=[acc])
                P.tt(sc64[:], psI[:, 0:64], selb[0][:], ALU.add, r=[psI, selb[0]], w=[sc64])
                P.op("dve", lambda e: e.max(out=m8a[:], in_=sc64[:]), [Prog._k(sc64)], [Prog._k(m8a)])
                P.op("dve", lambda e: e.match_replace(out=sc64b[:], in_to_replace=m8a[:], in_values=sc64[:], imm_value=-3.0e38),
                     [Prog._k(sc64), Prog._k(m8a)], [Prog._k(sc64b)])
                P.op("dve", lambda e: e.max(out=m8b[:], in_=sc64b[:]), [Prog._k(sc64b)], [Prog._k(m8b)])
                P.ts(selneg[:], sc64[:], m8b[:, 7:8], NEG, ALU.is_lt, ALU.mult, r=[sc64, m8b], w=[selneg])
                for r_ in range(3):
                    h = 3 * g + r_
                    qt = qts[r_]
                    items = []
                    for bi, br in enumerate(("sel", "win")):
                        kb0 = 0 if br == "sel" else max(0, qi - 4)
                        items.append(dict(br=br, bi=bi, kb0=kb0, nk=(qi - kb0 + 1) * 128, KAt=KA[("ks" if br == "sel" else "kw", g)],
                                          voff=(0 if br == "sel" else 128) + g * 64, gate=G[:, 3 * h + (1 if br == "sel" else 2):3 * h + (2 if br == "sel" else 3)],
                                          sc=scs[bi], pb=pbs[bi], rmax=rmaxs[bi], ssum=ssums[bi], gs=gss[bi], psO=psOs[bi]))
                    for it in items:
                        sc, nk, kb0 = it["sc"], it["nk"], it["kb0"]
                        for c0 in range(0, nk, 512):
                            wdt = min(512, nk - c0)
                            ps = psS[cnt["s"] % 3]
                            cnt["s"] += 1
                            P.mm(ps[:, 0:wdt], qt[:], it["KAt"][:, kb0 * 128 + c0:kb0 * 128 + c0 + wdt], True, True, r=[qt, it["KAt"]], w=[ps])
                            if it["br"] == "sel":
                                nj = wdt // 64
                                P.tt(sc[:, c0:c0 + wdt].rearrange("p (j k) -> p j k", k=64), ps[:, 0:wdt].rearrange("p (j k) -> p j k", k=64),
                                     selneg[:, c0 // 64:c0 // 64 + nj].unsqueeze(2).to_broadcast([128, nj, 64]), ALU.add,
                                     r=[ps, selneg], w=[sc])
                            else:
                                P.cp(sc[:, c0:c0 + wdt], ps[:, 0:wdt], r=[ps], w=[sc], eng="act")
                        P.tt(sc[:, nk - 128:nk], sc[:, nk - 128:nk], causal[:], ALU.add, r=[sc, causal], w=[sc])
                        if it["br"] == "win" and qi >= 4:
                            P.tt(sc[:, 0:128], sc[:, 0:128], winlo[:], ALU.add, r=[sc, winlo], w=[sc])
                    for it in items:
                        P.red(it["rmax"][:], it["sc"][:, 0:it["nk"]], ALU.max, r=[it["sc"]], w=[it["rmax"]])
                        P.ts(it["rmax"][:], it["rmax"][:], -1.0, None, ALU.mult, r=[it["rmax"]], w=[it["rmax"]])
                    for it in items:
                        P.act(it["pb"][:, 0:it["nk"]], it["sc"][:, 0:it["nk"]], AF.Exp, bias=it["rmax"][:], r=[it["sc"], it["rmax"]], w=[it["pb"]])
                    for it in items:
                        ssum, gs = it["ssum"], it["gs"]
                        P.red(ssum[:], it["pb"][:, 0:it["nk"]], ALU.add, r=[it["pb"]], w=[ssum])
                        P.op("dve", lambda e, ssum=ssum: e.reciprocal(out=ssum[:], in_=ssum[:]), [Prog._k(ssum)], [Prog._k(ssum)])
                        P.tt(gs[:], ssum[:], it["gate"], ALU.mult, r=[ssum, "gates"], w=[gs])
                    for it in items:
                        pb, nb, psO_ = it["pb"], it["nk"] // 128, it["psO"]
                        for c0 in range(0, nb, 4):
                            n4 = min(4, nb - c0)
                            pst = psTT[cnt["t"] % 2]
                            ptt = pT[cnt["t"] % 2]
                            cnt["t"] += 1
                            for k in range(n4):
                                P.mm(pst[:, k * 128:(k + 1) * 128], pb[:, (c0 + k) * 128:(c0 + k + 1) * 128], identb[:], True, True,
                                     r=[pb, identb], w=[pst])
                            P.cp(ptt[:, 0:n4, :], pst[:, 0:n4 * 128].rearrange("p (k q) -> p k q", q=128), r=[pst], w=[ptt],
                                 eng="act" if cnt["t"] % 2 else "dve")
                            for k in range(n4):
                                P.mm(psO_[:, 0:64], ptt[:, k, :], vsw[:, it["kb0"] + c0 + k, it["voff"]:it["voff"] + 64],
                                     c0 + k == 0, c0 + k == nb - 1, r=[ptt, vsw], w=[psO_])
                    for it in items:
                        P.stt(acc[:, h * 64:(h + 1) * 64], it["psO"][:, 0:64], it["gs"][:, 0:1], acc[:, h * 64:(h + 1) * 64], ALU.mult, ALU.add,
                              r=[it["psO"], it["gs"], acc], w=[acc])
            P.cp(accb[:], acc[:], r=[acc], w=[accb])
            pst = psTT[cnt["t"] % 2]
            cnt["t"] += 1
            for k in range(3):
                P.mm(pst[:, k * 128:(k + 1) * 128], accb[:, k * 128:(k + 1) * 128], identb[:], True, True, r=[accb, identb], w=[pst])
            ot = OT[(qi // 4) % 2]
            P.cp(ot[:, :, (qi % 4) * 128:(qi % 4 + 1) * 128], pst[:, 0:384].rearrange("p (g t) -> p g t", t=128), r=[pst], w=[ot])
            if qi % 4 == 3 or qi == ntiles - 1:
                P.dma("sp", mixT[qi // 4][:, 3:6, :], ot[:], reads=[ot], writes=["mix_nsa"])
        P.barrier()
        P.emit()
        P.renew_sems()
    P.es = P.ges
```

```python
import math
import numpy as np
from contextlib import ExitStack
import concourse.bass as bass
import concourse.mybir as mybir
from concourse.bass_utils import run_bass_kernel_spmd

F32 = mybir.dt.float32
BF16 = mybir.dt.bfloat16
AF = mybir.ActivationFunctionType
ALU = mybir.AluOpType
AX = mybir.AxisListType

T = 4096
D = 1024
DEPTH = 2
NIN = 2834
DFF = 2816
C_DEC = math.exp(-0.5)
NEG = -1.0e30
NT_DBG = 4
SLOPES = [0.25, 0.0625, 0.015625, 0.00390625, 0.5, 0.125]


class Prog:
    COMPUTE = ("pe", "act", "dve", "pool")
    QUEUES = ("sp", "act", "pool")
    NSLOT = 8

    def __init__(self, nc, es):
        self.nc = nc
        self.ges = es
        self.es = es
        self.ops = {e: [] for e in ("pe", "act", "dve", "pool", "sp")}
        self.sem = {}
        self.cnt = {}
        for e in self.COMPUTE:
            self.sem[e] = es.enter_context(nc.semaphore("s_" + e))
            self.cnt[e] = 0
        self.slots = {}
        for q in self.QUEUES:
            self.slots[q] = [[es.enter_context(nc.semaphore(f"d_{q}{i}")), 0] for i in range(self.NSLOT)]
        self.slot_rr = {q: 0 for q in self.QUEUES}
        self.waited = {e: {} for e in self.ops}
        self.lastw = {}
        self.readers = {}
        self.nsb = 0
        self.ndram = 0

    def sb(self, shape, dt=F32, name=None):
        self.nsb += 1
        return self.es.enter_context(self.nc.sbuf_tensor(f"{name or 'sb'}_{self.nsb}", list(shape), dt))

    def ps(self, shape, dt=F32, name=None):
        self.nsb += 1
        return self.es.enter_context(self.nc.psum_tensor(f"{name or 'ps'}_{self.nsb}", list(shape), dt))

    def _need(self, eng, ev, waits):
        if ev is None:
            return
        sem_key, sem, val, src = ev
        if src == eng and sem_key == src and eng == "pe":
            return
        w = self.waited[eng]
        if w.get(sem_key, 0) >= val:
            return
        w[sem_key] = val
        waits.append((sem, val))

    @staticmethod
    def _k(r):
        if isinstance(r, (str, int)):
            return r
        if isinstance(r, tuple):
            return tuple(Prog._k(x) for x in r)
        return "T:" + str(getattr(r, "name", id(r)))

    def _deps(self, eng, reads, writes):
        waits = []
        for r in reads:
            self._need(eng, self.lastw.get(r), waits)
        for w in writes:
            self._need(eng, self.lastw.get(w), waits)
            for ev in self.readers.get(w, []):
                self._need(eng, ev, waits)
        return waits

    def _commit(self, ev, reads, writes):
        for r in reads:
            lst = self.readers.setdefault(r, [])
            lst.append(ev)
            if len(lst) > 32:
                del lst[0]
        for w in writes:
            self.lastw[w] = ev
            self.readers[w] = []

    def op(self, eng, fn, reads=(), writes=()):
        reads = [self._k(r) for r in reads]
        writes = [self._k(r) for r in writes]
        waits = self._deps(eng, reads, writes)
        self.cnt[eng] += 1
        ev = (eng, self.sem[eng], self.cnt[eng], eng)
        self.ops[eng].append((waits, fn, (self.sem[eng], 1)))
        self._commit(ev, reads, writes)

    def dma(self, q, out, in_, reads=(), writes=(), **kw):
        reads = [self._k(r) for r in reads]
        writes = [self._k(r) for r in writes]
        slots = self.slots[q]
        i = self.slot_rr[q]
        self.slot_rr[q] = (i + 1) % self.NSLOT
        sem, val = slots[i]
        waits = self._deps(q, reads, writes)
        key = f"d_{q}{i}"
        if val > 0 and self.waited[q].get(key, 0) < val:
            self.waited[q][key] = val
            waits.append((sem, val))
        slots[i][1] = val + 16
        ev = (key, sem, val + 16, q)

        def fn(e, out=out, in_=in_, kw=kw):
            return e.dma_start(out=out, in_=in_, **kw)

        self.ops[q].append((waits, fn, (sem, 16)))
        self._commit(ev, reads, writes)

    def barrier(self):
        for eng in self.ops:
            waits = []
            for e in self.COMPUTE:
                if e != eng and self.cnt[e] > self.waited[eng].get(e, 0):
                    self.waited[eng][e] = self.cnt[e]
                    waits.append((self.sem[e], self.cnt[e]))
            for q in self.QUEUES:
                for i, (sem, val) in enumerate(self.slots[q]):
                    key = f"d_{q}{i}"
                    if val > self.waited[eng].get(key, 0):
                        self.waited[eng][key] = val
                        waits.append((sem, val))
            if waits:
                self.ops[eng].append((waits, None, None))
        self.lastw = {}
        self.readers = {}
        self._fresh = True

    def renew_sems(self):
        self.gen = getattr(self, "gen", 0) + 1
        for e in self.COMPUTE:
            self.sem[e] = self.ges.enter_context(self.nc.semaphore(f"s_{e}_{self.gen}"))
            self.cnt[e] = 0
            for eng in self.waited:
                self.waited[eng].pop(e, None)

    def emit(self):
        nc = self.nc
        ops = self.ops
        with nc.Block() as block:
            def play(name):
                def run(e):
                    for waits, fn, inc in ops[name]:
                        for sem, val in waits:
                            e.wait_ge(sem, val)
                        if fn is not None:
                            fn(e).then_inc(inc[0], inc[1])
                return run
            block.tensor(play("pe"))
            block.scalar(play("act"))
            block.vector(play("dve"))
            block.gpsimd(play("pool"))
            block.sync(play("sp"))
        self.ops = {e: [] for e in ops}

    def mm(self, out, lhsT, rhs, start, stop, r=(), w=()):
        self.op("pe", lambda e: e.matmul(out, lhsT=lhsT, rhs=rhs, start=start, stop=stop), r, w)

    def tr(self, out, in_, ident, r=(), w=()):
        self.op("pe", lambda e: e.transpose(out, in_, ident), r, w)

    def tt(self, out, in0, in1, op, r=(), w=(), eng="dve"):
        self.op(eng, lambda e: e.tensor_tensor(out=out, in0=in0, in1=in1, op=op), r, w)

    def ts(self, out, in0, s1, s2, op0, op1=None, r=(), w=(), eng="dve"):
        if op1 is None:
            self.op(eng, lambda e: e.tensor_scalar(out=out, in0=in0, scalar1=s1, scalar2=None, op0=op0), r, w)
        else:
            self.op(eng, lambda e: e.tensor_scalar(out=out, in0=in0, scalar1=s1, scalar2=s2, op0=op0, op1=op1), r, w)

    def stt(self, out, in0, scalar, in1, op0, op1, r=(), w=()):
        self.op("dve", lambda e: e.scalar_tensor_tensor(out=out, in0=in0, scalar=scalar, in1=in1, op0=op0, op1=op1), r, w)

    def act(self, out, in_, func, bias=None, scale=None, accum=None, r=(), w=()):
        kw = {}
        if bias is not None:
            kw["bias"] = bias
        if scale is not None:
            kw["scale"] = scale
        if accum is not None:
            kw["accum_out"] = accum
        self.op("act", lambda e: e.activation(out=out, in_=in_, func=func, **kw), r, w)

    def cp(self, out, in_, r=(), w=(), eng="dve"):
        if eng == "act":
            self.op("act", lambda e: e.activation(out=out, in_=in_, func=AF.Copy), r, w)
        else:
            self.op(eng, lambda e: e.tensor_copy(out=out, in_=in_), r, w)

    def red(self, out, in_, op, r=(), w=(), axis=AX.X):
        self.op("dve", lambda e: e.tensor_reduce(out=out, in_=in_, axis=axis, op=op), r, w)

    def memset(self, ap, val, w=(), eng="dve"):
        self.op(eng, lambda e: e.memset(ap, val), (), w)


def bc(ap, shape):
    return ap.to_broadcast(list(shape))


def host_consts():
    c = {}
    t = np.arange(128)
    same = (t[:, None] // 64) == (t[None, :] // 64)
    c["tri_incl"] = (same & (t[:, None] <= t[None, :])).astype(np.float32)
    c["same"] = same.astype(np.float32)
    ci = np.zeros((128, 2), np.float32)
    ci[:64, 0] = 1
    ci[64:, 1] = 1
    c["chunkind"] = ci
    ci64 = np.zeros((128, 64), np.float32)
    ci64[:, 0:2] = ci
    c["chunkind64"] = ci64
    su = (same & (t[:, None] < t[None, :])).astype(np.float32)
    iu = (same & (t[:, None] <= t[None, :])).astype(np.float32)
    c["mask4"] = np.concatenate([su, iu, su, iu], axis=1)
    c["mask_lt"] = su.T.copy()
    c["ident"] = np.eye(128, dtype=np.float32)
    c["ones"] = np.ones((128, 128), np.float32)
    pos = np.arange(T)
    kaug = np.stack([np.ones(T), np.ones(T), pos // 64, pos % 64]).astype(np.float32)
    c["kaug"] = kaug
    qaug = np.zeros((6, 4, T), np.float32)
    for h, s in enumerate(SLOPES):
        qaug[h, 0] = -s * 64 * (pos // 64)
        qaug[h, 1] = -s * (pos % 64)
        qaug[h, 2] = s * 64
        qaug[h, 3] = s
    c["qaug"] = qaug
    ce = np.arange(256) * 16 + 31
    c["caug"] = np.stack([np.ones(256), np.ones(256), ce // 64, ce % 64]).astype(np.float32)
    p = np.arange(128)
    c["causal"] = np.where(p[None, :] <= p[:, None], 0.0, NEG).astype(np.float32)
    c["winlo"] = np.where(p[None, :] > p[:, None], 0.0, NEG).astype(np.float32)
    m = np.arange(-1, 7)
    c["cmask"] = np.where(16 * m[None, :] + 31 <= p[:, None], 0.0, NEG).astype(np.float32)
    rv = np.ones((128, 1), np.float32)
    rv[:31] = 0
    c["rowvalid0"] = rv
    n = np.arange(256)
    j = np.arange(64)
    ov = ((16 * n[:, None] < 64 * j[None, :] + 64) & (16 * n[:, None] + 31 >= 64 * j[None, :])).astype(np.float32)
    ov[255] = 0
    c["overlap"] = ov.reshape(2, 128, 64)
    sb = np.zeros((32, 128, 64), np.float32)
    for qi in range(32):
        tt = qi * 128 + p
        cur = tt // 64
        forced = (j[None, :] == 0) | (j[None, :] == cur[:, None]) | (j[None, :] == cur[:, None] - 1)
        valid = (64 * j[None, :]) <= tt[:, None]
        sb[qi] = np.where(valid, 1000.0 * forced, NEG)
    c["selbias"] = sb
    return c


CONST_SHAPES = None


def _dram_in(nc, name, arr_shape, dt=F32):
    return nc.dram_tensor(name, list(arr_shape), dt, kind="ExternalInput").ap()


def load_bcast(P, q, dst, src_row, n):
    P.dma(q, dst, src_row.unsqueeze(0).to_broadcast([128, n]), writes=[dst.tensor])


def rms_stats(P, sq_bf, nchunk, ones_bf, ps, rstd, ntok, eps=1e-6, dim=D):
    for c in range(nchunk):
        P.mm(ps[:, :ntok], ones_bf[:], sq_bf[:, c, :], c == 0, c == nchunk - 1, r=[sq_bf, ones_bf], w=[ps])
    P.ts(rstd[:, :ntok], ps[:, :ntok], 1.0 / dim, eps, ALU.mult, ALU.add, r=[ps], w=[rstd])
    P.act(rstd[:, :ntok], rstd[:, :ntok], AF.Sqrt, r=[rstd], w=[rstd])
    P.op("dve", lambda e: e.reciprocal(out=rstd[:, :ntok], in_=rstd[:, :ntok]), [Prog._k(rstd)], [Prog._k(rstd)])


def phase_ffn(P, A, l, hin, hout):
    nc = P.nc
    TB = 256
    with ExitStack() as es:
        P.es = es
        wup = P.sb([128, 8, 2 * DFF], BF16)
        wdn = P.sb([128, 22, D], BF16)
        ones_bf = P.sb([128, 128], BF16)
        cw = P.sb([128, 3, 44], F32)
        cb = P.sb([128, 44], F32)
        gpre = P.sb([128, 8], F32)
        gpost = P.sb([128, 8], F32)
        P.dma("pool", ones_bf[:], A["ones"], writes=[ones_bf])
        for c in range(8):
            P.dma("pool", wup[:, c, :], A["w_up"][l, c * 128:(c + 1) * 128, :], writes=[wup])
        for c in range(22):
            P.dma("pool", wdn[:, c, :], A["w_down"][l, c * 128:(c + 1) * 128, :], writes=[wdn])
        P.dma("sp", cw[:], A["conv_wT"][l], writes=[cw])
        P.dma("sp", cb[:], A["conv_bT"][l], writes=[cb])
        P.dma("sp", gpre[:], A["pre_ffn_normT"][l], writes=[gpre])
        P.dma("sp", gpost[:], A["post_ffn_normT"][l], writes=[gpost])

        hb512 = P.sb([128, 8, 512], F32)
        sq = P.sb([128, 8, TB], BF16)
        xn = P.sb([128, 8, TB], BF16)
        rstd = P.sb([128, TB], F32)
        hu = [P.sb([128, 2 + TB], F32, name=f"hu{i}") for i in range(4)]
        carry = P.sb([128, 44, 2], F32)
        cv = [P.sb([128, TB], F32, name=f"cv{i}") for i in range(4)]
        tmp = [P.sb([128, TB], F32, name=f"tmpf{i}") for i in range(2)]
        actT = P.sb([128, 22, TB], BF16)
        fo = P.sb([128, 8, TB], F32)
        pss = [P.ps([128, 512], F32, name=f"psf{i}") for i in range(6)]
        psst = P.ps([128, 512], F32, name="psfst")
        P.memset(carry[:], 0.0, w=[carry])
        GC = 2.0 * math.sqrt(2.0 / math.pi)
        pi = 0
        for b in range(T // TB):
            if b % 2 == 0:
                P.dma("sp", hb512[:], hin[b // 2], writes=[hb512])
            hb = hb512[:, :, (b % 2) * TB:(b % 2 + 1) * TB]
            P.act(sq[:], hb, AF.Square, r=[hb512], w=[sq])
            rms_stats(P, sq, 8, ones_bf, psst, rstd, TB)
            for c in range(8):
                P.stt(xn[:, c, :], hb[:, c, :], gpre[:, c:c + 1], rstd[:], ALU.mult, ALU.mult, r=[hb512, rstd, gpre], w=[xn])
            for i in range(22):
                res = []
                for half in range(2):
                    m = i + 22 * half
                    ps = pss[pi % 6]
                    pi += 1
                    for c in range(8):
                        P.mm(ps[:, :TB], wup[:, c, m * 128:(m + 1) * 128], xn[:, c, :], c == 0, c == 7, r=[wup, xn], w=[ps])
                    h = hu[(2 * i + half) % 4]
                    o = cv[(2 * i + half) % 4]
                    P.cp(h[:, 0:2], carry[:, m, :], r=[carry], w=[h], eng="pool")
                    P.act(h[:, 2:2 + TB], ps[:, :TB], AF.Copy, r=[ps], w=[h])
                    P.cp(carry[:, m, :], h[:, TB:TB + 2], r=[h], w=[carry], eng="pool")
                    P.ts(o[:], h[:, 0:TB], cw[:, 0, m:m + 1], cb[:, m:m + 1], ALU.mult, ALU.add, r=[h, cw, cb], w=[o])
                    P.stt(o[:], h[:, 1:1 + TB], cw[:, 1, m:m + 1], o[:], ALU.mult, ALU.add, r=[h, o], w=[o])
                    P.stt(o[:], h[:, 2:2 + TB], cw[:, 2, m:m + 1], o[:], ALU.mult, ALU.add, r=[h, o], w=[o])
                    res.append(o)
                gte, up = res
                t0_, t1_ = tmp
                P.tt(t0_[:], gte[:], gte[:], ALU.mult, r=[gte], w=[t0_], eng="pool")
                P.ts(t0_[:], t0_[:], 0.044715, 1.0, ALU.mult, ALU.add, r=[t0_], w=[t0_], eng="pool")
                P.tt(t0_[:], t0_[:], gte[:], ALU.mult, r=[t0_, gte], w=[t0_], eng="pool")
                P.act(t0_[:], t0_[:], AF.Sigmoid, scale=GC, r=[t0_], w=[t0_])
                P.tt(t1_[:], gte[:], up[:], ALU.mult, r=[gte, up], w=[t1_], eng="pool")
                P.tt(actT[:, i, :], t0_[:], t1_[:], ALU.mult, r=[t0_, t1_], w=[actT])
            for n in range(8):
                ps = pss[pi % 6]
                pi += 1
                for c in range(22):
                    P.mm(ps[:, :TB], wdn[:, c, n * 128:(n + 1) * 128], actT[:, c, :], c == 0, c == 21, r=[wdn, actT], w=[ps])
                P.act(fo[:, n, :], ps[:, :TB], AF.Copy, r=[ps], w=[fo])
            P.act(sq[:], fo[:], AF.Square, r=[fo], w=[sq])
            rms_stats(P, sq, 8, ones_bf, psst, rstd, TB)
            for c in range(8):
                P.stt(fo[:, c, :], fo[:, c, :], gpost[:, c:c + 1], rstd[:], ALU.mult, ALU.mult, r=[fo, rstd, gpost], w=[fo])
            P.tt(hb, hb, fo[:], ALU.add, r=[hb512, fo], w=[hb512])
            if b % 2 == 1:
                P.dma("sp", hout[b // 2], hb512[:], reads=[hb512], writes=["hout"])
        P.barrier()
        P.emit()
        P.renew_sems()
    P.es = P.ges


def phase_ple(P, A, l, hin, hout):
    TB = 512
    with ExitStack() as es:
        P.es = es
        wg = P.sb([128, 8, D], BF16)
        wpl = P.sb([128, 2, D], BF16)
        ones_bf = P.sb([128, 128], BF16)
        gple = P.sb([128, 8], F32)
        P.dma("pool", ones_bf[:], A["ones"], writes=[ones_bf])
        for c in range(8):
            P.dma("pool", wg[:, c, :], A["w_ple_gate"][l, c * 128:(c + 1) * 128, :], writes=[wg])
        for c in range(2):
            P.dma("pool", wpl[:, c, :], A["w_ple"][l, c * 128:(c + 1) * 128, :], writes=[wpl])
        P.dma("sp", gple[:], A["ple_normT"][l], writes=[gple])
        hb = [P.sb([128, 8, TB], F32, name=f"phb{i}") for i in range(2)]
        sq = P.sb([128, 8, TB], BF16)
        rstd = P.sb([128, TB], F32)
        fo = P.sb([128, 8, TB], F32)
        pb = P.sb([128, 2, TB], BF16)
        eo = P.sb([128, 8, TB], F32)
        h2b = P.sb([128, 8, TB], BF16)
        pss = [P.ps([128, 512], F32, name=f"psp{i}") for i in range(6)]
        psst = P.ps([128, 512], F32, name="pspst")
        pi = 0
        for b in range(T // TB):
            h = hb[b % 2]
            P.dma("sp", h[:], hin[b], writes=[h])
            P.dma("pool", pb[:], A["pT"][l, b], writes=[pb])
            P.cp(h2b[:], h[:], r=[h], w=[h2b], eng="act")
            for n in range(8):
                ps = pss[pi % 6]
                pi += 1
                for c in range(2):
                    P.mm(ps[:, :TB], wpl[:, c, n * 128:(n + 1) * 128], pb[:, c, :], c == 0, c == 1, r=[wpl, pb], w=[ps])
                P.act(eo[:, n, :], ps[:, :TB], AF.Copy, r=[ps], w=[eo])
            P.act(sq[:], eo[:], AF.Square, r=[eo], w=[sq])
            rms_stats(P, sq, 8, ones_bf, psst, rstd, TB)
            for c in range(8):
                P.stt(eo[:, c, :], eo[:, c, :], gple[:, c:c + 1], rstd[:], ALU.mult, ALU.mult, r=[eo, rstd, gple], w=[eo])
            for n in range(8):
                ps = pss[pi % 6]
                pi += 1
                for c in range(8):
                    P.mm(ps[:, :TB], wg[:, c, n * 128:(n + 1) * 128], h2b[:, c, :], c == 0, c == 7, r=[wg, h2b], w=[ps])
                P.act(fo[:, n, :], ps[:, :TB], AF.Sigmoid, r=[ps], w=[fo])
            P.tt(eo[:], eo[:], fo[:], ALU.mult, r=[eo, fo], w=[eo])
            P.tt(h[:], h[:], eo[:], ALU.add, r=[h, eo], w=[h])
            P.dma("sp", hout[b], h[:], reads=[h], writes=["hout"])
        P.barrier()
        P.emit()
        P.renew_sems()
    P.es = P.ges


def normT(g):
    L = g.shape[0]
    return np.ascontiguousarray(g.reshape(L, -1, 128).transpose(0, 2, 1))


def prep_shared(inp):
    f = lambda a: np.ascontiguousarray(np.asarray(a, dtype=np.float32))
    S = {}
    S.update(host_consts())
    for k in ("w_in", "w_out", "w_up", "w_down", "w_ple", "w_ple_gate"):
        S[k] = f(inp[k])
    L = DEPTH
    S["conv_wT"] = f(inp["conv_w"].reshape(L, 3, 44, 128).transpose(0, 3, 1, 2))
    S["conv_bT"] = f(inp["conv_b"].reshape(L, 44, 128).transpose(0, 2, 1))
    for k in ("pre_mix_norm", "post_mix_norm", "pre_ffn_norm", "post_ffn_norm", "ple_norm"):
        S[k + "T"] = f(normT(np.asarray(inp[k])))
    for k in ("shift_mu", "rw_w2", "rw_a2", "rw_g2", "rw_w0", "rw_a0", "rw_k_k", "rw_k_a", "rw_r_k", "rw_gn_w", "rw_gn_b"):
        S[k] = f(inp[k])
    z64 = np.zeros((DEPTH, 64, 384), np.float32)
    S["rw_w2p"] = f(np.concatenate([np.asarray(inp["rw_w2"]), z64], axis=1))
    S["rw_a2p"] = f(np.concatenate([z64, np.asarray(inp["rw_a2"])], axis=1))
    prep_s5(inp, S)
    for kv in ("k", "v"):
        S["cmp_w1_" + kv] = f(inp["cmp_w1_" + kv])
        w2 = np.asarray(inp["cmp_w2_" + kv])
        S["cmp_w2p_" + kv] = f(np.concatenate([w2, np.zeros_like(w2)], axis=2))
        pe = np.asarray(inp["cmp_pe_" + kv]).reshape(DEPTH, 16, 128)
        S["cmp_pe2_" + kv] = f(np.repeat(pe.transpose(0, 2, 1)[:, :, :, None], 64, axis=3))
    return S


def build(shared, mode="full"):
    nc = bass.Bass("TRN2", target_bir_lowering=False)
    A = {}
    for k, v in shared.items():
        A[k] = _dram_in(nc, k, v.shape)
    A["xT"] = _dram_in(nc, "xT", [8, 128, 8, 512])
    A["pT"] = _dram_in(nc, "pT", [DEPTH, 8, 128, 2, 512])
    yT = nc.dram_tensor("yT", [8, 128, 8, 512], F32, kind="ExternalOutput").ap()
    dbg = mode != "full" and not mode.startswith("layer")
    kind = "ExternalOutput" if dbg else "Internal"
    scr = {}

    def scratch(name, shape, dt=F32):
        scr[name] = nc.dram_tensor(name, list(shape), dt, kind=kind).ap()
        return scr[name]

    hA = scratch("hA", [8, 128, 8, 512])
    hB = scratch("hB", [8, 128, 8, 512])
    S = {}
    S["rkv"] = scratch("rkv", [T, 1152])
    S["lor"] = scratch("lor", [T, 1152])
    for nm in ("q0", "q1", "q2", "kc", "vc", "ks", "kw"):
        S[nm] = scratch(nm, [128, T], BF16)
    S["vsw"] = scratch("vsw", [T, 256], BF16)
    S["gates"] = scratch("gates", [T, 18])
    S["uT"] = scratch("uT", [256, T])
    mixT = scratch("mixT", [8, 128, 8, 512], BF16)
    if dbg:
        S["dbg"] = scratch("dbg", [128, 128])
        S["dbgO"] = scratch("dbgO", [128, 2400])
    with ExitStack() as es:
        P = Prog(nc, es)
        if mode.startswith("layer"):
            l = int(mode[5:])
            phase_proj(P, A, l, A["xT"], S)
            phase_rwkv(P, A, l, S, mixT)
            phase_nsa(P, A, l, S, mixT)
            phase_s5(P, A, l, S, mixT)
            phase_out(P, A, l, A["xT"], mixT, hA)
            phase_ffn(P, A, l, hA, hB)
            phase_ple(P, A, l, hB, yT)
        if mode == "full" or mode.startswith("ph:"):
            import os
            sel = mode[3:].split(",") if mode.startswith("ph:") else os.environ.get("FULLSEL", "proj,rwkv,nsa,s5,out,ffn,ple").split(",")
            nl = int(os.environ.get("NLAYERS", DEPTH))
            hcur = A["xT"]
            for l in range(nl):
                if "proj" in sel:
                    phase_proj(P, A, l, hcur, S)
                if "rwkv" in sel:
                    phase_rwkv(P, A, l, S, mixT, ntiles=int(os.environ.get("RWT", "32")), stage=int(os.environ.get("STAGE", "9")))
                if "nsa" in sel:
                    phase_nsa(P, A, l, S, mixT, ntiles=int(os.environ.get("NST", "32")))
                if "s5" in sel:
                    phase_s5(P, A, l, S, mixT)
                if "out" in sel:
                    phase_out(P, A, l, hcur, mixT, hA)
                if "ffn" in sel:
                    phase_ffn(P, A, l, hA, hB)
                if "ple" in sel:
                    phase_ple(P, A, l, hB, yT if l == nl - 1 else hA)
                hcur = hA
        if mode == "proj":
            phase_proj(P, A, 0, A["xT"], S)
            phase_s5(P, A, 0, S, mixT)
        if mode == "rwkv":
            phase_proj(P, A, 0, A["xT"], S)
            phase_rwkv(P, A, 0, S, mixT, ntiles=NT_DBG)
        if mode == "rwkv_only":
            import os
            phase_rwkv(P, A, 0, S, mixT, ntiles=1, stage=int(os.environ.get("STAGE", "9")))
        if mode == "s5":
            phase_s5(P, A, 0, S, mixT)
        if mode == "ffn":
            phase_ffn(P, A, 0, A["xT"], hA)
            phase_ple(P, A, 0, hA, yT)
        P.barrier()
        P.emit()
    return nc, list(scr.keys())


def kernel(**inputs):
    return run_layers(inputs)


def run_layers(inputs, cores=8):
    import os
    shared = prep_shared(inputs)
    x = np.asarray(inputs["x"], dtype=np.float32)
    p = np.asarray(inputs["p"], dtype=np.float32)
    hs = [np.ascontiguousarray(x[b].reshape(8, 512, 8, 128).transpose(0, 3, 2, 1)) for b in range(cores)]
    pTs = [np.ascontiguousarray(p[:, b].reshape(DEPTH, 8, 512, 2, 128).transpose(0, 1, 4, 3, 2)) for b in range(cores)]
    for l in range(DEPTH):
        nc, _ = build(shared, "layer%d" % l)
        in_maps = []
        for b in range(cores):
            m = dict(shared)
            m["xT"] = hs[b]
            m["pT"] = pTs[b]
            in_maps.append(m)
        res = run_bass_kernel_spmd(nc, in_maps, core_ids=list(range(cores)))
        hs = [np.ascontiguousarray(r["yT"]) for r in res.results]
    out = np.stack([np.ascontiguousarray(h.transpose(0, 3, 2, 1)).reshape(T, D) for h in hs], axis=0)
    return out.astype(np.float32)


def run(inputs, mode="full", cores=8):
    shared = prep_shared(inputs)
    nc, scr = build(shared, mode)
    x = np.asarray(inputs["x"], dtype=np.float32)
    p = np.asarray(inputs["p"], dtype=np.float32)
    in_maps = []
    import os
    boff = int(os.environ.get("BOFF", "0"))
    for b in range(boff, boff + cores):
        m = dict(shared)
        m["xT"] = np.ascontiguousarray(x[b].reshape(8, 512, 8, 128).transpose(0, 3, 2, 1))
        m["pT"] = np.ascontiguousarray(p[:, b].reshape(DEPTH, 8, 512, 2, 128).transpose(0, 1, 4, 3, 2))
        in_maps.append(m)
    cids = [int(c) for c in os.environ["CIDS"].split(",")] if "CIDS" in os.environ else list(range(cores))
    res = run_bass_kernel_spmd(nc, in_maps, core_ids=cids)
    if mode != "full":
        return res.results
    out = np.stack([np.ascontiguousarray(r["yT"].transpose(0, 3, 2, 1)).reshape(T, D) for r in res.results], axis=0)
    return out.astype(np.float32)


def phase_proj(P, A, l, hin, S):
    TB = 512
    with ExitStack() as es:
        P.es = es
        W1 = P.sb([128, 8, 1408], BF16)
        W2 = P.sb([128, 8, 1408], BF16)
        wn = P.sb([128, 8, 1426], BF16)
        ones_bf = P.sb([128, 128], BF16)
        gpre = P.sb([128, 8], F32)
        mu = P.sb([128, 1408], F32)
        stg = [P.sb([128, 1408], F32, name=f"stg{i}") for i in range(2)]
        w2b = P.sb([128, 384], BF16)
        a2b = P.sb([128, 384], BF16)
        g2b = P.sb([128, 384], BF16)
        P.dma("pool", ones_bf[:], A["ones"], writes=[ones_bf])
        P.dma("sp", gpre[:], A["pre_mix_normT"][l], writes=[gpre])
        load_bcast(P, "sp", mu[:], A["shift_mu"][l], 1408)
        P.dma("pool", w2b[:], A["rw_w2p"][l], writes=[w2b])
        P.dma("pool", a2b[:], A["rw_a2p"][l], writes=[a2b])
        P.dma("pool", g2b[:], A["rw_g2"][l], writes=[g2b])
        for c in range(8):
            P.dma("pool", wn[:, c, :], A["w_in"][l, c * 128:(c + 1) * 128, 1408:2834], writes=[wn])
            st = stg[c % 2]
            P.dma("sp", st[:], A["w_in"][l, c * 128:(c + 1) * 128, 0:1408], writes=[st])
            P.tt(W2[:, c, :], st[:], mu[:], ALU.mult, r=[st, mu], w=[W2])
            P.tt(W1[:, c, :], st[:], W2[:, c, :], ALU.subtract, r=[st, W2], w=[W1], eng="pool")

        hb = P.sb([128, 8, TB], F32)
        sq = P.sb([128, 8, TB], BF16)
        xn = P.sb([128, 8, 1 + TB], BF16)
        rstd = P.sb([128, TB], F32)
        rkv = [P.sb([128, 1152], F32, name=f"rkv{i}") for i in range(2)]
        lor = [P.sb([128, 1152], F32, name=f"lor{i}") for i in range(2)]
        twa = P.sb([128, TB], BF16)
        tg = P.sb([128, TB], BF16)
        fmo = [P.sb([128, TB], BF16, name=f"fmo{i}") for i in range(3)]
        uo = [P.sb([128, TB], F32, name=f"uo{i}") for i in range(2)]
        vsw = [P.sb([128, 256], BF16, name=f"vsw{i}") for i in range(2)]
        gts = [P.sb([128, 18], F32, name=f"gts{i}") for i in range(2)]
        pss = [P.ps([128, 512], F32, name=f"psj{i}") for i in range(6)]
        psst = P.ps([128, 512], F32, name="psjst")
        P.memset(xn[:], 0.0, w=[xn])
        pi = 0

        def shifted_group(ps_ap, cols, tok_lo, tok_n, fm):
            for c in range(8):
                for sh, W in ((1, W1), (0, W2)):
                    xs = xn[:, c, sh + tok_lo: sh + tok_lo + tok_n]
                    ws = W[:, c, cols]
                    first = (c == 0 and sh == 1)
                    last = (c == 7 and sh == 0)
                    if fm:
                        P.mm(ps_ap, ws, xs, first, last, r=[W1, W2, xn], w=[ps_ap.tensor])
                    else:
                        P.mm(ps_ap, xs, ws, first, last, r=[W1, W2, xn], w=[ps_ap.tensor])

        for b in range(T // TB):
            t0 = b * TB
            ts_ = slice(t0, t0 + TB)
            P.dma("sp", hb[:], hin[b], writes=[hb])
            P.act(sq[:], hb[:], AF.Square, r=[hb], w=[sq])
            rms_stats(P, sq, 8, ones_bf, psst, rstd, TB)
            if b > 0:
                P.cp(xn[:, :, 0:1], xn[:, :, TB:TB + 1], r=[xn], w=[xn])
            for c in range(8):
                P.stt(xn[:, c, 1:1 + TB], hb[:, c, :], gpre[:, c:c + 1], rstd[:], ALU.mult, ALU.mult, r=[hb, rstd, gpre], w=[xn])
            ps = pss[pi % 6]
            pi += 1
            shifted_group(ps[:, :], slice(1152, 1280), 0, TB, True)
            P.act(twa[0:64, :], ps[0:64, :], AF.Tanh, r=[ps], w=[twa])
            P.act(twa[64:128, :], ps[64:128, :], AF.Copy, r=[ps], w=[twa])
            ps = pss[pi % 6]
            pi += 1
            shifted_group(ps[:, :], slice(1280, 1408), 0, TB, True)
            P.act(tg[:, :], ps[:, :], AF.Sigmoid, r=[ps], w=[tg])
            for tt in range(4):
                tsl = slice(tt * 128, (tt + 1) * 128)
                rk = rkv[tt % 2]
                for j in range(3):
                    ps = pss[pi % 6]
                    pi += 1
                    shifted_group(ps[:, 0:384], slice(j * 384, (j + 1) * 384), tt * 128, 128, False)
                    if j == 1:
                        P.cp(rk[:, j * 384:(j + 1) * 384], ps[:, 0:384], r=[ps], w=[rk])
                    else:
                        P.act(rk[:, j * 384:(j + 1) * 384], ps[:, 0:384], AF.Copy, r=[ps], w=[rk])
                P.dma("sp", S["rkv"][t0 + tt * 128: t0 + (tt + 1) * 128, :], rk[:], reads=[rk], writes=["rkv_d"])
                lo = lor[tt % 2]
                for j, (src, wgt) in enumerate(((twa, w2b), (twa, a2b), (tg, g2b))):
                    ps = pss[pi % 6]
                    pi += 1
                    P.mm(ps[:, 0:384], src[:, tsl], wgt[:, :], True, True, r=[src, wgt], w=[ps])
                    P.cp(lo[:, j * 384:(j + 1) * 384], ps[:, 0:384], r=[ps], w=[lo])
                P.dma("sp", S["lor"][t0 + tt * 128: t0 + (tt + 1) * 128, :], lo[:], reads=[lo], writes=["lor_d"])
                ps = pss[pi % 6]
                pi += 1
                for (c0, n, o0) in ((2176 - 1408, 128, 0), (2432 - 1408, 128, 128), (2560 - 1408, 18, 256)):
                    for c in range(8):
                        P.mm(ps[:, o0:o0 + n], xn[:, c, 1 + tt * 128: 1 + (tt + 1) * 128], wn[:, c, c0:c0 + n], c == 0, c == 7,
                             r=[xn, wn], w=[ps])
                vv = vsw[tt % 2]
                gg = gts[tt % 2]
                P.cp(vv[:], ps[:, 0:256], r=[ps], w=[vv])
                P.act(gg[:], ps[:, 256:274], AF.Sigmoid, r=[ps], w=[gg])
                P.dma("sp", S["vsw"][t0 + tt * 128: t0 + (tt + 1) * 128, :], vv[:], reads=[vv], writes=["vsw_d"])
                P.dma("sp", S["gates"][t0 + tt * 128: t0 + (tt + 1) * 128, :], gg[:], reads=[gg], writes=["gates_d"])
            for k, (c0, dname, scale) in enumerate(((0, "q0", 0.125), (128, "q1", 0.125), (256, "q2", 0.125),
                                                    (1792 - 1408, "kc", 1.0), (1920 - 1408, "vc", 1.0),
                                                    (2048 - 1408, "ks", 1.0), (2304 - 1408, "kw", 1.0))):
                ps = pss[pi % 6]
                pi += 1
                for c in range(8):
                    P.mm(ps[:, :], wn[:, c, c0:c0 + 128], xn[:, c, 1:1 + TB], c == 0, c == 7, r=[wn, xn], w=[ps])
                o = fmo[k % 3]
                P.act(o[:], ps[:], AF.Copy, scale=scale, r=[ps], w=[o])
                P.dma("sp", S[dname][:, ts_], o[:], reads=[o], writes=[dname + "_d"])
            for k in range(2):
                ps = pss[pi % 6]
                pi += 1
                c0 = 2578 - 1408 + k * 128
                for c in range(8):
                    P.mm(ps[:, :], wn[:, c, c0:c0 + 128], xn[:, c, 1:1 + TB], c == 0, c == 7, r=[wn, xn], w=[ps])
                o = uo[k]
                P.cp(o[:], ps[:], r=[ps], w=[o])
                P.dma("sp", S["uT"][k * 128:(k + 1) * 128, ts_], o[:], reads=[o], writes=["uT_d"])
        P.barrier()
        P.emit()
        P.renew_sems()
    P.es = P.ges


def phase_out(P, A, l, hin, mixT, hout):
    TB = 512
    with ExitStack() as es:
        P.es = es
        wo = P.sb([128, 8, D], BF16)
        ones_bf = P.sb([128, 128], BF16)
        gpost = P.sb([128, 8], F32)
        P.dma("pool", ones_bf[:], A["ones"], writes=[ones_bf])
        P.dma("sp", gpost[:], A["post_mix_normT"][l], writes=[gpost])
        for c in range(8):
            P.dma("pool", wo[:, c, :], A["w_out"][l, c * 128:(c + 1) * 128, :], writes=[wo])
        hb = [P.sb([128, 8, TB], F32, name=f"ohb{i}") for i in range(2)]
        mx = [P.sb([128, 8, TB], BF16, name=f"omx{i}") for i in range(2)]
        fo = P.sb([128, 8, TB], F32)
        sq = P.sb([128, 8, TB], BF16)
        rstd = P.sb([128, TB], F32)
        pss = [P.ps([128, 512], F32, name=f"pso{i}") for i in range(6)]
        psst = P.ps([128, 512], F32, name="psost")
        pi = 0
        for b in range(T // TB):
            ts_ = slice(b * TB, (b + 1) * TB)
            h = hb[b % 2]
            m = mx[b % 2]
            P.dma("sp", h[:], hin[b], writes=[h])
            P.dma("sp", m[:], mixT[b], writes=[m])
            for n in range(8):
                ps = pss[pi % 6]
                pi += 1
                for c in range(8):
                    P.mm(ps[:], wo[:, c, n * 128:(n + 1) * 128], m[:, c, :], c == 0, c == 7, r=[wo, m], w=[ps])
                P.act(fo[:, n, :], ps[:], AF.Copy, r=[ps], w=[fo])
            P.act(sq[:], fo[:], AF.Square, r=[fo], w=[sq])
            rms_stats(P, sq, 8, ones_bf, psst, rstd, TB)
            for c in range(8):
                P.stt(fo[:, c, :], fo[:, c, :], gpost[:, c:c + 1], rstd[:], ALU.mult, ALU.mult, r=[fo, rstd, gpost], w=[fo])
            P.tt(h[:], h[:], fo[:], ALU.add, r=[h, fo], w=[h])
            P.dma("sp", hout[b], h[:], reads=[h], writes=["hout"])
        P.barrier()
        P.emit()
        P.renew_sems()
    P.es = P.ges


def prep_s5(inp, S):
    f = lambda a: np.ascontiguousarray(np.asarray(a, dtype=np.float32))
    L = DEPTH
    st = lambda a: f(np.asarray(a).reshape(L, 8, 128).transpose(0, 2, 1))
    S["s5_lam_reT"] = st(inp["s5_lam_re"])
    S["s5_lam_imT"] = st(inp["s5_lam_im"])
    S["s5_logdtT"] = st(np.repeat(np.asarray(inp["s5_log_dt"])[:, :, None], 64, axis=2))
    bre = np.zeros((L, 256, 1024), np.float32)
    bim = np.zeros((L, 256, 1024), np.float32)
    cre = np.zeros((L, 1024, 256), np.float32)
    cim = np.zeros((L, 1024, 256), np.float32)
    for g in range(16):
        bre[:, g * 16:(g + 1) * 16, g * 64:(g + 1) * 64] = np.asarray(inp["s5_b_re"])[:, g].transpose(0, 2, 1)
        bim[:, g * 16:(g + 1) * 16, g * 64:(g + 1) * 64] = np.asarray(inp["s5_b_im"])[:, g].transpose(0, 2, 1)
        cre[:, g * 64:(g + 1) * 64, g * 16:(g + 1) * 16] = np.asarray(inp["s5_c_re"])[:, g].transpose(0, 2, 1)
        cim[:, g * 64:(g + 1) * 64, g * 16:(g + 1) * 16] = np.asarray(inp["s5_c_im"])[:, g].transpose(0, 2, 1)
    S["s5_bre"], S["s5_bim"], S["s5_cre"], S["s5_cim"] = bre, bim, cre, cim
    S["s5_dT"] = f(np.asarray(inp["s5_d"]).reshape(L, 2, 128).transpose(0, 2, 1))
    S["s5_w_glu"] = f(inp["s5_w_glu"])


def phase_s5(P, A, l, S, mixT):
    TB = 512
    NL = 9
    PI = math.pi
    with ExitStack() as es:
        P.es = es
        lre = P.sb([128, 8], F32)
        lim = P.sb([128, 8], F32)
        ldt = P.sb([128, 8], F32)
        bre = P.sb([128, 2, 1024], BF16)
        bim = P.sb([128, 2, 1024], BF16)
        cre = P.sb([128, 8, 256], BF16)
        cim = P.sb([128, 8, 256], BF16)
        dsk = P.sb([128, 2], F32)
        wgl = P.sb([128, 2, 512], BF16)
        P.dma("sp", lre[:], A["s5_lam_reT"][l], writes=[lre])
        P.dma("sp", lim[:], A["s5_lam_imT"][l], writes=[lim])
        P.dma("sp", ldt[:], A["s5_logdtT"][l], writes=[ldt])
        P.dma("sp", dsk[:], A["s5_dT"][l], writes=[dsk])
        for c in range(2):
            P.dma("pool", bre[:, c, :], A["s5_bre"][l, c * 128:(c + 1) * 128, :], writes=[bre])
            P.dma("pool", bim[:, c, :], A["s5_bim"][l, c * 128:(c + 1) * 128, :], writes=[bim])
            P.dma("pool", wgl[:, c, :], A["s5_w_glu"][l, c * 128:(c + 1) * 128, :], writes=[wgl])
        for c in range(8):
            P.dma("pool", cre[:, c, :], A["s5_cre"][l, c * 128:(c + 1) * 128, :], writes=[cre])
            P.dma("pool", cim[:, c, :], A["s5_cim"][l, c * 128:(c + 1) * 128, :], writes=[cim])
        sm = lambda n: P.sb([128, 8], F32, name=n)
        dt, mag, ang, x, acc, tmp = sm("dt"), sm("mag"), sm("ang"), sm("x5"), sm("acc5"), sm("tmp5")
        abr, abi, fre, fim, den, t2 = sm("abr"), sm("abi"), sm("fre"), sm("fim"), sm("den"), sm("t25")
        nfim = sm("nfim")
        P.act(dt[:], ldt[:], AF.Exp, r=[ldt], w=[dt])
        P.tt(mag[:], lre[:], dt[:], ALU.mult, r=[lre, dt], w=[mag])
        P.act(mag[:], mag[:], AF.Exp, r=[mag], w=[mag])
        P.tt(ang[:], lim[:], dt[:], ALU.mult, r=[lim, dt], w=[ang])

        def sin_of(dst, shift):
            P.ts(x[:], ang[:], shift + PI, None, ALU.add, r=[ang], w=[x])
            P.cp(acc[:], x[:], r=[x], w=[acc])
            for k in (1, 2, 3):
                P.ts(tmp[:], x[:], 2 * PI * k, -2 * PI, ALU.is_ge, ALU.mult, r=[x], w=[tmp])
                P.tt(acc[:], acc[:], tmp[:], ALU.add, r=[acc, tmp], w=[acc])
            P.ts(acc[:], acc[:], -PI, None, ALU.add, r=[acc], w=[acc])
            P.ts(tmp[:], acc[:], -1.0, PI, ALU.mult, ALU.add, r=[acc], w=[tmp])
            P.tt(tmp[:], tmp[:], acc[:], ALU.min, r=[tmp, acc], w=[tmp])
            P.ts(acc[:], acc[:], -1.0, -PI, ALU.mult, ALU.add, r=[acc], w=[acc])
            P.tt(acc[:], acc[:], tmp[:], ALU.max, r=[tmp, acc], w=[acc])
            P.tt(t2[:], acc[:], acc[:], ALU.mult, r=[acc], w=[t2])
            P.ts(tmp[:], t2[:], 1.0 / 6227020800.0, None, ALU.mult, r=[t2], w=[tmp])
            for cf in (-1.0 / 39916800.0, 1.0 / 362880.0, -1.0 / 5040.0, 1.0 / 120.0, -1.0 / 6.0):
                P.stt(tmp[:], tmp[:], cf, t2[:], ALU.add, ALU.mult, r=[tmp, t2], w=[tmp])
            P.stt(dst[:], tmp[:], 1.0, acc[:], ALU.add, ALU.mult, r=[tmp, acc], w=[dst])

        sin_of(abi, 0.0)
        sin_of(abr, PI / 2)
        P.tt(abr[:], abr[:], mag[:], ALU.mult, r=[abr, mag], w=[abr])
        P.tt(abi[:], abi[:], mag[:], ALU.mult, r=[abi, mag], w=[abi])
        P.tt(den[:], lre[:], lre[:], ALU.mult, r=[lre], w=[den])
        P.tt(t2[:], lim[:], lim[:], ALU.mult, r=[lim], w=[t2])
        P.tt(den[:], den[:], t2[:], ALU.add, r=[den, t2], w=[den])
        P.op("dve", lambda e: e.reciprocal(out=den[:], in_=den[:]), [Prog._k(den)], [Prog._k(den)])
        P.ts(tmp[:], abr[:], -1.0, None, ALU.add, r=[abr], w=[tmp])
        P.tt(fre[:], tmp[:], lre[:], ALU.mult, r=[tmp, lre], w=[fre])
        P.tt(t2[:], abi[:], lim[:], ALU.mult, r=[abi, lim], w=[t2])
        P.tt(fre[:], fre[:], t2[:], ALU.add, r=[fre, t2], w=[fre])
        P.tt(fre[:], fre[:], den[:], ALU.mult, r=[fre, den], w=[fre])
        P.tt(fim[:], abi[:], lre[:], ALU.mult, r=[abi, lre], w=[fim])
        P.tt(t2[:], tmp[:], lim[:], ALU.mult, r=[tmp, lim], w=[t2])
        P.tt(fim[:], fim[:], t2[:], ALU.subtract, r=[fim, t2], w=[fim])
        P.tt(fim[:], fim[:], den[:], ALU.mult, r=[fim, den], w=[fim])
        P.ts(nfim[:], fim[:], -1.0, None, ALU.mult, r=[fim], w=[nfim])
        pwr = [abr] + [sm(f"pwr{k}") for k in range(1, NL)]
        pwi = [abi] + [sm(f"pwi{k}") for k in range(1, NL)]
        npwi = [sm(f"npwi{k}") for k in range(NL)]
        for k in range(NL):
            P.ts(npwi[k][:], pwi[k][:], -1.0, None, ALU.mult, r=[pwi[k]], w=[npwi[k]])
            if k + 1 < NL:
                P.tt(pwr[k + 1][:], pwr[k][:], pwr[k][:], ALU.mult, r=[pwr[k]], w=[pwr[k + 1]])
                P.tt(t2[:], pwi[k][:], pwi[k][:], ALU.mult, r=[pwi[k]], w=[t2])
                P.tt(pwr[k + 1][:], pwr[k + 1][:], t2[:], ALU.subtract, r=[pwr[k + 1], t2], w=[pwr[k + 1]])
                P.tt(pwi[k + 1][:], pwr[k][:], pwi[k][:], ALU.mult, r=[pwr[k], pwi[k]], w=[pwi[k + 1]])
                P.ts(pwi[k + 1][:], pwi[k + 1][:], 2.0, None, ALU.mult, r=[pwi[k + 1]], w=[pwi[k + 1]])
        if "dbg" in S:
            for i_, t_ in enumerate((dt, mag, ang, abr, abi, fre, fim, den, pwr[NL - 1], pwi[NL - 1])):
                P.dma("sp", S["dbg"][:, i_ * 8:(i_ + 1) * 8], t_[:], reads=[t_])
        cst_re = P.sb([128, 8], F32)
        cst_im = P.sb([128, 8], F32)
        P.memset(cst_re[:], 0.0, w=[cst_re])
        P.memset(cst_im[:], 0.0, w=[cst_im])
        uf = P.sb([128, 2, TB], F32)
        ub = P.sb([128, 2, TB], BF16)
        Are = [P.sb([128, TB], F32, name=f"Are{i}") for i in range(2)]
        Aim = [P.sb([128, TB], F32, name=f"Aim{i}") for i in range(2)]
        sre = P.sb([128, 8, TB], BF16)
        sim = P.sb([128, 8, TB], BF16)
        yv = P.sb([128, 2, TB], F32)
        yt = P.sb([128, TB], F32)
        yg = P.sb([128, 2, TB], BF16)
        gl = P.sb([128, 4, TB], F32)
        ob = P.sb([128, 2, TB], BF16)
        c4 = P.sb([128, 4], F32)
        psx = [P.ps([128, 512], F32, name=f"ps5{i}") for i in range(4)]
        psy = [P.ps([128, 512], F32, name=f"ps5y{i}") for i in range(2)]
        GC = 2.0 * math.sqrt(2.0 / math.pi)
        uT = S["uT"].rearrange("(c p) t -> p c t", p=128)
        for b in range(T // TB):
            ts_ = slice(b * TB, (b + 1) * TB)
            P.dma("sp", uf[:], uT[:, :, ts_], writes=[uf])
            P.cp(ub[:], uf[:], r=[uf], w=[ub], eng="act")
            for m in range(8):
                kc = m // 4
                pr, pim = psx[(2 * m) % 4], psx[(2 * m + 1) % 4]
                P.mm(pr[:], bre[:, kc, m * 128:(m + 1) * 128], ub[:, kc, :], True, True, r=[bre, ub], w=[pr])
                P.mm(pim[:], bim[:, kc, m * 128:(m + 1) * 128], ub[:, kc, :], True, True, r=[bim, ub], w=[pim])
                a_re, a_im = Are[0], Aim[0]
                mc = slice(m, m + 1)
                P.ts(a_re[:], pr[:], fre[:, mc], None, ALU.mult, r=[pr, fre], w=[a_re])
                P.stt(a_re[:], pim[:], nfim[:, mc], a_re[:], ALU.mult, ALU.add, r=[pim, nfim, a_re], w=[a_re])
                P.ts(a_im[:], pim[:], fre[:, mc], None, ALU.mult, r=[pim, fre], w=[a_im])
                P.stt(a_im[:], pr[:], fim[:, mc], a_im[:], ALU.mult, ALU.add, r=[pr, fim, a_im], w=[a_im])
                P.tt(c4[:, 0:1], abr[:, mc], cst_re[:, mc], ALU.mult, r=[abr, cst_re], w=[c4])
                P.tt(c4[:, 1:2], abi[:, mc], cst_im[:, mc], ALU.mult, r=[abi, cst_im], w=[c4])
                P.tt(c4[:, 2:3], abr[:, mc], cst_im[:, mc], ALU.mult, r=[abr, cst_im], w=[c4])
                P.tt(c4[:, 3:4], abi[:, mc], cst_re[:, mc], ALU.mult, r=[abi, cst_re], w=[c4])
                P.tt(a_re[:, 0:1], a_re[:, 0:1], c4[:, 0:1], ALU.add, r=[a_re, c4], w=[a_re])
                P.tt(a_re[:, 0:1], a_re[:, 0:1], c4[:, 1:2], ALU.subtract, r=[a_re, c4], w=[a_re])
                P.tt(a_im[:, 0:1], a_im[:, 0:1], c4[:, 2:3], ALU.add, r=[a_im, c4], w=[a_im])
                P.tt(a_im[:, 0:1], a_im[:, 0:1], c4[:, 3:4], ALU.add, r=[a_im, c4], w=[a_im])
                cur = 0
                for k in range(NL):
                    d = 1 << k
                    sr, si = Are[cur], Aim[cur]
                    dr, di = Are[1 - cur], Aim[1 - cur]
                    P.stt(dr[:, d:], sr[:, :TB - d], pwr[k][:, mc], sr[:, d:], ALU.mult, ALU.add, r=[sr, pwr[k]], w=[dr])
                    P.stt(dr[:, d:], si[:, :TB - d], npwi[k][:, mc], dr[:, d:], ALU.mult, ALU.add, r=[si, npwi[k], dr], w=[dr])
                    P.stt(di[:, d:], si[:, :TB - d], pwr[k][:, mc], si[:, d:], ALU.mult, ALU.add, r=[si, pwr[k]], w=[di])
                    P.stt(di[:, d:], sr[:, :TB - d], pwi[k][:, mc], di[:, d:], ALU.mult, ALU.add, r=[sr, pwi[k], di], w=[di])
                    P.cp(dr[:, :d], sr[:, :d], r=[sr], w=[dr], eng="pool")
                    P.cp(di[:, :d], si[:, :d], r=[si], w=[di], eng="pool")
                    cur = 1 - cur
                fr, fi = Are[cur], Aim[cur]
                P.cp(cst_re[:, mc], fr[:, TB - 1:TB], r=[fr], w=[cst_re])
                P.cp(cst_im[:, mc], fi[:, TB - 1:TB], r=[fi], w=[cst_im])
                P.cp(sre[:, m, :], fr[:], r=[fr], w=[sre], eng="act")
                P.act(sim[:, m, :], fi[:], AF.Copy, scale=-1.0, r=[fi], w=[sim])
                if cur != 0:
                    pass
            for j in range(2):
                py = psy[j]
                n = 0
                for kc in range(4 * j, 4 * j + 4):
                    P.mm(py[:], cre[:, kc, j * 128:(j + 1) * 128], sre[:, kc, :], n == 0, False, r=[cre, sre], w=[py])
                    n += 1
                    P.mm(py[:], cim[:, kc, j * 128:(j + 1) * 128], sim[:, kc, :], False, kc == 4 * j + 3, r=[cim, sim], w=[py])
                P.stt(yv[:, j, :], uf[:, j, :], dsk[:, j:j + 1], py[:], ALU.mult, ALU.add, r=[uf, dsk, py], w=[yv])
                P.tt(yt[:], yv[:, j, :], yv[:, j, :], ALU.mult, r=[yv], w=[yt])
                P.ts(yt[:], yt[:], 0.044715, 1.0, ALU.mult, ALU.add, r=[yt], w=[yt])
                P.tt(yt[:], yt[:], yv[:, j, :], ALU.mult, r=[yt, yv], w=[yt])
                P.act(yt[:], yt[:], AF.Sigmoid, scale=GC, r=[yt], w=[yt])
                P.tt(yg[:, j, :], yt[:], yv[:, j, :], ALU.mult, r=[yt, yv], w=[yg])
            for n in range(4):
                pg = psx[n]
                for kc in range(2):
                    P.mm(pg[:], wgl[:, kc, n * 128:(n + 1) * 128], yg[:, kc, :], kc == 0, kc == 1, r=[wgl, yg], w=[pg])
                if n < 2:
                    P.cp(gl[:, n, :], pg[:], r=[pg], w=[gl])
                else:
                    P.act(gl[:, n, :], pg[:], AF.Sigmoid, r=[pg], w=[gl])
            P.tt(ob[:], gl[:, 0:2, :], gl[:, 2:4, :], ALU.mult, r=[gl], w=[ob])
            P.dma("sp", mixT[b][:, 6:8, :], ob[:], reads=[ob], writes=["mix_s5"])
        P.barrier()
        P.emit()
        P.renew_sems()
    P.es = P.ges


def phase_rwkv(P, A, l, S, mixT, ntiles=T // 128, stage=9):
    C = C_DEC
    with ExitStack() as es:
        P.es = es

        def cst(name, shape):
            t_ = P.sb(shape, F32, name="c_" + name)
            P.dma("sp", t_[:], A[name], writes=[t_])
            return t_

        tri = cst("tri_incl", [128, 128])
        same = cst("same", [128, 128])
        cind = cst("chunkind", [128, 2])
        cind64 = cst("chunkind64", [128, 64])
        mask4 = cst("mask4", [128, 512])
        masklt = cst("mask_lt", [128, 128])
        ident = cst("ident", [128, 128])
        identb = P.sb([128, 128], BF16)
        P.dma("pool", identb[:], A["ident"], writes=[identb])
        par = {}
        for nm in ("rw_w0", "rw_a0", "rw_k_k", "rw_k_a", "rw_r_k", "rw_gn_w", "rw_gn_b"):
            t_ = P.sb([128, 384], F32, name="p_" + nm)
            load_bcast(P, "sp", t_[:], A[nm][l], 384)
            par[nm] = t_
        ST = P.sb([128, 3, 2, 64], F32)
        P.memset(ST[:], 0.0, w=[ST])
        fmz = [[P.sb([128, 512], F32, name=f"fmz{g}{e}") for e in range(2)] for g in range(3)]
        Bdz = [P.sb([128, 384], F32, name=f"Bdz{c}") for c in range(2)]
        Kdz = [P.sb([128, 384], F32, name=f"Kdz{c}") for c in range(2)]
        rkvb = [P.sb([128, 1152], F32, name=f"rkvb{i}") for i in range(2)]
        lorb = [P.sb([128, 1152], F32, name=f"lorb{i}") for i in range(2)]
        w = lambda n: P.sb([128, 384], F32, name="w_" + n)
        sig, a, kk, kp, cs, tmpx, tmpe, Ep, Em, Ex, Eend, ka, Bd, Kd, t4, On = [w(n) for n in (
            "sig", "a", "kk", "kp", "cs", "tmpx", "tmpe", "Ep", "Em", "Ex", "Eend", "ka", "Bd", "Kd", "t4", "On")]
        Q4 = P.sb([128, 4, 384], F32)
        fm = [P.sb([128, 512], F32, name=f"fm{g}") for g in range(3)]
        Gs = P.sb([128, 6, 512], F32)
        MA = [P.sb([128, 6, 128], F32, name=f"MA{i}") for i in range(2)]
        MTA = [P.sb([128, 6, 128], F32, name=f"MTA{i}") for i in range(2)]
        Rall = P.sb([128, 6, 128], F32)
        XT = P.sb([128, 384], F32)
        WT = P.sb([128, 384], F32)
        P.memset(XT[:], 0.0, w=[XT])
        P.memset(WT[:], 0.0, w=[WT])
        O = P.sb([128, 384], F32)
        Ob = P.sb([128, 384], BF16)
        OT = [P.sb([128, 3, 512], BF16, name=f"OT{i}") for i in range(2)]
        ss = P.sb([128, 6], F32)
        bsum = P.sb([128, 6], F32)
        s1 = P.sb([128, 6], F32)
        s2 = P.sb([128, 6], F32)
        m2 = P.sb([128, 6], F32)
        pcs = P.sb([128, 6], F32)
        pool_ps = [P.ps([128, 512], F32, name=f"psr{i}") for i in range(5)]
        psM_fixed = [P.ps([128, 512], F32, name=f"psrM{i}") for i in range(2)]
        rr = [0]

        def nps():
            p_ = pool_ps[rr[0] % 5]
            rr[0] += 1
            return p_

        v3 = lambda t_: t_[:].rearrange("p (h j) -> p h j", j=64)
        b3 = lambda t_: t_[:].unsqueeze(2).to_broadcast([128, 6, 64])
        recip = lambda t_: P.op("dve", lambda e: e.reciprocal(out=t_[:], in_=t_[:]), [Prog._k(t_)], [Prog._k(t_)])

        for ti in range(ntiles):
            t0 = ti * 128
            RK = rkvb[ti % 2]
            LO = lorb[ti % 2]
            P.dma("sp", RK[:], S["rkv"][t0:t0 + 128, :], writes=[RK])
            P.dma("sp", LO[:], S["lor"][t0:t0 + 128, :], writes=[LO])
            R, Kx, V = RK[:, 0:384], RK[:, 384:768], RK[:, 768:1152]
            XW, XA, G = LO[:, 0:384], LO[:, 384:768], LO[:, 768:1152]
            P.tt(sig[:], XW, par["rw_w0"][:], ALU.add, r=[LO, par["rw_w0"]], w=[sig])
            P.act(sig[:], sig[:], AF.Sigmoid, r=[sig], w=[sig])
            P.tt(a[:], XA, par["rw_a0"][:], ALU.add, r=[LO, par["rw_a0"]], w=[a])
            P.act(a[:], a[:], AF.Sigmoid, r=[a], w=[a])
            P.tt(kk[:], Kx, par["rw_k_k"][:], ALU.mult, r=[RK, par["rw_k_k"]], w=[kk])
            P.tt(t4[:], kk[:], kk[:], ALU.mult, r=[kk], w=[t4])
            P.red(ss[:], v3(t4), ALU.add, r=[t4], w=[ss])
            P.ts(ss[:], ss[:], 1e-24, None, ALU.max, r=[ss], w=[ss])
            P.act(ss[:], ss[:], AF.Sqrt, r=[ss], w=[ss])
            recip(ss)
            P.tt(v3(kk), v3(kk), b3(ss), ALU.mult, r=[kk, ss], w=[kk])
            P.stt(t4[:], a[:], -1.0, par["rw_k_a"][:], ALU.add, ALU.mult, r=[a, par["rw_k_a"]], w=[t4])
            P.stt(kp[:], t4[:], 1.0, Kx, ALU.add, ALU.mult, r=[t4, RK], w=[kp])
            psA, psB = nps(), nps()
            P.mm(psA[:, 0:384], tri[:], sig[:], True, True, r=[tri, sig], w=[psA])
            P.mm(psB[:, 0:384], same[:], sig[:], True, True, r=[same, sig], w=[psB])
            P.cp(cs[:], psA[:, 0:384], r=[psA], w=[cs], eng="act")
            P.act(Ep[:], cs[:], AF.Exp, scale=-C, r=[cs], w=[Ep])
            P.act(Em[:], cs[:], AF.Exp, scale=C, r=[cs], w=[Em])
            P.tt(tmpx[:], cs[:], sig[:], ALU.subtract, r=[cs, sig], w=[tmpx])
            P.act(Ex[:], tmpx[:], AF.Exp, scale=-C, r=[tmpx], w=[Ex])
            P.tt(tmpe[:], psB[:, 0:384], cs[:], ALU.subtract, r=[psB, cs], w=[tmpe])
            P.act(Eend[:], tmpe[:], AF.Exp, scale=-C, r=[tmpe], w=[Eend])
            P.tt(ka[:], kk[:], a[:], ALU.mult, r=[kk, a], w=[ka])
            P.stt(Q4[:, 0, :], kk[:], -1.0, Ex[:], ALU.mult, ALU.mult, r=[kk, Ex], w=[Q4])
            P.tt(Q4[:, 1, :], R, Ep[:], ALU.mult, r=[RK, Ep], w=[Q4])
            P.tt(Q4[:, 2, :], ka[:], Em[:], ALU.mult, r=[ka, Em], w=[Q4])
            P.tt(Q4[:, 3, :], kp[:], Em[:], ALU.mult, r=[kp, Em], w=[Q4])
            P.tt(Bd[:], ka[:], Eend[:], ALU.mult, r=[ka, Eend], w=[Bd])
            P.tt(Kd[:], kp[:], Eend[:], ALU.mult, r=[kp, Eend], w=[Kd])
            for c_ in range(2):
                P.ts(Bdz[c_][:], Bd[:], cind[:, c_:c_ + 1], None, ALU.mult, r=[Bd, cind], w=[Bdz[c_]], eng="pool")
                P.ts(Kdz[c_][:], Kd[:], cind[:, c_:c_ + 1], None, ALU.mult, r=[Kd, cind], w=[Kdz[c_]], eng="pool")
            P.tt(t4[:], R, kp[:], ALU.mult, r=[RK, kp], w=[t4])
            P.tt(t4[:], t4[:], par["rw_r_k"][:], ALU.mult, r=[t4, par["rw_r_k"]], w=[t4])
            P.red(bsum[:], v3(t4), ALU.add, r=[t4], w=[bsum])
            if stage <= 1:
                continue
            psP = nps()
            for g in range(3):
                psT = nps()
                for q in range(4):
                    P.mm(psT[:, q * 128:(q + 1) * 128], Q4[:, q, g * 128:(g + 1) * 128], ident[:], True, True, r=[Q4, ident], w=[psT])
                P.cp(fm[g][:], psT[:], r=[psT], w=[fm[g]], eng="act" if g % 2 else "dve")
                for e_ in range(2):
                    P.ts(fmz[g][e_][:], fm[g][:], cind[:, e_:e_ + 1], None, ALU.mult, r=[fm[g], cind], w=[fmz[g][e_]],
                         eng="pool" if e_ else "dve")
                P.mm(psP[:, g * 64:(g + 1) * 64], sig[:, g * 128:(g + 1) * 128], cind64[:], True, True, r=[sig, cind64], w=[psP])
            P.act(pcs[:].rearrange("p (g c) -> p g c", c=2), psP[:, 0:192].rearrange("p (g c) -> p g c", c=64)[:, :, 0:2], AF.Exp, scale=-C,
                  r=[psP], w=[pcs])
            if stage <= 2:
                continue
            psM = psM_fixed
            for h in range(6):
                g, e_ = h // 2, h % 2
                f_, fz = fm[g], fmz[g][e_]
                psG = nps()
                P.mm(psG[:, 0:256], f_[:, 256:384], fz[:, 0:256], True, True, r=[f_, fz], w=[psG])
                P.mm(psG[:, 256:512], f_[:, 384:512], fz[:, 0:256], True, True, r=[f_, fz], w=[psG])
                P.tt(Gs[:, h, :], psG[:], mask4[:], ALU.mult, r=[psG, mask4], w=[Gs])
                pm = psM[h // 3]
                P.mm(pm[:, (h % 3) * 128:(h % 3 + 1) * 128], f_[:, 0:128], fz[:, 256:384], True, True, r=[f_, fz], w=[pm])
            for half in range(2):
                P.tt(MTA[0][:, 3 * half:3 * half + 3, :], psM[half][:, 0:384].rearrange("p (h u) -> p h u", u=128),
                     masklt[:].unsqueeze(1).to_broadcast([128, 3, 128]), ALU.mult, r=[psM[half], masklt], w=[MTA[0]])
            P.cp(MA[0][:], Gs[:, :, 0:128], r=[Gs], w=[MA[0]], eng="pool")
            P.tt(Rall[:], Gs[:, :, 0:128], ident[:].unsqueeze(1).to_broadcast([128, 6, 128]), ALU.add, r=[Gs, ident], w=[Rall])
            if stage <= 3:
                continue
            cur = 0
            for lvl in range(1, 6):
                last = lvl == 5
                Mc, MTc, Mn, MTn = MA[cur], MTA[cur], MA[1 - cur], MTA[1 - cur]
                for half in range(2):
                    hs = slice(3 * half, 3 * half + 3)
                    psa, psb, psc = nps(), nps(), nps()
                    for hh in range(3):
                        h = 3 * half + hh
                        cs_ = slice(hh * 128, (hh + 1) * 128)
                        if not last:
                            P.mm(psa[:, cs_], MTc[:, h, :], Mc[:, h, :], True, True, r=[MTc, Mc], w=[psa])
                        P.mm(psb[:, cs_], Mc[:, h, :], MTc[:, h, :], True, True, r=[MTc, Mc], w=[psb])
                    if not last:
                        P.cp(Mn[:, hs, :], psa[:, 0:384].rearrange("p (h u) -> p h u", u=128), r=[psa], w=[Mn], eng="act")
                    P.cp(MTn[:, hs, :], psb[:, 0:384].rearrange("p (h u) -> p h u", u=128), r=[psb], w=[MTn])
                    for hh in range(3):
                        h = 3 * half + hh
                        P.mm(psc[:, hh * 128:(hh + 1) * 128], MTn[:, h, :], Rall[:, h, :], True, True, r=[MTn, Rall], w=[psc])
                    P.tt(Rall[:, hs, :], Rall[:, hs, :], psc[:, 0:384].rearrange("p (h u) -> p h u", u=128), ALU.add,
                         r=[Rall, psc], w=[Rall])
                cur = 1 - cur
            if stage <= 4:
                continue
            for cc in range(2):
                cb = cc * 64
                psX, psW, psO, psS = nps(), nps(), nps(), nps()
                for h in range(6):
                    g, e_ = h // 2, h % 2
                    hc = slice(h * 64, (h + 1) * 64)
                    P.mm(psX[:, hc], fm[g][:, 0:128], ST[:, g, e_, :], True, False, r=[fm[g], ST], w=[psX])
                    P.mm(psX[:, hc], Gs[:, h, 256:384], RK[:, 768 + h * 64:768 + (h + 1) * 64], False, True,
                         r=[Gs, RK], w=[psX])
                P.cp(XT[cb:cb + 64, :], psX[cb:cb + 64, 0:384], r=[psX], w=[XT], eng="act")
                for h in range(6):
                    hc = slice(h * 64, (h + 1) * 64)
                    P.mm(psW[:, hc], Rall[:, h, :], XT[:, hc], True, True, r=[Rall, XT], w=[psW])
                P.cp(WT[cb:cb + 64, :], psW[cb:cb + 64, 0:384], r=[psW], w=[WT])
                for h in range(6):
                    g, e_ = h // 2, h % 2
                    hc = slice(h * 64, (h + 1) * 64)
                    Vh = RK[:, 768 + h * 64:768 + (h + 1) * 64]
                    P.mm(psO[:, hc], fm[g][:, 128:256], ST[:, g, e_, :], True, False, r=[fm[g], ST], w=[psO])
                    P.mm(psO[:, hc], Gs[:, h, 128:256], WT[:, hc], False, False, r=[Gs, WT], w=[psO])
                    P.mm(psO[:, hc], Gs[:, h, 384:512], Vh, False, True, r=[Gs, RK], w=[psO])
                    P.mm(psS[:, hc], Bdz[cc][:, g * 128:(g + 1) * 128], WT[:, hc], True, False, r=[Bdz[cc], WT], w=[psS])
                    P.mm(psS[:, hc], Kdz[cc][:, g * 128:(g + 1) * 128], Vh, False, True, r=[Kdz[cc], RK], w=[psS])
                P.cp(O[cb:cb + 64, :], psO[cb:cb + 64, 0:384], r=[psO], w=[O], eng="act")
                for h in range(6):
                    g, e_ = h // 2, h % 2
                    jb = e_ * 64
                    P.stt(ST[jb:jb + 64, g, e_, :], ST[jb:jb + 64, g, e_, :], pcs[jb:jb + 64, 2 * g + cc:2 * g + cc + 1],
                          psS[jb:jb + 64, h * 64:(h + 1) * 64], ALU.mult, ALU.add, r=[ST, pcs, psS], w=[ST])
            if stage <= 5:
                continue
            if "dbgO" in S and ti == 0:
                P.dma("sp", S["dbgO"][:, 0:384], O[:], reads=[O])
                P.dma("sp", S["dbgO"][:, 384:768], XT[:], reads=[XT])
                P.dma("sp", S["dbgO"][:, 768:1152], WT[:], reads=[WT])
                P.dma("sp", S["dbgO"][:, 1152:1664], Gs[:, 0, :], reads=[Gs])
                P.dma("sp", S["dbgO"][:, 1664:1792], Rall[:, 0, :], reads=[Rall])
                P.dma("sp", S["dbgO"][:, 1792:2304], fm[0][:], reads=[fm[0]])
                P.dma("sp", S["dbgO"][:, 2304:2310], pcs[:], reads=[pcs])
            P.red(s1[:], v3(O), ALU.add, r=[O], w=[s1])
            P.tt(t4[:], O[:], O[:], ALU.mult, r=[O], w=[t4])
            P.red(s2[:], v3(t4), ALU.add, r=[t4], w=[s2])
            P.ts(s1[:], s1[:], 1.0 / 64, None, ALU.mult, r=[s1], w=[s1])
            P.ts(s2[:], s2[:], 1.0 / 64, None, ALU.mult, r=[s2], w=[s2])
            P.tt(m2[:], s1[:], s1[:], ALU.mult, r=[s1], w=[m2])
            P.tt(s2[:], s2[:], m2[:], ALU.subtract, r=[s2, m2], w=[s2])
            P.ts(s2[:], s2[:], 64e-5, None, ALU.add, r=[s2], w=[s2])
            P.act(s2[:], s2[:], AF.Sqrt, r=[s2], w=[s2])
            recip(s2)
            P.tt(v3(On), v3(O), b3(s1), ALU.subtract, r=[O, s1], w=[On])
            P.tt(v3(On), v3(On), b3(s2), ALU.mult, r=[On, s2], w=[On])
            P.tt(On[:], On[:], par["rw_gn_w"][:], ALU.mult, r=[On, par["rw_gn_w"]], w=[On])
            P.tt(On[:], On[:], par["rw_gn_b"][:], ALU.add, r=[On, par["rw_gn_b"]], w=[On])
            P.tt(v3(t4), V.rearrange("p (h j) -> p h j", j=64), b3(bsum), ALU.mult, r=[RK, bsum], w=[t4])
            P.tt(On[:], On[:], t4[:], ALU.add, r=[On, t4], w=[On])
            P.tt(Ob[:], On[:], G, ALU.mult, r=[On, LO], w=[Ob])
            psTo = nps()
            for g in range(3):
                P.mm(psTo[:, g * 128:(g + 1) * 128], Ob[:, g * 128:(g + 1) * 128], identb[:], True, True, r=[Ob, identb], w=[psTo])
            ot = OT[(ti // 4) % 2]
            P.cp(ot[:, :, (ti % 4) * 128:(ti % 4 + 1) * 128], psTo[:, 0:384].rearrange("p (g t) -> p g t", t=128), r=[psTo], w=[ot])
            if ti % 4 == 3 or ti == ntiles - 1:
                P.dma("sp", mixT[ti // 4][:, 0:3, :], ot[:], reads=[ot], writes=["mix_rw"])
        P.barrier()
        P.emit()
        P.renew_sems()
    P.es = P.ges


def phase_nsa(P, A, l, S, mixT, ntiles=T // 128):
    GC = 2.0 * math.sqrt(2.0 / math.pi)
    with ExitStack() as es:
        P.es = es

        def cst(name, shape, dt=F32, q="sp", src=None):
            t_ = P.sb(shape, dt, name="n_" + name)
            P.dma(q, t_[:], A[name] if src is None else src, writes=[t_])
            return t_

        identb = cst("ident", [128, 128], BF16, "pool")
        causal = cst("causal", [128, 128])
        winlo = cst("winlo", [128, 128])
        cmask = cst("cmask", [128, 8])
        rowv0 = cst("rowvalid0", [128, 1])
        ovl = P.sb([128, 2, 64], BF16)
        for ch in range(2):
            P.dma("pool", ovl[:, ch, :], A["overlap"][ch], writes=[ovl])
        KA = {}
        for nm in ("ks", "kw"):
            for g in range(2):
                t_ = P.sb([128, T], BF16, name=f"KA{nm}{g}")
                P.memset(t_[:], 0.0, w=[t_], eng="pool")
                P.dma("sp", t_[0:64, :], S[nm][g * 64:(g + 1) * 64, :], writes=[t_])
                P.dma("pool", t_[64:68, :], A["kaug"], writes=[t_])
                KA[(nm, g)] = t_
        vsw = P.sb([128, 32, 256], BF16)
        P.dma("sp", vsw[:], S["vsw"].rearrange("(b p) c -> p b c", p=128), writes=[vsw])
        kcmpA = [P.sb([128, 256], BF16, name=f"kcmpA{g}") for g in range(2)]
        vcmp = P.sb([128, 2, 2, 64], BF16)
        with ExitStack() as es2:
            P.es = es2
            w1 = {}
            w2 = {}
            pe2 = {}
            for kv in ("k", "v"):
                w1[kv] = P.sb([128, 16, 128], BF16, name="w1" + kv)
                P.dma("pool", w1[kv][:], A["cmp_w1_" + kv][l].rearrange("(a p) m -> p a m", p=128), writes=[w1[kv]])
                w2[kv] = P.sb([128, 128], BF16, name="w2" + kv)
                P.dma("pool", w2[kv][:], A["cmp_w2p_" + kv][l], writes=[w2[kv]])
                pe2[kv] = P.sb([128, 16, 64], BF16, name="pe2" + kv)
                P.dma("pool", pe2[kv][:], A["cmp_pe2_" + kv][l], writes=[pe2[kv]])
            kc2 = P.sb([128, T], BF16)
            hg = P.sb([128, 256], BF16)
            hf = P.sb([128, 256], F32)
            ht = P.sb([128, 256], F32)
            bias = P.sb([128, 64], F32)
            psa = P.ps([128, 512], F32, name="pscA")
            psb = P.ps([128, 512], F32, name="pscB")
            psc = P.ps([128, 512], F32, name="pscC")
            P.memset(kc2[:], 0.0, w=[kc2])
            P.memset(hg[:], 0.0, w=[hg])
            for kv in ("k", "v"):
                for a_ in range(16):
                    P.mm(psb[:, 0:64], w1[kv][:, a_, :], pe2[kv][:, a_, :], a_ == 0, a_ == 15, r=[w1[kv], pe2[kv]], w=[psb])
                P.cp(bias[:], psb[:, 0:64], r=[psb], w=[bias])
                for g in range(2):
                    src = S["kc" if kv == "k" else "vc"]
                    P.dma("sp", kc2[0:64, :], src[g * 64:(g + 1) * 64, :], writes=[kc2])
                    P.dma("sp", kc2[64:128, 0:T - 1], src[g * 64:(g + 1) * 64, 1:T], writes=[kc2])
                    for a_ in range(16):
                        P.mm(psa[:, 0:255], w1[kv][:, a_, :], kc2[:, 2 * a_: 2 * a_ + 16 * 254 + 1: 16], a_ == 0, a_ == 15,
                             r=[w1[kv], kc2], w=[psa])
                    P.ts(hf[:, 0:255], psa[:, 0:255], bias[:, 0:1], None, ALU.add, r=[psa, bias], w=[hf])
                    P.tt(ht[:, 0:255], hf[:, 0:255], hf[:, 0:255], ALU.mult, r=[hf], w=[ht])
                    P.ts(ht[:, 0:255], ht[:, 0:255], 0.044715, 1.0, ALU.mult, ALU.add, r=[ht], w=[ht])
                    P.tt(ht[:, 0:255], ht[:, 0:255], hf[:, 0:255], ALU.mult, r=[ht, hf], w=[ht])
                    P.act(ht[:, 0:255], ht[:, 0:255], AF.Sigmoid, scale=GC, r=[ht], w=[ht])
                    P.tt(hg[:, 0:255], ht[:, 0:255], hf[:, 0:255], ALU.mult, r=[ht, hf], w=[hg])
                    if kv == "k":
                        P.mm(psc[:, 0:256], w2[kv][:], hg[:], True, True, r=[w2[kv], hg], w=[psc])
                        P.cp(kcmpA[g][:], psc[:, 0:256], r=[psc], w=[kcmpA[g]])
                        P.dma("pool", kcmpA[g][64:68, :], A["caug"], writes=[kcmpA[g]])
                    else:
                        for ch in range(2):
                            P.mm(psc[:, ch * 64:(ch + 1) * 64], hg[:, ch * 128:(ch + 1) * 128], w2[kv][:, 0:64], True, True,
                                 r=[w2[kv], hg], w=[psc])
                        P.cp(vcmp[:, :, g, :], psc[:, 0:128].rearrange("p (c d) -> p c d", d=64), r=[psc], w=[vcmp])
            P.barrier()
            P.emit()
        P.es = es
        qA = [P.sb([128, 128], BF16, name=f"qA{i}") for i in range(4)]
        for t_ in qA:
            P.memset(t_[:], 0.0, w=[t_], eng="pool")
        gts = [P.sb([128, 18], F32, name=f"ngt{i}") for i in range(2)]
        selb = [P.sb([128, 64], F32, name=f"selb{i}") for i in range(2)]
        scs = [P.sb([128, T], F32, name=f"nsc{i}") for i in range(2)]
        pbs = [P.sb([128, T], BF16, name=f"npb{i}") for i in range(2)]
        pT = [P.sb([128, 4, 128], BF16, name=f"npT{i}") for i in range(2)]
        pn = [P.sb([128, 256], BF16, name=f"pn{i}") for i in range(3)]
        for t_ in pn:
            P.memset(t_[:], 0.0, w=[t_])
        pnT = [P.sb([128, 2, 128], BF16, name=f"pnT{i}") for i in range(3)]
        acc = P.sb([128, 384], F32)
        accb = P.sb([128, 384], BF16)
        OT = [P.sb([128, 3, 512], BF16, name=f"nOT{i}") for i in range(2)]
        sm = lambda n: P.sb([128, 1], F32, name=n)
        rmaxs = [sm(f"rmax{i}") for i in range(2)]
        ssums = [sm(f"ssum{i}") for i in range(2)]
        gss = [sm(f"gs{i}") for i in range(2)]
        sc64 = P.sb([128, 64], F32)
        sc64b = P.sb([128, 64], F32)
        selneg = P.sb([128, 64], F32)
        m8a = P.sb([128, 8], F32)
        m8b = P.sb([128, 8], F32)
        psS = [P.ps([128, 512], F32, name=f"psn{i}") for i in range(3)]
        psTT = [P.ps([128, 512], F32, name=f"psnT{i}") for i in range(2)]
        psOs = [P.ps([128, 512], F32, name=f"psnO{i}") for i in range(2)]
        psO = psOs[0]
        psI = P.ps([128, 512], F32, name="psnI")
        cnt = {"s": 0, "t": 0, "q": 0, "p": 0, "b": 0}
        csc = [P.sb([128, 256], F32, name=f"csc{i}") for i in range(3)]
        cpb = [P.sb([128, 256], BF16, name=f"cpb{i}") for i in range(3)]
        crmax = [sm(f"crmax{i}") for i in range(3)]
        cssum = [sm(f"cssum{i}") for i in range(3)]

        def softmax_pv(ncols, Vfn, nblk0, gate_ap, first, bi):
            sc, pb, rmax, ssum, gs = scs[bi], pbs[bi], rmaxs[bi], ssums[bi], gss[bi]
            P.red(rmax[:], sc[:, 0:ncols], ALU.max, r=[sc], w=[rmax])
            P.ts(rmax[:], rmax[:], -1.0, None, ALU.mult, r=[rmax], w=[rmax])
            P.act(pb[:, 0:ncols], sc[:, 0:ncols], AF.Exp, bias=rmax[:], r=[sc, rmax], w=[pb])
            P.red(ssum[:], pb[:, 0:ncols], ALU.add, r=[pb], w=[ssum])
            P.op("dve", lambda e: e.reciprocal(out=ssum[:], in_=ssum[:]), [Prog._k(ssum)], [Prog._k(ssum)])
            P.tt(gs[:], ssum[:], gate_ap, ALU.mult, r=[ssum, "gates"], w=[gs])
            nb = ncols // 128
            for c0 in range(0, nb, 4):
                n4 = min(4, nb - c0)
                pst = psTT[cnt["t"] % 2]
                ptt = pT[cnt["t"] % 2]
                cnt["t"] += 1
                for k in range(n4):
                    P.mm(pst[:, k * 128:(k + 1) * 128], pb[:, (c0 + k) * 128:(c0 + k + 1) * 128], identb[:], True, True,
                         r=[pb, identb], w=[pst])
                P.cp(ptt[:, 0:n4, :], pst[:, 0:n4 * 128].rearrange("p (k q) -> p k q", q=128), r=[pst], w=[ptt],
                     eng="act" if cnt["t"] % 2 else "dve")
                for k in range(n4):
                    P.mm(psO[:, 0:64], ptt[:, k, :], Vfn(nblk0 + c0 + k), c0 + k == 0, c0 + k == nb - 1, r=[ptt, vsw], w=[psO])
            return gs

        for qi in range(ntiles):
            t0 = qi * 128
            G = gts[qi % 2]
            P.dma("sp", G[:], S["gates"][t0:t0 + 128, :], writes=[G, "gates"])
            for g in range(2):
                sbias = selb[g]
                if g == 0:
                    P.dma("sp", selb[0][:], A["selbias"][qi], writes=[selb[0]])
                qts = []
                n0 = 8 * qi
                ncol = min(255, n0 + 7)
                nch = (ncol + 127) // 128
                cps = []
                for r_ in range(3):
                    h = 3 * g + r_
                    qt = qA[cnt["q"] % 4]
                    cnt["q"] += 1
                    qsrc = S[f"q{h // 2}"]
                    P.dma("sp", qt[0:64, :], qsrc[(h % 2) * 64:(h % 2 + 1) * 64, t0:t0 + 128], writes=[qt])
                    P.dma("pool", qt[64:68, :], A["qaug"][h, :, t0:t0 + 128], writes=[qt])
                    qts.append(qt)
                    ps = psS[cnt["s"] % 3]
                    cnt["s"] += 1
                    sc = csc[r_]
                    P.mm(ps[:, 0:ncol], qt[:], kcmpA[g][:, 0:ncol], True, True, r=[qt, kcmpA[g]], w=[ps])
                    P.cp(sc[:, 0:ncol], ps[:, 0:ncol], r=[ps], w=[sc], eng="act")
                    lo = max(0, n0 - 1)
                    mlo = lo - (n0 - 1)
                    P.tt(sc[:, lo:ncol], sc[:, lo:ncol], cmask[:, mlo:mlo + (ncol - lo)], ALU.add, r=[sc, cmask], w=[sc])
                for r_ in range(3):
                    P.red(crmax[r_][:], csc[r_][:, 0:ncol], ALU.max, r=[csc[r_]], w=[crmax[r_]])
                    P.ts(crmax[r_][:], crmax[r_][:], -1.0, None, ALU.mult, r=[crmax[r_]], w=[crmax[r_]])
                for r_ in range(3):
                    P.act(cpb[r_][:, 0:ncol], csc[r_][:, 0:ncol], AF.Exp, bias=crmax[r_][:], r=[csc[r_], crmax[r_]], w=[cpb[r_]])
                for r_ in range(3):
                    ssum = cssum[r_]
                    P.red(ssum[:], cpb[r_][:, 0:ncol], ALU.add, r=[cpb[r_]], w=[ssum])
                    P.op("dve", lambda e, ssum=ssum: e.reciprocal(out=ssum[:], in_=ssum[:]), [Prog._k(ssum)], [Prog._k(ssum)])
                    if qi == 0:
                        P.tt(ssum[:], ssum[:], rowv0[:], ALU.mult, r=[ssum, rowv0], w=[ssum])
                    P.ts(pn[r_][:, 0:ncol], cpb[r_][:, 0:ncol], ssum[:, 0:1], None, ALU.mult, r=[cpb[r_], ssum], w=[pn[r_]])
                for r_ in range(3):
                    h = 3 * g + r_
                    pnr, pnt, po = pn[r_], pnT[r_], psOs[r_ % 2]
                    pst = psTT[cnt["t"] % 2]
                    cnt["t"] += 1
                    for ch in range(nch):
                        P.mm(pst[:, ch * 128:(ch + 1) * 128], pnr[:, ch * 128:(ch + 1) * 128], identb[:], True, True,
                             r=[pnr, identb], w=[pst])
                    P.cp(pnt[:, 0:nch, :], pst[:, 0:nch * 128].rearrange("p (k q) -> p k q", q=128), r=[pst], w=[pnt])
                    for ch in range(nch):
                        P.mm(po[:, 0:64], pnt[:, ch, :], vcmp[:, ch, g, :], ch == 0, ch == nch - 1, r=[pnt, vcmp], w=[po])
                        P.mm(psI[:, 0:64], pnt[:, ch, :], ovl[:, ch, :], r_ == 0 and ch == 0, r_ == 2 and ch == nch - 1,
                             r=[pnt, ovl], w=[psI])
                    P.ts(acc[:, h * 64:(h + 1) * 64], po[:, 0:64], G[:, 3 * h:3 * h + 1], None, ALU.mult, r=[po, G], w=[acc])
                P.tt(sc64[:], psI[:, 0:64], selb[0][:], ALU.add, r=[psI, selb[0]], w=[sc64])
                P.op("dve", lambda e: e.max(out=m8a[:], in_=sc64[:]), [Prog._k(sc64)], [Prog._k(m8a)])
                P.op("dve", lambda e: e.match_replace(out=sc64b[:], in_to_replace=m8a[:], in_values=sc64[:], imm_value=-3.0e38),
                     [Prog._k(sc64), Prog._k(m8a)], [Prog._k(sc64b)])
                P.op("dve", lambda e: e.max(out=m8b[:], in_=sc64b[:]), [Prog._k(sc64b)], [Prog._k(m8b)])
                P.ts(selneg[:], sc64[:], m8b[:, 7:8], NEG, ALU.is_lt, ALU.mult, r=[sc64, m8b], w=[selneg])
                for r_ in range(3):
                    h = 3 * g + r_
                    qt = qts[r_]
                    items = []
                    for bi, br in enumerate(("sel", "win")):
                        kb0 = 0 if br == "sel" else max(0, qi - 4)
                        items.append(dict(br=br, bi=bi, kb0=kb0, nk=(qi - kb0 + 1) * 128, KAt=KA[("ks" if br == "sel" else "kw", g)],
                                          voff=(0 if br == "sel" else 128) + g * 64, gate=G[:, 3 * h + (1 if br == "sel" else 2):3 * h + (2 if br == "sel" else 3)],
                                          sc=scs[bi], pb=pbs[bi], rmax=rmaxs[bi], ssum=ssums[bi], gs=gss[bi], psO=psOs[bi]))
                    for it in items:
                        sc, nk, kb0 = it["sc"], it["nk"], it["kb0"]
                        for c0 in range(0, nk, 512):
                            wdt = min(512, nk - c0)
                            ps = psS[cnt["s"] % 3]
                            cnt["s"] += 1
                            P.mm(ps[:, 0:wdt], qt[:], it["KAt"][:, kb0 * 128 + c0:kb0 * 128 + c0 + wdt], True, True, r=[qt, it["KAt"]], w=[ps])
                            if it["br"] == "sel":
                                nj = wdt // 64
                                P.tt(sc[:, c0:c0 + wdt].rearrange("p (j k) -> p j k", k=64), ps[:, 0:wdt].rearrange("p (j k) -> p j k", k=64),
                                     selneg[:, c0 // 64:c0 // 64 + nj].unsqueeze(2).to_broadcast([128, nj, 64]), ALU.add,
                                     r=[ps, selneg], w=[sc])
                            else:
                                P.cp(sc[:, c0:c0 + wdt], ps[:, 0:wdt], r=[ps], w=[sc], eng="act")
                        P.tt(sc[:, nk - 128:nk], sc[:, nk - 128:nk], causal[:], ALU.add, r=[sc, causal], w=[sc])
                        if it["br"] == "win" and qi >= 4:
                            P.tt(sc[:, 0:128], sc[:, 0:128], winlo[:], ALU.add, r=[sc, winlo], w=[sc])
                    for it in items:
                        P.red(it["rmax"][:], it["sc"][:, 0:it["nk"]], ALU.max, r=[it["sc"]], w=[it["rmax"]])
                        P.ts(it["rmax"][:], it["rmax"][:], -1.0, None, ALU.mult, r=[it["rmax"]], w=[it["rmax"]])
                    for it in items:
                        P.act(it["pb"][:, 0:it["nk"]], it["sc"][:, 0:it["nk"]], AF.Exp, bias=it["rmax"][:], r=[it["sc"], it["rmax"]], w=[it["pb"]])
                    for it in items:
                        ssum, gs = it["ssum"], it["gs"]
                        P.red(ssum[:], it["pb"][:, 0:it["nk"]], ALU.add, r=[it["pb"]], w=[ssum])
                        P.op("dve", lambda e, ssum=ssum: e.reciprocal(out=ssum[:], in_=ssum[:]), [Prog._k(ssum)], [Prog._k(ssum)])
                        P.tt(gs[:], ssum[:], it["gate"], ALU.mult, r=[ssum, "gates"], w=[gs])
                    for it in items:
                        pb, nb, psO_ = it["pb"], it["nk"] // 128, it["psO"]
                        for c0 in range(0, nb, 4):
                            n4 = min(4, nb - c0)
                            pst = psTT[cnt["t"] % 2]
                            ptt = pT[cnt["t"] % 2]
                            cnt["t"] += 1
                            for k in range(n4):
                                P.mm(pst[:, k * 128:(k + 1) * 128], pb[:, (c0 + k) * 128:(c0 + k + 1) * 128], identb[:], True, True,
                                     r=[pb, identb], w=[pst])
                            P.cp(ptt[:, 0:n4, :], pst[:, 0:n4 * 128].rearrange("p (k q) -> p k q", q=128), r=[pst], w=[ptt],
                                 eng="act" if cnt["t"] % 2 else "dve")
                            for k in range(n4):
                                P.mm(psO_[:, 0:64], ptt[:, k, :], vsw[:, it["kb0"] + c0 + k, it["voff"]:it["voff"] + 64],
                                     c0 + k == 0, c0 + k == nb - 1, r=[ptt, vsw], w=[psO_])
                    for it in items:
                        P.stt(acc[:, h * 64:(h + 1) * 64], it["psO"][:, 0:64], it["gs"][:, 0:1], acc[:, h * 64:(h + 1) * 64], ALU.mult, ALU.add,
                              r=[it["psO"], it["gs"], acc], w=[acc])
            P.cp(accb[:], acc[:], r=[acc], w=[accb])
            pst = psTT[cnt["t"] % 2]
            cnt["t"] += 1
            for k in range(3):
                P.mm(pst[:, k * 128:(k + 1) * 128], accb[:, k * 128:(k + 1) * 128], identb[:], True, True, r=[accb, identb], w=[pst])
            ot = OT[(qi // 4) % 2]
            P.cp(ot[:, :, (qi % 4) * 128:(qi % 4 + 1) * 128], pst[:, 0:384].rearrange("p (g t) -> p g t", t=128), r=[pst], w=[ot])
            if qi % 4 == 3 or qi == ntiles - 1:
                P.dma("sp", mixT[qi // 4][:, 3:6, :], ot[:], reads=[ot], writes=["mix_nsa"])
        P.barrier()
        P.emit()
        P.renew_sems()
    P.es = P.ges
```
